# Optimizing a Trainium2 kernel written in Bass

```python
import jax, jax.numpy as jnp
from jax import lax
import numpy as np

D_MODEL = 1024
BATCH = 4
SEQ = 8192
DEPTH = 4

N_MIXERS = 2
EPS = 1e-6

CHUNK = 128
A_WIDTH = 2 * D_MODEL
A_GROUPS = 8
A_GROUP_DIM = A_WIDTH // A_GROUPS

B_WINDOWS = (2, 4, 8, 16)
B_GROUPS = len(B_WINDOWS)
B_WIDTH = D_MODEL
B_GROUP_DIM = B_WIDTH // B_GROUPS

D_FF = ((8 * D_MODEL + 3 * 256 - 1) // (3 * 256)) * 256

N_A_LAYERS = (DEPTH + 1) // 2
N_B_LAYERS = DEPTH // 2

kernel_name = 'hybrid_gmlp_pool_swiglu_trunk'


def rmsnorm(x, g):
    xf = x.astype(jnp.float32)
    y = xf * lax.rsqrt(jnp.mean(xf * xf, axis=-1, keepdims=True) + EPS)
    return (y * g.astype(jnp.float32)).astype(x.dtype)


def layernorm(x, g, b):
    xf = x.astype(jnp.float32)
    mu = jnp.mean(xf, axis=-1, keepdims=True)
    xc = xf - mu
    var = jnp.mean(xc * xc, axis=-1, keepdims=True)
    y = xc * lax.rsqrt(var + EPS) * g.astype(jnp.float32) + b.astype(jnp.float32)
    return y.astype(x.dtype)


def mixer_a(h, w_in, ln_g, ln_b, w_s, b_s, w_out):
    bsz, s, _ = h.shape
    z = jax.nn.gelu(h @ w_in, approximate=False)
    u, v = jnp.split(z, 2, axis=-1)
    v = layernorm(v, ln_g, ln_b)
    n_chunks = s // CHUNK
    v = v.reshape(bsz, n_chunks, CHUNK, A_GROUPS, A_GROUP_DIM)
    u = u.reshape(bsz, n_chunks, CHUNK, A_GROUPS, A_GROUP_DIM)
    causal = jnp.tril(jnp.ones((CHUNK, CHUNK), dtype=bool))
    w = jnp.where(causal[None], w_s, jnp.zeros_like(w_s))
    sv = jnp.einsum('gts,bnsgd->bntgd', w, v)
    sv = sv + jnp.transpose(b_s)[None, None, :, :, None]
    gated = (u * sv).reshape(bsz, s, A_WIDTH)
    return gated @ w_out


def mixer_b(h, w_in, w_grp, scale, w_out):
    bsz, s, _ = h.shape
    p = h @ w_in
    pf = p.astype(jnp.float32)
    cs = jnp.cumsum(pf, axis=1)
    cs0 = jnp.concatenate([jnp.zeros((bsz, 1, B_WIDTH), jnp.float32), cs], axis=1)
    t = jnp.arange(s)
    pooled = []
    for g, win in enumerate(B_WINDOWS):
        lo, hi = g * B_GROUP_DIM, (g + 1) * B_GROUP_DIM
        c = cs0[..., lo:hi]
        c_pad = jnp.concatenate([jnp.zeros((bsz, win - 1, B_GROUP_DIM), jnp.float32), c], axis=1)
        total = c[:, 1:] - c_pad[:, :s]
        count = jnp.minimum(t + 1, win).astype(jnp.float32)
        pooled.append(total / count[None, :, None] - pf[..., lo:hi])
    pooled = jnp.stack(pooled, axis=2)
    mixed = jnp.einsum('bsgd,gde->bsge', pooled, w_grp.astype(jnp.float32))
    mixed = mixed.reshape(bsz, s, B_WIDTH) * scale.astype(jnp.float32)
    return mixed.astype(h.dtype) @ w_out


def swiglu(h, w_gate, w_up, w_down):
    return (jax.nn.silu(h @ w_gate) * (h @ w_up)) @ w_down


def setup_inputs(seed: int = 0) -> dict:
    key = jax.random.key(seed)
    ks = jax.random.split(key, 20)
    f32 = jnp.float32
    d = D_MODEL
    x = jax.random.normal(ks[0], (BATCH, SEQ, d), f32)
    a_w_in = jax.random.normal(ks[1], (N_A_LAYERS, d, 2 * A_WIDTH), f32) * d ** -0.5
    a_ln_g = 1.0 + 0.02 * jax.random.normal(ks[2], (N_A_LAYERS, A_WIDTH), f32)
    a_ln_b = 0.02 * jax.random.normal(ks[3], (N_A_LAYERS, A_WIDTH), f32)
    a_w_s = jnp.tril(jax.random.normal(ks[4], (N_A_LAYERS, A_GROUPS, CHUNK, CHUNK), f32) * CHUNK ** -0.5)
    a_b_s = 1.0 + 0.1 * jax.random.normal(ks[5], (N_A_LAYERS, A_GROUPS, CHUNK), f32)
    a_w_out = jax.random.normal(ks[6], (N_A_LAYERS, A_WIDTH, d), f32) * A_WIDTH ** -0.5
    b_w_in = jax.random.normal(ks[7], (N_B_LAYERS, d, B_WIDTH), f32) * d ** -0.5
    b_w_grp = jax.random.normal(ks[8], (N_B_LAYERS, B_GROUPS, B_GROUP_DIM, B_GROUP_DIM), f32) * B_GROUP_DIM ** -0.5
    b_scale = 1.0 + 0.1 * jax.random.normal(ks[9], (N_B_LAYERS, B_WIDTH), f32)
    b_w_out = jax.random.normal(ks[10], (N_B_LAYERS, B_WIDTH, d), f32) * B_WIDTH ** -0.5
    mix_pre_g = 1.0 + 0.02 * jax.random.normal(ks[11], (DEPTH, d), f32)
    mix_post_g = 1.0 + 0.02 * jax.random.normal(ks[12], (DEPTH, d), f32)
    ffn_pre_g = 1.0 + 0.02 * jax.random.normal(ks[13], (DEPTH, d), f32)
    ffn_post_g = 1.0 + 0.02 * jax.random.normal(ks[14], (DEPTH, d), f32)
    ffn_w_gate = jax.random.normal(ks[15], (DEPTH, d, D_FF), f32) * d ** -0.5
    ffn_w_up = jax.random.normal(ks[16], (DEPTH, d, D_FF), f32) * d ** -0.5
    ffn_w_down = jax.random.normal(ks[17], (DEPTH, D_FF, d), f32) * D_FF ** -0.5
    return {'x': x, 'a_w_in': a_w_in, 'a_ln_g': a_ln_g, 'a_ln_b': a_ln_b,
            'a_w_s': a_w_s, 'a_b_s': a_b_s, 'a_w_out': a_w_out,
            'b_w_in': b_w_in, 'b_w_grp': b_w_grp, 'b_scale': b_scale, 'b_w_out': b_w_out,
            'mix_pre_g': mix_pre_g, 'mix_post_g': mix_post_g,
            'ffn_pre_g': ffn_pre_g, 'ffn_post_g': ffn_post_g,
            'ffn_w_gate': ffn_w_gate, 'ffn_w_up': ffn_w_up, 'ffn_w_down': ffn_w_down}


def reference(x, a_w_in, a_ln_g, a_ln_b, a_w_s, a_b_s, a_w_out,
              b_w_in, b_w_grp, b_scale, b_w_out,
              mix_pre_g, mix_post_g, ffn_pre_g, ffn_post_g,
              ffn_w_gate, ffn_w_up, ffn_w_down):
    for i in range(DEPTH):
        j = i // N_MIXERS
        h = rmsnorm(x, mix_pre_g[i])
        if i % N_MIXERS == 0:
            m = mixer_a(h, a_w_in[j], a_ln_g[j], a_ln_b[j], a_w_s[j], a_b_s[j], a_w_out[j])
        else:
            m = mixer_b(h, b_w_in[j], b_w_grp[j], b_scale[j], b_w_out[j])
        x = x + rmsnorm(m, mix_post_g[i])
        h = rmsnorm(x, ffn_pre_g[i])
        f = swiglu(h, ffn_w_gate[i], ffn_w_up[i], ffn_w_down[i])
        x = x + rmsnorm(f, ffn_post_g[i])
    return x
```

```python
import numpy as np
import concourse.bass as bass
import concourse.mybir as mybir
from concourse.bass_utils import run_bass_kernel_spmd

F32 = mybir.dt.float32
BF16 = mybir.dt.bfloat16
AF = mybir.ActivationFunctionType
ALU = mybir.AluOpType
AX = mybir.AxisListType

NCORES = 8
D = 1024
KC = 8
NCH = 7
T = NCH * 128
HALVES = ((0, 512), (512, 384))
NTILES = 5
HALO = 3
NCHUNK = NTILES * NCH
NTOK = NCHUNK * 128
OWN = 4096
DFF = 2816
NJ = DFF // 128
EPS = 1e-6
WSLOT = 4096
NWSLOT = 3

GC_MIXPRE, GC_MIXPOST, GC_FFNPRE, GC_FFNPOST, GC_BSCALE, GC_LNB, GC_N = 0, 32, 64, 96, 128, 144, 176


class Op:
    __slots__ = ("eng", "fn", "deps", "sig", "sigidx", "dsem", "dval")

    def __init__(self, eng, fn, dsem=None, dval=0):
        self.eng = eng
        self.fn = fn
        self.deps = []
        self.sig = False
        self.sigidx = 0
        self.dsem = dsem
        self.dval = dval


class Sched:
    ENGS = ("pe", "act", "dve", "pool", "sp")

    def __init__(self):
        self.ops = {e: [] for e in self.ENGS}
        self.res = {}
        self.dma_count = {}

    def add(self, eng, fn, reads=(), writes=(), dsem=None):
        if dsem is not None:
            self.dma_count[dsem] = self.dma_count.get(dsem, 0) + 16
            op = Op(eng, fn, dsem, self.dma_count[dsem])
        else:
            op = Op(eng, fn)
        deps = {}
        res = self.res
        for k in reads:
            rec = res.get(k)
            if rec is not None and rec[0] is not None:
                deps[id(rec[0])] = (rec[0], "RAW")
        for k in writes:
            rec = res.get(k)
            if rec is not None:
                if rec[0] is not None and id(rec[0]) not in deps:
                    deps[id(rec[0])] = (rec[0], "WAW")
                for r in rec[1]:
                    if id(r) not in deps:
                        deps[id(r)] = (r, "WAR")
        for d, kind in deps.values():
            if d is op:
                continue
            if d.dsem is None and op.dsem is None and d.eng == eng:
                if eng == "pe" or kind != "RAW":
                    continue
            op.deps.append(d)
        for k in reads:
            rec = res.get(k)
            if rec is None:
                res[k] = [None, [op]]
            else:
                rec[1].append(op)
        for k in writes:
            res[k] = [op, []]
        self.ops[eng].append(op)
        return op

    def emit(self, nc, engsem):
        for e in self.ENGS:
            for op in self.ops[e]:
                for d in op.deps:
                    if d.dsem is None:
                        d.sig = True
        for e in self.ENGS:
            n = 0
            for op in self.ops[e]:
                if op.dsem is None and op.sig:
                    n += 1
                    op.sigidx = n
        ops = self.ops

        def run(eng_name, eng):
            seen = {}
            for op in ops[eng_name]:
                need = {}
                for d in op.deps:
                    if d.dsem is not None:
                        sem, val = d.dsem, d.dval
                    else:
                        sem, val = engsem[d.eng], d.sigidx
                    k = id(sem)
                    if k not in need or need[k][1] < val:
                        need[k] = (sem, val)
                for k, (sem, val) in need.items():
                    if seen.get(k, 0) < val:
                        eng.wait_ge(sem, val)
                        seen[k] = val
                inst = op.fn(eng)
                if op.dsem is not None:
                    inst.then_inc(op.dsem, 16)
                elif op.sig:
                    inst.then_inc(engsem[eng_name], 1)

        with nc.Block() as block:
            @block.tensor
            def _(e):
                run("pe", e)

            @block.scalar
            def _(e):
                run("act", e)

            @block.vector
            def _(e):
                run("dve", e)

            @block.gpsimd
            def _(e):
                run("pool", e)

            @block.sync
            def _(e):
                run("sp", e)
                for sem, cnt in self.dma_count.items():
                    e.wait_ge(sem, cnt)


def build_nc(layers=(0, 1, 2, 3), ntiles=NTILES):
    nc = bass.Bass("TRN2", target_bir_lowering=False)
    S = Sched()

    def dram(name, shape, dt=F32, kind="ExternalInput"):
        return nc.dram_tensor(name, list(shape), dt, kind=kind).ap()

    xT = dram("xT", [D, NTOK])
    yT = dram("yT", [D, OWN], kind="ExternalOutput")
    a_w_in = dram("a_w_in", [2, D, 4096])
    a_w_out = dram("a_w_out", [2, 2048, D])
    b_w_in = dram("b_w_in", [2, D, D])
    b_w_grp = dram("b_w_grp", [2, 4, 256, 256])
    b_w_out = dram("b_w_out", [2, D, D])
    w_gate = dram("ffn_w_gate", [4, D, DFF])
    w_up = dram("ffn_w_up", [4, D, DFF])
    w_down = dram("ffn_w_down", [4, DFF, D])
    gvec_d = dram("gvec", [128, GC_N])
    lng_d = dram("a_ln_g", [2, 2048])
    wsT_d = dram("a_w_sT", [2, 128, 1024])
    bs_d = dram("a_b_s", [2, 1024])
    maskT_d = dram("maskT", [128, 128])
    pm_d = dram("poolm", [3, 128, 512])

    sb = nc.alloc_sbuf_tensor
    X = sb("X", [128, KC * T], F32)
    H = sb("H", [128, KC * T], BF16)
    MS = sb("MS", [128, KC * T], F32)
    SQ = sb("SQ", [128, KC * T], BF16)
    BIG = sb("BIG", [128, 28672], BF16)
    WR = sb("WR", [128, NWSLOT * WSLOT], BF16)
    RS = [sb("RS0", [128, T], F32)]
    SG = [sb("SG0", [128, T], BF16)]
    GV = sb("GV", [128, GC_N], F32)
    GBC = sb("GBC", [128, 2 * 2048], BF16)
    WSM = sb("WSM", [128, 2 * 1024], BF16)
    BIAS = sb("BIAS", [128, 2 * 2048], F32)
    PM = sb("PM", [128, 3 * 512], BF16)
    PH = sb("PH", [128, 2 * 1024], BF16)
    ONESM = sb("ONESM", [128, 128], BF16)
    ONES1 = sb("ONES1", [128, 128], BF16)
    EPSC = sb("EPSC", [128, 1], F32)
    ST = sb("STATS", [128, 64], F32)
    ps = nc.alloc_psum_tensor("ps", [128, 4096], F32)

    engsem = {e: nc.alloc_semaphore("sem_" + e) for e in ("pe", "act", "dve")}
    wsem = [nc.alloc_semaphore("sem_w%d" % i) for i in range(NWSLOT)]
    xsem = nc.alloc_semaphore("sem_x")
    ysem = nc.alloc_semaphore("sem_y")
    _cs = [0]

    def csem_new():
        _cs[0] += 1
        return nc.alloc_semaphore("sem_c%d" % _cs[0])

    def Xv(fc, c0=0, n=T):
        return X[:, fc * T + c0: fc * T + c0 + n]

    def Hv(fc, c0=0, n=T):
        return H[:, fc * T + c0: fc * T + c0 + n]

    def MSv(fc, c0=0, n=T):
        return MS[:, fc * T + c0: fc * T + c0 + n]

    def SQv(fc, c0=0, n=T):
        return SQ[:, fc * T + c0: fc * T + c0 + n]

    def kX(fc): return ("X", fc)
    def kH(fc): return ("H", fc)
    def kMS(fc): return ("MS", fc)

    def kSQb(lo, n):
        return tuple(("SQ", b) for b in range(lo // 128, (lo + n + 127) // 128))

    def kSQ(fc): return kSQb(fc * T, T)

    def kBIG(lo, n):
        return tuple(("BIG", b) for b in range(lo // 128, (lo + n + 127) // 128))

    def Uv(fc, c0=0, n=T): return BIG[:, fc * T + c0: fc * T + c0 + n]
    def kU(fc): return kBIG(fc * T, T)
    VOFF = 16 * T
    def Vv(c, c0=0, n=2048): return BIG[:, VOFF + c * 2048 + c0: VOFF + c * 2048 + c0 + n]
    def kV(c): return kBIG(VOFF + c * 2048, 2048)
    def Gv(j, c0=0, n=T): return BIG[:, j * T + c0: j * T + c0 + n]
    def kG(j): return kBIG(j * T, T)
    def PTv(c, c0=0, n=1024): return BIG[:, c * 1024 + c0: c * 1024 + c0 + n]
    def kPT(c): return kBIG(c * 1024, 1024)
    PLOFF = 7 * 1024
    def PLv(fc, c0=0, n=T): return BIG[:, PLOFF + fc * T + c0: PLOFF + fc * T + c0 + n]
    def kPL(fc): return kBIG(PLOFF + fc * T, T)
    MXOFF = PLOFF + 8 * T
    def MXv(fc, c0=0, n=T): return BIG[:, MXOFF + fc * T + c0: MXOFF + fc * T + c0 + n]
    def kMX(fc): return kBIG(MXOFF + fc * T, T)

    def kPS(slot): return (("ps", 2 * slot), ("ps", 2 * slot + 1))
    def PSv(slot, c0=0, n=T): return ps[:, slot * 1024 + c0: slot * 1024 + c0 + n]
    def PSb(bank, c0=0, n=512): return ps[:, bank * 512 + c0: bank * 512 + c0 + n]

    state = {"slot": 0, "bank": 0, "w": 0, "rs": 0, "sg": 0}

    def next_slot():
        s = state["slot"]
        state["slot"] = (s + 1) % 3
        return s

    def next_bank():
        b = state["bank"]
        state["bank"] = (b + 1) % 6
        return b

    STAT = 3

    def pe_mm(out, lhsT, rhs, start, stop, reads, writes):
        S.add("pe", lambda e: e.matmul(out, lhsT=lhsT, rhs=rhs, start=start, stop=stop), reads, writes)

    def act(out, in_, func, reads, writes, scale=None, bias=None, accum=None):
        kw = {}
        if scale is not None:
            kw["scale"] = scale
        if bias is not None:
            kw["bias"] = bias
        if accum is not None:
            kw["accum_out"] = accum
        S.add("act", lambda e: e.activation(out=out, in_=in_, func=func, **kw), reads, writes)

    def dve(fn, reads, writes):
        S.add("dve", fn, reads, writes)

    def wload(parts):
        w = state["w"]
        state["w"] = (w + 1) % NWSLOT
        key = ("W", w)
        for src, off in parts:
            k, n = src.shape[1], src.shape[2]
            dst = WR[:, w * WSLOT + off: w * WSLOT + off + k * n].rearrange("p (k n) -> p k n", k=k)
            S.add("pool", lambda e, dst=dst, src=src: e.dma_start(out=dst, in_=src), (), (key,), dsem=wsem[w])
        base = w * WSLOT
        return (lambda off, n: WR[:, base + off: base + off + n]), key

    def wrows(w2d, r0, nk, c0, n):
        return w2d[r0 * 128:(r0 + nk) * 128, c0:c0 + n].rearrange("(k p) n -> p k n", p=128)

    S.add("sp", lambda e: e.dma_start(out=GV[:], in_=gvec_d), (), (("GV",),), dsem=csem_new())
    S.add("pool", lambda e: e.dma_start(out=PM[:].rearrange("p (a n) -> p a n", a=3),
                                        in_=pm_d.rearrange("a p n -> p a n")), (), (("PM",),), dsem=csem_new())
    S.add("pool", lambda e: e.dma_start(out=GBC[:].rearrange("p (a n) -> p a n", a=2),
                                        in_=lng_d.partition_broadcast(128)), (), (("GBC",),), dsem=csem_new())
    dve(lambda e: e.memset(ONESM[:], 1.0 / 1024.0), (), (("ONESM",),))
    dve(lambda e: e.memset(ONES1[:], 1.0), (), (("ONES1",),))
    dve(lambda e: e.memset(EPSC[:], EPS), (), (("EPSC",),))
    dve(lambda e: e.memset(PH[:], 0.0), (), (("PH", 0), ("PH", 1)))
    dve(lambda e: e.memset(ST[:], 0.0), (), (("ST",),))

    a_layers = sorted({l // 2 for l in layers if l % 2 == 0})
    if a_layers:
        kscr = (kMS(0), kMS(1), kMS(2))
        kscr2 = (kMS(3),)
        S.add("sp", lambda e: e.dma_start(out=MS[:, 2048:2176], in_=maskT_d), (), kscr, dsem=csem_new())
        for j in a_layers:
            S.add("sp", lambda e, j=j: e.dma_start(out=MS[:, 0:1024], in_=wsT_d[j]), (), kscr, dsem=csem_new())
            S.add("sp", lambda e, j=j: e.dma_start(out=MS[:, 2688:3712], in_=bs_d[j].partition_broadcast(128)),
                  (), kscr2, dsem=csem_new())
            dve(lambda e, j=j: e.tensor_tensor(
                out=WSM[:, j * 1024:(j + 1) * 1024].rearrange("p (g t) -> p g t", g=8),
                in0=MS[:, 0:1024].rearrange("p (g t) -> p g t", g=8),
                in1=MS[:, 2048:2176].unsqueeze(1).broadcast_to([128, 8, 128]), op=ALU.mult),
                kscr, (("WSM", j),))
            for g in range(8):
                pe_mm(PSv(0, g * 128, 128), ONES1[:], WSM[:, j * 1024 + g * 128: j * 1024 + (g + 1) * 128],
                      True, True, (("ONES1",), ("WSM", j)), kPS(0))
            for fc in range(16):
                g = fc // 2
                dve(lambda e, j=j, fc=fc, g=g: e.scalar_tensor_tensor(
                    out=BIAS[:, j * 2048 + fc * 128: j * 2048 + (fc + 1) * 128],
                    in0=PSv(0, g * 128, 128), scalar=GV[:, GC_LNB + j * 16 + fc: GC_LNB + j * 16 + fc + 1],
                    in1=MS[:, 2688 + g * 128: 2688 + (g + 1) * 128], op0=ALU.mult, op1=ALU.add),
                    kPS(0) + (("GV",),) + kscr2, (("BIAS", j),))

    def rstd_from_stats():
        r = state["rs"]
        state["rs"] = (r + 1) % len(RS)
        rs = RS[r]
        act(rs[:], PSv(STAT), AF.Sqrt, kPS(STAT) + (("EPSC",),), (("RS", r),), bias=EPSC[:])
        dve(lambda e: e.reciprocal(out=rs[:], in_=rs[:]), (("RS", r),), (("RS", r),))
        return r

    def stats_mm(fc, first, last):
        for (h0, hn) in HALVES:
            pe_mm(PSv(STAT, h0, hn), ONESM[:], SQv(fc, h0, hn), first, last,
                  kSQ(fc) + (("ONESM",),), kPS(STAT))

    def prenorm(gcol):
        for fc in range(KC):
            act(SQv(fc), Xv(fc), AF.Square, (kX(fc),), kSQ(fc))
            stats_mm(fc, fc == 0, fc == KC - 1)
        r = rstd_from_stats()
        for fc in range(KC):
            dve(lambda e, fc=fc: e.scalar_tensor_tensor(
                out=Hv(fc), in0=Xv(fc), scalar=GV[:, gcol + fc: gcol + fc + 1], in1=RS[r][:],
                op0=ALU.mult, op1=ALU.mult), (kX(fc), ("GV",), ("RS", r)), (kH(fc),))

    class PostNorm:
        def __init__(self, gcol):
            self.gcol = gcol
            self.pending = None

        def evac(self, slot, oc):
            gcol = self.gcol
            act(MSv(oc), PSv(slot), AF.Identity, kPS(slot) + (("GV",),), (kMS(oc),),
                scale=GV[:, gcol + oc: gcol + oc + 1])
            act(SQv(oc), PSv(slot), AF.Square, kPS(slot), kSQ(oc))
            if self.pending is not None:
                stats_mm(self.pending, self.pending == 0, False)
            self.pending = oc

        def finish(self):
            stats_mm(self.pending, self.pending == 0, True)
            r = rstd_from_stats()
            for fc in range(KC):
                dve(lambda e, fc=fc: e.tensor_tensor(out=MSv(fc), in0=MSv(fc), in1=RS[r][:], op=ALU.mult),
                    (kMS(fc), ("RS", r)), (kMS(fc),))
            for fc in range(KC):
                dve(lambda e, fc=fc: e.tensor_tensor(out=Xv(fc), in0=MSv(fc), in1=Xv(fc), op=ALU.add),
                    (kMS(fc), kX(fc)), (kX(fc),))

    def group_fm(slot, wv, wkey, woff_fn, nk, rhs_fn, rhs_keys_fn):
        for k in range(nk):
            for (h0, hn) in HALVES:
                pe_mm(PSv(slot, h0, hn), wv(woff_fn(k), 128), rhs_fn(k, h0, hn), k == 0, k == nk - 1,
                      (wkey,) + rhs_keys_fn(k), kPS(slot))

    def ffn(l):
        prenorm(GC_FFNPRE + l * 8)
        for pi in range(NJ // 2):
            wv, wkey = wload([(wrows(w_gate[l], 0, 8, pi * 256, 256), 0),
                              (wrows(w_up[l], 0, 8, pi * 256, 256), 2048)])
            for jj in range(2):
                j = pi * 2 + jj
                sg_slot = next_slot()
                group_fm(sg_slot, wv, wkey, lambda k, jj=jj: k * 256 + jj * 128, KC,
                         lambda k, h0, hn: Hv(k, h0, hn), lambda k: (kH(k),))
                su_slot = next_slot()
                group_fm(su_slot, wv, wkey, lambda k, jj=jj: 2048 + k * 256 + jj * 128, KC,
                         lambda k, h0, hn: Hv(k, h0, hn), lambda k: (kH(k),))
                q = state["sg"]
                state["sg"] = (q + 1) % len(SG)
                act(SG[q][:], PSv(sg_slot), AF.Silu, kPS(sg_slot), (("SG", q),))
                dve(lambda e, j=j, q=q, su_slot=su_slot: e.tensor_tensor(
                    out=Gv(j), in0=PSv(su_slot), in1=SG[q][:], op=ALU.mult),
                    kPS(su_slot) + (("SG", q),), kG(j))
        pn = PostNorm(GC_FFNPOST + l * 8)
        for oc in range(KC):
            wv, wkey = wload([(wrows(w_down[l], 0, NJ, oc * 128, 128), 0)])
            slot = next_slot()
            group_fm(slot, wv, wkey, lambda k: k * 128, NJ,
                     lambda k, h0, hn: Gv(k, h0, hn), lambda k: kG(k))
            pn.evac(slot, oc)
        pn.finish()

    def mixer_a(l):
        j = l // 2
        prenorm(GC_MIXPRE + l * 8)
        dve(lambda e: e.memset(ST[:, 0:35], 0.0), (), (("ST",),))
        for vb in range(4):
            wv, wkey = wload([(wrows(a_w_in[j], 0, 8, 2048 + vb * 512, 512), 0)])
            for c in range(NCH):
                b = next_bank()
                for k in range(KC):
                    pe_mm(PSb(b), Hv(k, c * 128, 128), wv(k * 512, 512), k == 0, k == KC - 1,
                          (kH(k), wkey), (("ps", b),))
                act(Vv(c, vb * 512, 512), PSb(b), AF.Gelu, (("ps", b),), kV(c) + (("ST",),),
                    accum=ST[:, c * 4 + vb: c * 4 + vb + 1])
        for c in range(NCH):
            act(SQ[:, 0:2048], Vv(c), AF.Square, kV(c), kSQb(0, 2048) + (("ST",),),
                accum=ST[:, 28 + c: 29 + c])
        kst = (("ST",),)
        dve(lambda e: e.tensor_reduce(out=ST[:, 35:42], in_=ST[:, 0:28].rearrange("p (c v) -> p c v", v=4),
                                      axis=AX.X, op=ALU.add), kst, kst)
        dve(lambda e: e.tensor_scalar(out=ST[:, 35:42], in0=ST[:, 35:42], scalar1=1.0 / 2048.0, scalar2=None,
                                      op0=ALU.mult), kst, kst)
        dve(lambda e: e.tensor_tensor(out=ST[:, 42:49], in0=ST[:, 35:42], in1=ST[:, 35:42], op=ALU.mult), kst, kst)
        dve(lambda e: e.scalar_tensor_tensor(out=ST[:, 42:49], in0=ST[:, 28:35], scalar=1.0 / 2048.0,
                                             in1=ST[:, 42:49], op0=ALU.mult, op1=ALU.subtract), kst, kst)
        dve(lambda e: e.tensor_scalar(out=ST[:, 42:49], in0=ST[:, 42:49], scalar1=0.0, scalar2=None,
                                      op0=ALU.max), kst, kst)
        act(ST[:, 49:56], ST[:, 42:49], AF.Sqrt, kst + (("EPSC",),), kst, bias=EPSC[:])
        dve(lambda e: e.reciprocal(out=ST[:, 49:56], in_=ST[:, 49:56]), kst, kst)
        for ub in range(4):
            wv, wkey = wload([(wrows(a_w_in[j], 0, 8, ub * 512, 512), 0)])
            for q in range(4):
                oc = ub * 4 + q
                slot = next_slot()
                group_fm(slot, wv, wkey, lambda k, q=q: k * 512 + q * 128, KC,
                         lambda k, h0, hn: Hv(k, h0, hn), lambda k: (kH(k),))
                act(Uv(oc), PSv(slot), AF.Gelu, kPS(slot), kU(oc))
            if ub == 0:
                for c in range(NCH):
                    dve(lambda e, c=c: e.scalar_tensor_tensor(
                        out=Vv(c), in0=Vv(c), scalar=ST[:, 35 + c: 36 + c],
                        in1=GBC[:, j * 2048:(j + 1) * 2048], op0=ALU.subtract, op1=ALU.mult),
                        kV(c) + kst + (("GBC",),), kV(c))
                    dve(lambda e, c=c: e.tensor_scalar(
                        out=SQ[:, c * 1024:(c + 1) * 1024], in0=WSM[:, j * 1024:(j + 1) * 1024],
                        scalar1=ST[:, 49 + c: 50 + c], scalar2=None, op0=ALU.mult),
                        (("WSM", j),) + kst, kSQb(c * 1024, 1024))
        prev = None
        for fc in range(16):
            g = fc // 2
            slot = next_slot()
            for c in range(NCH):
                pe_mm(PSv(slot, c * 128, 128), Vv(c, fc * 128, 128),
                      SQ[:, c * 1024 + g * 128: c * 1024 + (g + 1) * 128], True, True,
                      kV(c) + kSQb(c * 1024 + g * 128, 128), kPS(slot))
            tmp = fc % 8
            dve(lambda e, fc=fc, slot=slot, tmp=tmp: e.tensor_tensor(
                out=MSv(tmp).rearrange("p (c t) -> p c t", c=NCH),
                in0=PSv(slot).rearrange("p (c t) -> p c t", c=NCH),
                in1=BIAS[:, j * 2048 + fc * 128: j * 2048 + (fc + 1) * 128].unsqueeze(1).broadcast_to([128, NCH, 128]),
                op=ALU.add), kPS(slot) + (("BIAS", j),), (kMS(tmp),))
            if prev is not None:
                pfc, ptmp = prev
                dve(lambda e, pfc=pfc, ptmp=ptmp: e.tensor_tensor(out=Uv(pfc), in0=MSv(ptmp), in1=Uv(pfc), op=ALU.mult),
                    (kMS(ptmp),) + kU(pfc), kU(pfc))
            prev = (fc, tmp)
        pfc, ptmp = prev
        dve(lambda e: e.tensor_tensor(out=Uv(pfc), in0=MSv(ptmp), in1=Uv(pfc), op=ALU.mult),
            (kMS(ptmp),) + kU(pfc), kU(pfc))
        pn = PostNorm(GC_MIXPOST + l * 8)
        for pi in range(4):
            wv, wkey = wload([(wrows(a_w_out[j], 0, 16, pi * 256, 256), 0)])
            for q in range(2):
                oc = pi * 2 + q
                slot = next_slot()
                group_fm(slot, wv, wkey, lambda k, q=q: k * 256 + q * 128, 16,
                         lambda k, h0, hn: Uv(k, h0, hn), lambda k: kU(k))
                pn.evac(slot, oc)
        pn.finish()

    def mixer_b(l, first_chunk):
        j = l // 2
        prenorm(GC_MIXPRE + l * 8)
        for nh in range(2):
            wv, wkey = wload([(wrows(b_w_in[j], 0, 8, nh * 512, 512), 0)])
            for c in range(NCH):
                b = next_bank()
                for k in range(KC):
                    pe_mm(PSb(b), Hv(k, c * 128, 128), wv(k * 512, 512), k == 0, k == KC - 1,
                          (kH(k), wkey), (("ps", b),))
                act(PTv(c, nh * 512, 512), PSb(b), AF.Copy, (("ps", b),), kPT(c))
        wv, wkey = wload([(b_w_grp[j].rearrange("g (dc p) e -> p (g dc) e", p=128), 0)])
        for fc in range(KC):
            g = fc // 2
            slot = next_slot()
            for c in range(NCH):
                pm = 2 if c == first_chunk else 0
                if c == 0:
                    prev_ap, prev_key = PH[:, j * 1024 + fc * 128: j * 1024 + (fc + 1) * 128], (("PH", j),)
                else:
                    prev_ap, prev_key = PTv(c - 1, fc * 128, 128), kPT(c - 1)
                pe_mm(PSv(slot, c * 128, 128), PTv(c, fc * 128, 128),
                      PM[:, pm * 512 + g * 128: pm * 512 + (g + 1) * 128], True, False,
                      kPT(c) + (("PM",),), kPS(slot))
                pe_mm(PSv(slot, c * 128, 128), prev_ap,
                      PM[:, 512 + g * 128: 512 + (g + 1) * 128], False, True,
                      prev_key + (("PM",),), kPS(slot))
            act(PLv(fc), PSv(slot), AF.Copy, kPS(slot), kPL(fc))
        dve(lambda e: e.tensor_copy(out=PH[:, j * 1024:(j + 1) * 1024], in_=PTv(NCH - 1)),
            kPT(NCH - 1), (("PH", j),))
        for ec in range(KC):
            g = ec // 2
            slot = next_slot()
            for dc in range(2):
                for (h0, hn) in HALVES:
                    pe_mm(PSv(slot, h0, hn), wv((g * 2 + dc) * 256 + (ec % 2) * 128, 128),
                          PLv(2 * g + dc, h0, hn), dc == 0, dc == 1, (wkey,) + kPL(2 * g + dc), kPS(slot))
            act(MXv(ec), PSv(slot), AF.Identity, kPS(slot) + (("GV",),), kMX(ec),
                scale=GV[:, GC_BSCALE + j * 8 + ec: GC_BSCALE + j * 8 + ec + 1])
        pn = PostNorm(GC_MIXPOST + l * 8)
        for pi in range(2):
            wv, wkey = wload([(wrows(b_w_out[j], 0, 8, pi * 512, 512), 0)])
            for q in range(4):
                oc = pi * 4 + q
                slot = next_slot()
                group_fm(slot, wv, wkey, lambda k, q=q: k * 512 + q * 128, KC,
                         lambda k, h0, hn: MXv(k, h0, hn), lambda k: kMX(k))
                pn.evac(slot, oc)
        pn.finish()

    xT3 = xT.rearrange("(fc p) t -> p fc t", p=128)
    yT3 = yT.rearrange("(fc p) t -> p fc t", p=128)
    X3 = X[:].rearrange("p (fc t) -> p fc t", fc=KC)
    allX = tuple(kX(fc) for fc in range(KC))
    for t in range(ntiles):
        S.add("sp", lambda e, t=t: e.dma_start(out=X3, in_=xT3[:, :, t * T:(t + 1) * T]), (), allX, dsem=xsem)
        for l in layers:
            if l % 2 == 0:
                mixer_a(l)
            else:
                mixer_b(l, HALO if t == 0 else -1)
            ffn(l)
        if t == 0:
            n_own = (NCH - HALO) * 128
            S.add("sp", lambda e: e.dma_start(out=yT3[:, :, 0:n_own], in_=X3[:, :, HALO * 128:T]),
                  allX, (), dsem=ysem)
        else:
            o0 = (NCH - HALO) * 128 + (t - 1) * T
            S.add("sp", lambda e, o0=o0: e.dma_start(out=yT3[:, :, o0:o0 + T], in_=X3), allX, (), dsem=ysem)

    S.emit(nc, engsem)
    return nc


def _host_consts():
    s = np.arange(128)[:, None]
    t = np.arange(128)[None, :]
    maskT = (s <= t).astype(np.float32)
    wins = (2, 4, 8, 16)
    pm = np.zeros((3, 128, 4, 128), np.float32)
    for g, w in enumerate(wins):
        band = ((s <= t) & (s > t - w)).astype(np.float32)
        pm[0, :, g, :] = band / w - np.eye(128, dtype=np.float32)
        bandp = ((s - 128) > (t - w)).astype(np.float32)
        pm[1, :, g, :] = bandp / w
        cnt = np.minimum(t + 1, w).astype(np.float32)
        pm[2, :, g, :] = band / cnt - np.eye(128, dtype=np.float32)
    return maskT, pm.reshape(3, 128, 512)


def _vec_cols(v):
    return np.ascontiguousarray(v.reshape(-1, 128).T)


_NC_CACHE = {}


def _get_nc(layers, ntiles):
    key = (tuple(layers), ntiles)
    if key not in _NC_CACHE:
        _NC_CACHE[key] = build_nc(layers, ntiles)
    return _NC_CACHE[key]


def _make_in_maps(x, a_w_in, a_ln_g, a_ln_b, a_w_s, a_b_s, a_w_out, b_w_in, b_w_grp, b_scale, b_w_out,
                  mix_pre_g, mix_post_g, ffn_pre_g, ffn_post_g, ffn_w_gate, ffn_w_up, ffn_w_down):
    f = np.float32
    B, Sq, _ = x.shape
    maskT, pm = _host_consts()
    gv = np.zeros((128, GC_N), f)
    for l in range(4):
        gv[:, GC_MIXPRE + l * 8: GC_MIXPRE + l * 8 + 8] = _vec_cols(np.asarray(mix_pre_g[l], f))
        gv[:, GC_MIXPOST + l * 8: GC_MIXPOST + l * 8 + 8] = _vec_cols(np.asarray(mix_post_g[l], f))
        gv[:, GC_FFNPRE + l * 8: GC_FFNPRE + l * 8 + 8] = _vec_cols(np.asarray(ffn_pre_g[l], f))
        gv[:, GC_FFNPOST + l * 8: GC_FFNPOST + l * 8 + 8] = _vec_cols(np.asarray(ffn_post_g[l], f))
    for j in range(2):
        gv[:, GC_BSCALE + j * 8: GC_BSCALE + j * 8 + 8] = _vec_cols(np.asarray(b_scale[j], f))
        gv[:, GC_LNB + j * 16: GC_LNB + j * 16 + 16] = _vec_cols(np.asarray(a_ln_b[j], f))
    wsT = np.ascontiguousarray(np.transpose(np.asarray(a_w_s, f), (0, 3, 1, 2))).reshape(2, 128, 1024)
    bs = np.ascontiguousarray(np.asarray(a_b_s, f)).reshape(2, 1024)
    shared = {
        "a_w_in": np.ascontiguousarray(a_w_in, f), "a_w_out": np.ascontiguousarray(a_w_out, f),
        "b_w_in": np.ascontiguousarray(b_w_in, f), "b_w_grp": np.ascontiguousarray(b_w_grp, f),
        "b_w_out": np.ascontiguousarray(b_w_out, f), "ffn_w_gate": np.ascontiguousarray(ffn_w_gate, f),
        "ffn_w_up": np.ascontiguousarray(ffn_w_up, f), "ffn_w_down": np.ascontiguousarray(ffn_w_down, f),
        "gvec": gv, "a_ln_g": np.ascontiguousarray(a_ln_g, f), "a_w_sT": wsT, "a_b_s": bs, "maskT": maskT,
    }
    pm_mid = pm.copy()
    pm_mid[2] = pm_mid[0]
    in_maps = []
    for core in range(NCORES):
        b, half = core // 2, core % 2
        start = half * OWN - HALO * 128
        xw = np.zeros((NTOK, D), f)
        lo = max(start, 0)
        xw[lo - start:, :] = x[b, lo:start + NTOK, :]
        m = dict(shared)
        m["xT"] = np.ascontiguousarray(xw.T)
        m["poolm"] = pm if half == 0 else pm_mid
        in_maps.append(m)
    return in_maps


def kernel(**inputs):
    inputs = {k: np.asarray(v) for k, v in inputs.items()}
    x = inputs["x"].astype(np.float32, copy=False)
    B, Sq, _ = x.shape
    in_maps = _make_in_maps(**inputs)
    nc = _get_nc((0, 1, 2, 3), NTILES)
    res = run_bass_kernel_spmd(nc, in_maps, core_ids=list(range(NCORES)))
    out = np.empty((B, Sq, D), np.float32)
    for core in range(NCORES):
        b, half = core // 2, core % 2
        out[b, half * OWN:(half + 1) * OWN, :] = res.results[core]["yT"].T
    return out
```

```python
import numpy as np
import concourse.bass as bass
import concourse.mybir as mybir
from concourse.bass_utils import run_bass_kernel_spmd

F32 = mybir.dt.float32
BF16 = mybir.dt.bfloat16
AF = mybir.ActivationFunctionType
ALU = mybir.AluOpType
AX = mybir.AxisListType

NCORES = 8
D = 1024
KC = 8
NCH = 7
T = NCH * 128
HALVES = ((0, 512), (512, 384))
NTILES = 5
HALO = 3
NCHUNK = NTILES * NCH
NTOK = NCHUNK * 128
OWN = 4096
DFF = 2816
NJ = DFF // 128
EPS = 1e-6
WSLOT = 4096
NWSLOT = 3

GC_MIXPRE, GC_MIXPOST, GC_FFNPRE, GC_FFNPOST, GC_BSCALE, GC_LNB, GC_N = 0, 32, 64, 96, 128, 144, 176


class Op:
    __slots__ = ("eng", "fn", "deps", "sig", "sigidx", "dsem", "dval")

    def __init__(self, eng, fn, dsem=None, dval=0):
        self.eng = eng
        self.fn = fn
        self.deps = []
        self.sig = False
        self.sigidx = 0
        self.dsem = dsem
        self.dval = dval


class Sched:
    ENGS = ("pe", "act", "dve", "pool", "sp")

    def __init__(self):
        self.ops = {e: [] for e in self.ENGS}
        self.res = {}
        self.dma_count = {}

    def add(self, eng, fn, reads=(), writes=(), dsem=None):
        if dsem is not None:
            self.dma_count[dsem] = self.dma_count.get(dsem, 0) + 16
            op = Op(eng, fn, dsem, self.dma_count[dsem])
        else:
            op = Op(eng, fn)
        deps = {}
        res = self.res
        for k in reads:
            rec = res.get(k)
            if rec is not None and rec[0] is not None:
                deps[id(rec[0])] = (rec[0], "RAW")
        for k in writes:
            rec = res.get(k)
            if rec is not None:
                if rec[0] is not None and id(rec[0]) not in deps:
                    deps[id(rec[0])] = (rec[0], "WAW")
                for r in rec[1]:
                    if id(r) not in deps:
                        deps[id(r)] = (r, "WAR")
        for d, kind in deps.values():
            if d is op:
                continue
            if d.dsem is None and op.dsem is None and d.eng == eng:
                if eng == "pe" or kind != "RAW":
                    continue
            op.deps.append(d)
        for k in reads:
            rec = res.get(k)
            if rec is None:
                res[k] = [None, [op]]
            else:
                rec[1].append(op)
        for k in writes:
            res[k] = [op, []]
        self.ops[eng].append(op)
        return op

    def emit(self, nc, engsem):
        for e in self.ENGS:
            for op in self.ops[e]:
                for d in op.deps:
                    if d.dsem is None:
                        d.sig = True
        for e in self.ENGS:
            n = 0
            for op in self.ops[e]:
                if op.dsem is None and op.sig:
                    n += 1
                    op.sigidx = n
        ops = self.ops

        def run(eng_name, eng):
            seen = {}
            for op in ops[eng_name]:
                need = {}
                for d in op.deps:
                    if d.dsem is not None:
                        sem, val = d.dsem, d.dval
                    else:
                        sem, val = engsem[d.eng], d.sigidx
                    k = id(sem)
                    if k not in need or need[k][1] < val:
                        need[k] = (sem, val)
                for k, (sem, val) in need.items():
                    if seen.get(k, 0) < val:
                        eng.wait_ge(sem, val)
                        seen[k] = val
                inst = op.fn(eng)
                if op.dsem is not None:
                    inst.then_inc(op.dsem, 16)
                elif op.sig:
                    inst.then_inc(engsem[eng_name], 1)

        with nc.Block() as block:
            @block.tensor
            def _(e):
                run("pe", e)

            @block.scalar
            def _(e):
                run("act", e)

            @block.vector
            def _(e):
                run("dve", e)

            @block.gpsimd
            def _(e):
                run("pool", e)

            @block.sync
            def _(e):
                run("sp", e)
                for sem, cnt in self.dma_count.items():
                    e.wait_ge(sem, cnt)


def build_nc(layers=(0, 1, 2, 3), ntiles=NTILES):
    nc = bass.Bass("TRN2", target_bir_lowering=False)
    S = Sched()

    def dram(name, shape, dt=F32, kind="ExternalInput"):
        return nc.dram_tensor(name, list(shape), dt, kind=kind).ap()

    xT = dram("xT", [D, NTOK])
    yT = dram("yT", [D, OWN], kind="ExternalOutput")
    a_w_in = dram("a_w_in", [2, D, 4096])
    a_w_out = dram("a_w_out", [2, 2048, D])
    b_w_in = dram("b_w_in", [2, D, D])
    b_w_grp = dram("b_w_grp", [2, 4, 256, 256])
    b_w_out = dram("b_w_out", [2, D, D])
    w_gate = dram("ffn_w_gate", [4, D, DFF])
    w_up = dram("ffn_w_up", [4, D, DFF])
    w_down = dram("ffn_w_down", [4, DFF, D])
    gvec_d = dram("gvec", [128, GC_N])
    lng_d = dram("a_ln_g", [2, 2048])
    wsT_d = dram("a_w_sT", [2, 128, 1024])
    bs_d = dram("a_b_s", [2, 1024])
    maskT_d = dram("maskT", [128, 128])
    pm_d = dram("poolm", [3, 128, 512])

    sb = nc.alloc_sbuf_tensor
    X = sb("X", [128, KC * T], F32)
    H = sb("H", [128, KC * T], BF16)
    MS = sb("MS", [128, KC * T], F32)
    SQ = sb("SQ", [128, KC * T], BF16)
    BIG = sb("BIG", [128, 28672], BF16)
    WR = sb("WR", [128, NWSLOT * WSLOT], BF16)
    RS = [sb("RS0", [128, T], F32)]
    SG = [sb("SG0", [128, T], BF16)]
    GV = sb("GV", [128, GC_N], F32)
    GBC = sb("GBC", [128, 2 * 2048], BF16)
    WSM = sb("WSM", [128, 2 * 1024], BF16)
    BIAS = sb("BIAS", [128, 2 * 2048], F32)
    PM = sb("PM", [128, 3 * 512], BF16)
    PH = sb("PH", [128, 2 * 1024], BF16)
    ONESM = sb("ONESM", [128, 128], BF16)
    ONES1 = sb("ONES1", [128, 128], BF16)
    EPSC = sb("EPSC", [128, 1], F32)
    ST = sb("STATS", [128, 64], F32)
    ps = nc.alloc_psum_tensor("ps", [128, 4096], F32)

    engsem = {e: nc.alloc_semaphore("sem_" + e) for e in ("pe", "act", "dve")}
    wsem = [nc.alloc_semaphore("sem_w%d" % i) for i in range(NWSLOT)]
    xsem = nc.alloc_semaphore("sem_x")
    ysem = nc.alloc_semaphore("sem_y")
    _cs = [0]

    def csem_new():
        _cs[0] += 1
        return nc.alloc_semaphore("sem_c%d" % _cs[0])

    cur = {"c0": 0}

    def _rng(c0, n):
        if c0 is None:
            c0 = cur["c0"] * 128
        if n is None:
            n = T - c0
        return c0, n

    def halves():
        col0 = cur["c0"] * 128
        return ((col0, 512 - col0), (512, 384))

    def Xv(fc, c0=None, n=None):
        c0, n = _rng(c0, n)
        return X[:, fc * T + c0: fc * T + c0 + n]

    def Hv(fc, c0=None, n=None):
        c0, n = _rng(c0, n)
        return H[:, fc * T + c0: fc * T + c0 + n]

    def MSv(fc, c0=None, n=None):
        c0, n = _rng(c0, n)
        return MS[:, fc * T + c0: fc * T + c0 + n]

    def SQv(fc, c0=None, n=None):
        c0, n = _rng(c0, n)
        return SQ[:, fc * T + c0: fc * T + c0 + n]

    def RSv(r):
        c0, n = _rng(None, None)
        return RS[r][:, c0:c0 + n]

    def kX(fc): return ("X", fc)
    def kH(fc): return ("H", fc)
    def kMS(fc): return ("MS", fc)

    def kSQb(lo, n):
        return tuple(("SQ", b) for b in range(lo // 128, (lo + n + 127) // 128))

    def kSQ(fc): return kSQb(fc * T, T)

    def kBIG(lo, n):
        return tuple(("BIG", b) for b in range(lo // 128, (lo + n + 127) // 128))

    def Uv(fc, c0=None, n=None):
        c0, n = _rng(c0, n)
        return BIG[:, fc * T + c0: fc * T + c0 + n]
    def kU(fc): return kBIG(fc * T, T)
    VOFF = 16 * T
    def Vv(c, c0=0, n=2048): return BIG[:, VOFF + c * 2048 + c0: VOFF + c * 2048 + c0 + n]
    def kV(c): return kBIG(VOFF + c * 2048, 2048)
    def Gv(j, c0=None, n=None):
        c0, n = _rng(c0, n)
        return BIG[:, j * T + c0: j * T + c0 + n]
    def kG(j): return kBIG(j * T, T)
    def PTv(c, c0=0, n=1024): return BIG[:, c * 1024 + c0: c * 1024 + c0 + n]
    def kPT(c): return kBIG(c * 1024, 1024)
    PLOFF = 7 * 1024
    def PLv(fc, c0=None, n=None):
        c0, n = _rng(c0, n)
        return BIG[:, PLOFF + fc * T + c0: PLOFF + fc * T + c0 + n]
    def kPL(fc): return kBIG(PLOFF + fc * T, T)
    MXOFF = PLOFF + 8 * T
    def MXv(fc, c0=None, n=None):
        c0, n = _rng(c0, n)
        return BIG[:, MXOFF + fc * T + c0: MXOFF + fc * T + c0 + n]
    def kMX(fc): return kBIG(MXOFF + fc * T, T)

    def kPS(slot): return (("ps", 2 * slot), ("ps", 2 * slot + 1))
    def PSv(slot, c0=None, n=None):
        c0, n = _rng(c0, n)
        return ps[:, slot * 1024 + c0: slot * 1024 + c0 + n]
    def PSb(bank, c0=0, n=512): return ps[:, bank * 512 + c0: bank * 512 + c0 + n]

    state = {"slot": 0, "bank": 0, "w": 0, "rs": 0, "sg": 0}

    def next_slot():
        s = state["slot"]
        state["slot"] = (s + 1) % 3
        return s

    def next_bank():
        b = state["bank"]
        state["bank"] = (b + 1) % 6
        return b

    STAT = 3

    def pe_mm(out, lhsT, rhs, start, stop, reads, writes):
        S.add("pe", lambda e: e.matmul(out, lhsT=lhsT, rhs=rhs, start=start, stop=stop), reads, writes)

    def act(out, in_, func, reads, writes, scale=None, bias=None, accum=None):
        kw = {}
        if scale is not None:
            kw["scale"] = scale
        if bias is not None:
            kw["bias"] = bias
        if accum is not None:
            kw["accum_out"] = accum
        S.add("act", lambda e: e.activation(out=out, in_=in_, func=func, **kw), reads, writes)

    def dve(fn, reads, writes):
        S.add("dve", fn, reads, writes)

    def wload(parts):
        w = state["w"]
        state["w"] = (w + 1) % NWSLOT
        key = ("W", w)
        for src, off in parts:
            k, n = src.shape[1], src.shape[2]
            dst = WR[:, w * WSLOT + off: w * WSLOT + off + k * n].rearrange("p (k n) -> p k n", k=k)
            S.add("pool", lambda e, dst=dst, src=src: e.dma_start(out=dst, in_=src), (), (key,), dsem=wsem[w])
        base = w * WSLOT
        return (lambda off, n: WR[:, base + off: base + off + n]), key

    def wrows(w2d, r0, nk, c0, n):
        return w2d[r0 * 128:(r0 + nk) * 128, c0:c0 + n].rearrange("(k p) n -> p k n", p=128)

    S.add("sp", lambda e: e.dma_start(out=GV[:], in_=gvec_d), (), (("GV",),), dsem=csem_new())
    S.add("pool", lambda e: e.dma_start(out=PM[:].rearrange("p (a n) -> p a n", a=3),
                                        in_=pm_d.rearrange("a p n -> p a n")), (), (("PM",),), dsem=csem_new())
    S.add("pool", lambda e: e.dma_start(out=GBC[:].rearrange("p (a n) -> p a n", a=2),
                                        in_=lng_d.partition_broadcast(128)), (), (("GBC",),), dsem=csem_new())
    dve(lambda e: e.memset(ONESM[:], 1.0 / 1024.0), (), (("ONESM",),))
    dve(lambda e: e.memset(ONES1[:], 1.0), (), (("ONES1",),))
    dve(lambda e: e.memset(EPSC[:], EPS), (), (("EPSC",),))
    dve(lambda e: e.memset(PH[:], 0.0), (), (("PH", 0), ("PH", 1)))
    dve(lambda e: e.memset(ST[:], 0.0), (), (("ST",),))

    a_layers = sorted({l // 2 for l in layers if l % 2 == 0})
    if a_layers:
        kscr = (kMS(0), kMS(1), kMS(2))
        kscr2 = (kMS(3),)
        S.add("sp", lambda e: e.dma_start(out=MS[:, 2048:2176], in_=maskT_d), (), kscr, dsem=csem_new())
        for j in a_layers:
            S.add("sp", lambda e, j=j: e.dma_start(out=MS[:, 0:1024], in_=wsT_d[j]), (), kscr, dsem=csem_new())
            S.add("sp", lambda e, j=j: e.dma_start(out=MS[:, 2688:3712], in_=bs_d[j].partition_broadcast(128)),
                  (), kscr2, dsem=csem_new())
            dve(lambda e, j=j: e.tensor_tensor(
                out=WSM[:, j * 1024:(j + 1) * 1024].rearrange("p (g t) -> p g t", g=8),
                in0=MS[:, 0:1024].rearrange("p (g t) -> p g t", g=8),
                in1=MS[:, 2048:2176].unsqueeze(1).broadcast_to([128, 8, 128]), op=ALU.mult),
                kscr, (("WSM", j),))
            for g in range(8):
                pe_mm(PSv(0, g * 128, 128), ONES1[:], WSM[:, j * 1024 + g * 128: j * 1024 + (g + 1) * 128],
                      True, True, (("ONES1",), ("WSM", j)), kPS(0))
            for fc in range(16):
                g = fc // 2
                dve(lambda e, j=j, fc=fc, g=g: e.scalar_tensor_tensor(
                    out=BIAS[:, j * 2048 + fc * 128: j * 2048 + (fc + 1) * 128],
                    in0=PSv(0, g * 128, 128), scalar=GV[:, GC_LNB + j * 16 + fc: GC_LNB + j * 16 + fc + 1],
                    in1=MS[:, 2688 + g * 128: 2688 + (g + 1) * 128], op0=ALU.mult, op1=ALU.add),
                    kPS(0) + (("GV",),) + kscr2, (("BIAS", j),))

    def rstd_from_stats():
        r = state["rs"]
        state["rs"] = (r + 1) % len(RS)
        rv = RSv(r)
        act(rv, PSv(STAT), AF.Sqrt, kPS(STAT) + (("EPSC",),), (("RS", r),), bias=EPSC[:])
        dve(lambda e: e.reciprocal(out=rv, in_=rv), (("RS", r),), (("RS", r),))
        return r

    def stats_mm(fc, first, last):
        for (h0, hn) in halves():
            pe_mm(PSv(STAT, h0, hn), ONESM[:], SQv(fc, h0, hn), first, last,
                  kSQ(fc) + (("ONESM",),), kPS(STAT))

    def prenorm(gcol):
        for fc in range(KC):
            act(SQv(fc), Xv(fc), AF.Square, (kX(fc),), kSQ(fc))
            stats_mm(fc, fc == 0, fc == KC - 1)
        r = rstd_from_stats()
        for fc in range(KC):
            o, a, g, rr = Hv(fc), Xv(fc), GV[:, gcol + fc: gcol + fc + 1], RSv(r)
            dve(lambda e, o=o, a=a, g=g, rr=rr: e.scalar_tensor_tensor(
                out=o, in0=a, scalar=g, in1=rr, op0=ALU.mult, op1=ALU.mult),
                (kX(fc), ("GV",), ("RS", r)), (kH(fc),))

    class PostNorm:
        def __init__(self, gcol, final_tile=None):
            self.gcol = gcol
            self.pending = None
            self.final_tile = final_tile

        def evac(self, slot, oc):
            gcol = self.gcol
            act(SQv(oc), PSv(slot), AF.Square, kPS(slot), kSQ(oc))
            act(MSv(oc), PSv(slot), AF.Identity, kPS(slot) + (("GV",),), (kMS(oc),),
                scale=GV[:, gcol + oc: gcol + oc + 1])
            if self.pending is not None:
                stats_mm(self.pending, self.pending == 0, False)
            self.pending = oc

        def finish(self):
            stats_mm(self.pending, self.pending == 0, True)
            r = rstd_from_stats()
            ft = self.final_tile

            def mult(fc):
                o, rr = MSv(fc), RSv(r)
                dve(lambda e: e.tensor_tensor(out=o, in0=o, in1=rr, op=ALU.mult),
                    (kMS(fc), ("RS", r)), (kMS(fc),))

            def add(fc):
                m, x = MSv(fc), Xv(fc)
                if ft is None:
                    dve(lambda e: e.tensor_tensor(out=x, in0=m, in1=x, op=ALU.add),
                        (kMS(fc), kX(fc)), (kX(fc),))
                else:
                    dve(lambda e: e.tensor_tensor(out=m, in0=m, in1=x, op=ALU.add),
                        (kMS(fc), kX(fc)), (kMS(fc),))
                    store_out(ft, fc)
                    if ft + 1 < ntiles:
                        load_x(ft + 1, fc)
            mult(0)
            for fc in range(1, KC):
                mult(fc)
                add(fc - 1)
            add(KC - 1)

    def group_fm(slot, wv, wkey, woff_fn, nk, rhs_fn, rhs_keys_fn):
        for k in range(nk):
            for (h0, hn) in halves():
                pe_mm(PSv(slot, h0, hn), wv(woff_fn(k), 128), rhs_fn(k, h0, hn), k == 0, k == nk - 1,
                      (wkey,) + rhs_keys_fn(k), kPS(slot))

    def ffn(l, final_tile=None):
        prenorm(GC_FFNPRE + l * 8)
        for pi in range(NJ // 2):
            wv, wkey = wload([(wrows(w_gate[l], 0, 8, pi * 256, 256), 0),
                              (wrows(w_up[l], 0, 8, pi * 256, 256), 2048)])
            for jj in range(2):
                j = pi * 2 + jj
                sg_slot = next_slot()
                group_fm(sg_slot, wv, wkey, lambda k, jj=jj: k * 256 + jj * 128, KC,
                         lambda k, h0, hn: Hv(k, h0, hn), lambda k: (kH(k),))
                su_slot = next_slot()
                group_fm(su_slot, wv, wkey, lambda k, jj=jj: 2048 + k * 256 + jj * 128, KC,
                         lambda k, h0, hn: Hv(k, h0, hn), lambda k: (kH(k),))
                q = state["sg"]
                state["sg"] = (q + 1) % len(SG)
                c0_, n_ = _rng(None, None)
                sgv = SG[q][:, c0_:c0_ + n_]
                act(sgv, PSv(sg_slot), AF.Silu, kPS(sg_slot), (("SG", q),))
                o, a = Gv(j), PSv(su_slot)
                dve(lambda e, o=o, a=a, sgv=sgv: e.tensor_tensor(out=o, in0=a, in1=sgv, op=ALU.mult),
                    kPS(su_slot) + (("SG", q),), kG(j))
        pn = PostNorm(GC_FFNPOST + l * 8, final_tile)
        for oc in range(KC):
            wv, wkey = wload([(wrows(w_down[l], 0, NJ, oc * 128, 128), 0)])
            slot = next_slot()
            group_fm(slot, wv, wkey, lambda k: k * 128, NJ,
                     lambda k, h0, hn: Gv(k, h0, hn), lambda k: kG(k))
            pn.evac(slot, oc)
        pn.finish()

    def mixer_a(l):
        j = l // 2
        c0 = cur["c0"]
        nlive = NCH - c0
        prenorm(GC_MIXPRE + l * 8)
        dve(lambda e: e.memset(ST[:, 0:35], 0.0), (), (("ST",),))
        for vb in range(4):
            wv, wkey = wload([(wrows(a_w_in[j], 0, 8, 2048 + vb * 512, 512), 0)])
            for c in range(c0, NCH):
                b = next_bank()
                for k in range(KC):
                    pe_mm(PSb(b), Hv(k, c * 128, 128), wv(k * 512, 512), k == 0, k == KC - 1,
                          (kH(k), wkey), (("ps", b),))
                act(Vv(c, vb * 512, 512), PSb(b), AF.Gelu, (("ps", b),), kV(c) + (("ST",),),
                    accum=ST[:, c * 4 + vb: c * 4 + vb + 1])
        for c in range(c0, NCH):
            act(SQ[:, 0:2048], Vv(c), AF.Square, kV(c), kSQb(0, 2048) + (("ST",),),
                accum=ST[:, 28 + c: 29 + c])
        kst = (("ST",),)
        dve(lambda e: e.tensor_reduce(out=ST[:, 35:42], in_=ST[:, 0:28].rearrange("p (c v) -> p c v", v=4),
                                      axis=AX.X, op=ALU.add), kst, kst)
        dve(lambda e: e.tensor_scalar(out=ST[:, 35:42], in0=ST[:, 35:42], scalar1=1.0 / 2048.0, scalar2=None,
                                      op0=ALU.mult), kst, kst)
        dve(lambda e: e.tensor_tensor(out=ST[:, 42:49], in0=ST[:, 35:42], in1=ST[:, 35:42], op=ALU.mult), kst, kst)
        dve(lambda e: e.scalar_tensor_tensor(out=ST[:, 42:49], in0=ST[:, 28:35], scalar=1.0 / 2048.0,
                                             in1=ST[:, 42:49], op0=ALU.mult, op1=ALU.subtract), kst, kst)
        dve(lambda e: e.tensor_scalar(out=ST[:, 42:49], in0=ST[:, 42:49], scalar1=0.0, scalar2=None,
                                      op0=ALU.max), kst, kst)
        act(ST[:, 49:56], ST[:, 42:49], AF.Sqrt, kst + (("EPSC",),), kst, bias=EPSC[:])
        dve(lambda e: e.reciprocal(out=ST[:, 49:56], in_=ST[:, 49:56]), kst, kst)
        for ub in range(4):
            wv, wkey = wload([(wrows(a_w_in[j], 0, 8, ub * 512, 512), 0)])
            for q in range(4):
                oc = ub * 4 + q
                slot = next_slot()
                group_fm(slot, wv, wkey, lambda k, q=q: k * 512 + q * 128, KC,
                         lambda k, h0, hn: Hv(k, h0, hn), lambda k: (kH(k),))
                act(Uv(oc), PSv(slot), AF.Gelu, kPS(slot), kU(oc))
            if ub == 0:
                for c in range(c0, NCH):
                    dve(lambda e, c=c: e.scalar_tensor_tensor(
                        out=Vv(c), in0=Vv(c), scalar=ST[:, 35 + c: 36 + c],
                        in1=GBC[:, j * 2048:(j + 1) * 2048], op0=ALU.subtract, op1=ALU.mult),
                        kV(c) + kst + (("GBC",),), kV(c))
                    dve(lambda e, c=c: e.tensor_scalar(
                        out=SQ[:, c * 1024:(c + 1) * 1024], in0=WSM[:, j * 1024:(j + 1) * 1024],
                        scalar1=ST[:, 49 + c: 50 + c], scalar2=None, op0=ALU.mult),
                        (("WSM", j),) + kst, kSQb(c * 1024, 1024))
        prev = None

        def gate_mul(pfc, ptmp):
            u, m = Uv(pfc), MSv(ptmp)
            dve(lambda e: e.tensor_tensor(out=u, in0=m, in1=u, op=ALU.mult), (kMS(ptmp),) + kU(pfc), kU(pfc))

        for fc in range(16):
            g = fc // 2
            slot = next_slot()
            for c in range(c0, NCH):
                pe_mm(PSv(slot, c * 128, 128), Vv(c, fc * 128, 128),
                      SQ[:, c * 1024 + g * 128: c * 1024 + (g + 1) * 128], True, True,
                      kV(c) + kSQb(c * 1024 + g * 128, 128), kPS(slot))
            tmp = fc % 8
            o = MSv(tmp).rearrange("p (c t) -> p c t", c=nlive)
            a = PSv(slot).rearrange("p (c t) -> p c t", c=nlive)
            bb = BIAS[:, j * 2048 + fc * 128: j * 2048 + (fc + 1) * 128].unsqueeze(1).broadcast_to([128, nlive, 128])
            dve(lambda e, o=o, a=a, bb=bb: e.tensor_tensor(out=o, in0=a, in1=bb, op=ALU.add),
                kPS(slot) + (("BIAS", j),), (kMS(tmp),))
            if prev is not None:
                gate_mul(*prev)
            prev = (fc, tmp)
        gate_mul(*prev)
        pn = PostNorm(GC_MIXPOST + l * 8)
        for pi in range(4):
            wv, wkey = wload([(wrows(a_w_out[j], 0, 16, pi * 256, 256), 0)])
            for q in range(2):
                oc = pi * 2 + q
                slot = next_slot()
                group_fm(slot, wv, wkey, lambda k, q=q: k * 256 + q * 128, 16,
                         lambda k, h0, hn: Uv(k, h0, hn), lambda k: kU(k))
                pn.evac(slot, oc)
        pn.finish()

    def mixer_b(l, first_chunk):
        j = l // 2
        c0 = cur["c0"]
        cn = max(c0 - 1, 0)
        cur["c0"] = cn
        prenorm(GC_MIXPRE + l * 8)
        for nh in range(2):
            wv, wkey = wload([(wrows(b_w_in[j], 0, 8, nh * 512, 512), 0)])
            for c in range(cn, NCH):
                b = next_bank()
                for k in range(KC):
                    pe_mm(PSb(b), Hv(k, c * 128, 128), wv(k * 512, 512), k == 0, k == KC - 1,
                          (kH(k), wkey), (("ps", b),))
                act(PTv(c, nh * 512, 512), PSb(b), AF.Copy, (("ps", b),), kPT(c))
        cur["c0"] = c0
        wv, wkey = wload([(b_w_grp[j].rearrange("g (dc p) e -> p (g dc) e", p=128), 0)])
        for fc in range(KC):
            g = fc // 2
            slot = next_slot()
            for c in range(c0, NCH):
                pm = 2 if c == first_chunk else 0
                if c == 0:
                    prev_ap, prev_key = PH[:, j * 1024 + fc * 128: j * 1024 + (fc + 1) * 128], (("PH", j),)
                else:
                    prev_ap, prev_key = PTv(c - 1, fc * 128, 128), kPT(c - 1)
                pe_mm(PSv(slot, c * 128, 128), PTv(c, fc * 128, 128),
                      PM[:, pm * 512 + g * 128: pm * 512 + (g + 1) * 128], True, False,
                      kPT(c) + (("PM",),), kPS(slot))
                pe_mm(PSv(slot, c * 128, 128), prev_ap,
                      PM[:, 512 + g * 128: 512 + (g + 1) * 128], False, True,
                      prev_key + (("PM",),), kPS(slot))
            act(PLv(fc), PSv(slot), AF.Copy, kPS(slot), kPL(fc))
        dve(lambda e: e.tensor_copy(out=PH[:, j * 1024:(j + 1) * 1024], in_=PTv(NCH - 1)),
            kPT(NCH - 1), (("PH", j),))
        for ec in range(KC):
            g = ec // 2
            slot = next_slot()
            for dc in range(2):
                for (h0, hn) in halves():
                    pe_mm(PSv(slot, h0, hn), wv((g * 2 + dc) * 256 + (ec % 2) * 128, 128),
                          PLv(2 * g + dc, h0, hn), dc == 0, dc == 1, (wkey,) + kPL(2 * g + dc), kPS(slot))
            act(MXv(ec), PSv(slot), AF.Identity, kPS(slot) + (("GV",),), kMX(ec),
                scale=GV[:, GC_BSCALE + j * 8 + ec: GC_BSCALE + j * 8 + ec + 1])
        pn = PostNorm(GC_MIXPOST + l * 8)
        for pi in range(2):
            wv, wkey = wload([(wrows(b_w_out[j], 0, 8, pi * 512, 512), 0)])
            for q in range(4):
                oc = pi * 4 + q
                slot = next_slot()
                group_fm(slot, wv, wkey, lambda k, q=q: k * 512 + q * 128, KC,
                         lambda k, h0, hn: MXv(k, h0, hn), lambda k: kMX(k))
                pn.evac(slot, oc)
        pn.finish()

    xT3 = xT.rearrange("(fc p) t -> p fc t", p=128)
    yT3 = yT.rearrange("(fc p) t -> p fc t", p=128)
    xsem = [nc.alloc_semaphore("sem_x%d" % i) for i in range(KC)]
    ysem = [nc.alloc_semaphore("sem_y%d" % i) for i in range(KC)]

    def load_x(t, fc):
        dst = X[:, fc * T:(fc + 1) * T]
        src = xT3[:, fc, t * T:(t + 1) * T]
        S.add("sp", lambda e: e.dma_start(out=dst, in_=src), (), (kX(fc),), dsem=xsem[fc])

    def store_out(t, fc):
        if t == 0:
            src = MS[:, fc * T + HALO * 128:(fc + 1) * T]
            dst = yT3[:, fc, 0:(NCH - HALO) * 128]
        else:
            o0 = (NCH - HALO) * 128 + (t - 1) * T
            src = MS[:, fc * T:(fc + 1) * T]
            dst = yT3[:, fc, o0:o0 + T]
        S.add("sp", lambda e: e.dma_start(out=dst, in_=src), (kMS(fc),), (), dsem=ysem[fc])

    full = tuple(layers) == (0, 1, 2, 3)
    live0 = {0: 1, 1: 2, 2: 2, 3: 3} if full else {l: 0 for l in layers}
    for fc in range(KC):
        load_x(0, fc)
    for t in range(ntiles):
        for l in layers:
            cur["c0"] = live0[l] if t == 0 else 0
            if l % 2 == 0:
                mixer_a(l)
            else:
                mixer_b(l, HALO if t == 0 else -1)
            ffn(l, t if l == layers[-1] else None)
    cur["c0"] = 0
    S.emit(nc, engsem)
    return nc


def _host_consts():
    s = np.arange(128)[:, None]
    t = np.arange(128)[None, :]
    maskT = (s <= t).astype(np.float32)
    wins = (2, 4, 8, 16)
    pm = np.zeros((3, 128, 4, 128), np.float32)
    for g, w in enumerate(wins):
        band = ((s <= t) & (s > t - w)).astype(np.float32)
        pm[0, :, g, :] = band / w - np.eye(128, dtype=np.float32)
        bandp = ((s - 128) > (t - w)).astype(np.float32)
        pm[1, :, g, :] = bandp / w
        cnt = np.minimum(t + 1, w).astype(np.float32)
        pm[2, :, g, :] = band / cnt - np.eye(128, dtype=np.float32)
    return maskT, pm.reshape(3, 128, 512)


def _vec_cols(v):
    return np.ascontiguousarray(v.reshape(-1, 128).T)


_NC_CACHE = {}


def _get_nc(layers, ntiles):
    key = (tuple(layers), ntiles)
    if key not in _NC_CACHE:
        _NC_CACHE[key] = build_nc(layers, ntiles)
    return _NC_CACHE[key]


def _make_in_maps(x, a_w_in, a_ln_g, a_ln_b, a_w_s, a_b_s, a_w_out, b_w_in, b_w_grp, b_scale, b_w_out,
                  mix_pre_g, mix_post_g, ffn_pre_g, ffn_post_g, ffn_w_gate, ffn_w_up, ffn_w_down):
    f = np.float32
    B, Sq, _ = x.shape
    maskT, pm = _host_consts()
    gv = np.zeros((128, GC_N), f)
    for l in range(4):
        gv[:, GC_MIXPRE + l * 8: GC_MIXPRE + l * 8 + 8] = _vec_cols(np.asarray(mix_pre_g[l], f))
        gv[:, GC_MIXPOST + l * 8: GC_MIXPOST + l * 8 + 8] = _vec_cols(np.asarray(mix_post_g[l], f))
        gv[:, GC_FFNPRE + l * 8: GC_FFNPRE + l * 8 + 8] = _vec_cols(np.asarray(ffn_pre_g[l], f))
        gv[:, GC_FFNPOST + l * 8: GC_FFNPOST + l * 8 + 8] = _vec_cols(np.asarray(ffn_post_g[l], f))
    for j in range(2):
        gv[:, GC_BSCALE + j * 8: GC_BSCALE + j * 8 + 8] = _vec_cols(np.asarray(b_scale[j], f))
        gv[:, GC_LNB + j * 16: GC_LNB + j * 16 + 16] = _vec_cols(np.asarray(a_ln_b[j], f))
    wsT = np.ascontiguousarray(np.transpose(np.asarray(a_w_s, f), (0, 3, 1, 2))).reshape(2, 128, 1024)
    bs = np.ascontiguousarray(np.asarray(a_b_s, f)).reshape(2, 1024)
    shared = {
        "a_w_in": np.ascontiguousarray(a_w_in, f), "a_w_out": np.ascontiguousarray(a_w_out, f),
        "b_w_in": np.ascontiguousarray(b_w_in, f), "b_w_grp": np.ascontiguousarray(b_w_grp, f),
        "b_w_out": np.ascontiguousarray(b_w_out, f), "ffn_w_gate": np.ascontiguousarray(ffn_w_gate, f),
        "ffn_w_up": np.ascontiguousarray(ffn_w_up, f), "ffn_w_down": np.ascontiguousarray(ffn_w_down, f),
        "gvec": gv, "a_ln_g": np.ascontiguousarray(a_ln_g, f), "a_w_sT": wsT, "a_b_s": bs, "maskT": maskT,
    }
    pm_mid = pm.copy()
    pm_mid[2] = pm_mid[0]
    in_maps = []
    for core in range(NCORES):
        b, half = core // 2, core % 2
        start = half * OWN - HALO * 128
        xw = np.zeros((NTOK, D), f)
        lo = max(start, 0)
        xw[lo - start:, :] = x[b, lo:start + NTOK, :]
        m = dict(shared)
        m["xT"] = np.ascontiguousarray(xw.T)
        m["poolm"] = pm if half == 0 else pm_mid
        in_maps.append(m)
    return in_maps


def kernel(**inputs):
    inputs = {k: np.asarray(v) for k, v in inputs.items()}
    x = inputs["x"].astype(np.float32, copy=False)
    B, Sq, _ = x.shape
    in_maps = _make_in_maps(**inputs)
    nc = _get_nc((0, 1, 2, 3), NTILES)
    res = run_bass_kernel_spmd(nc, in_maps, core_ids=list(range(NCORES)))
    out = np.empty((B, Sq, D), np.float32)
    for core in range(NCORES):
        b, half = core // 2, core % 2
        out[b, half * OWN:(half + 1) * OWN, :] = res.results[core]["yT"].T
    return out
```

```python
import numpy as np
import concourse.bass as bass
import concourse.mybir as mybir
from concourse.bass_utils import run_bass_kernel_spmd

F32 = mybir.dt.float32
BF16 = mybir.dt.bfloat16
AF = mybir.ActivationFunctionType
ALU = mybir.AluOpType
AX = mybir.AxisListType

NCORES = 8
D = 1024
KC = 8
NCH = 7
T = NCH * 128
HALVES = ((0, 512), (512, 384))
NTILES = 5
HALO = 3
NCHUNK = NTILES * NCH
NTOK = NCHUNK * 128
OWN = 4096
DFF = 2816
NJ = DFF // 128
EPS = 1e-6
WSLOT = 4096
NWSLOT = 3
USE_LNEXP = True

GC_MIXPRE, GC_MIXPOST, GC_FFNPRE, GC_FFNPOST, GC_BSCALE, GC_LNB, GC_N = 0, 32, 64, 96, 128, 144, 176


class Op:
    __slots__ = ("eng", "fn", "deps", "sig", "sigidx", "dsem", "dval", "pos", "gidx", "clk", "ckey")

    def __init__(self, eng, fn, dsem=None, dval=0):
        self.eng = eng
        self.fn = fn
        self.deps = []
        self.sig = False
        self.sigidx = 0
        self.dsem = dsem
        self.dval = dval


class Sched:
    ENGS = ("pe", "act", "dve", "pool", "sp")

    def __init__(self):
        self.ops = {e: [] for e in self.ENGS}
        self.res = {}
        self.dma_count = {}
        self.eclk = {e: {} for e in self.ENGS}
        self.gcount = 0

    def add(self, eng, fn, reads=(), writes=(), dsem=None):
        if dsem is not None:
            self.dma_count[dsem] = self.dma_count.get(dsem, 0) + 16
            op = Op(eng, fn, dsem, self.dma_count[dsem])
            op.ckey = ("dma", id(dsem))
            op.pos = op.dval
        else:
            op = Op(eng, fn)
            op.ckey = eng
            op.pos = len(self.ops[eng]) + 1
        self.gcount += 1
        op.gidx = self.gcount
        deps = {}
        res = self.res
        for k in reads:
            rec = res.get(k)
            if rec is not None and rec[0] is not None:
                deps[id(rec[0])] = (rec[0], "RAW")
        for k in writes:
            rec = res.get(k)
            if rec is not None:
                if rec[0] is not None and id(rec[0]) not in deps:
                    deps[id(rec[0])] = (rec[0], "WAW")
                for r in rec[1]:
                    if id(r) not in deps:
                        deps[id(r)] = (r, "WAR")
        cand = []
        for d, kind in deps.values():
            if d is op:
                continue
            if d.dsem is None and op.dsem is None and d.eng == eng:
                if eng == "pe":
                    continue
            cand.append(d)
        ek = self.eclk[eng]
        cand.sort(key=lambda d: -d.gidx)
        for d in cand:
            if ek.get(d.ckey, 0) >= d.pos:
                continue
            op.deps.append(d)
            for k, v in d.clk.items():
                if ek.get(k, 0) < v:
                    ek[k] = v
        clk = dict(ek)
        if op.dsem is None:
            clk[op.ckey] = op.pos
        else:
            clk[op.ckey] = op.pos
        op.clk = clk
        for k in reads:
            rec = res.get(k)
            if rec is None:
                res[k] = [None, [op]]
            else:
                rec[1].append(op)
        for k in writes:
            res[k] = [op, []]
        self.ops[eng].append(op)
        return op

    def emit(self, nc, engsem):
        for e in self.ENGS:
            for op in self.ops[e]:
                for d in op.deps:
                    if d.dsem is None:
                        d.sig = True
        for e in self.ENGS:
            n = 0
            for op in self.ops[e]:
                if op.dsem is None and op.sig:
                    n += 1
                    op.sigidx = n
        ops = self.ops

        def run(eng_name, eng):
            seen = {}
            for op in ops[eng_name]:
                need = {}
                for d in op.deps:
                    if d.dsem is not None:
                        sem, val = d.dsem, d.dval
                    else:
                        sem, val = engsem[d.eng], d.sigidx
                    k = id(sem)
                    if k not in need or need[k][1] < val:
                        need[k] = (sem, val)
                for k, (sem, val) in need.items():
                    if seen.get(k, 0) < val:
                        eng.wait_ge(sem, val)
                        seen[k] = val
                inst = op.fn(eng)
                if op.dsem is not None:
                    inst.then_inc(op.dsem, 16)
                elif op.sig:
                    inst.then_inc(engsem[eng_name], 1)

        with nc.Block() as block:
            @block.tensor
            def _(e):
                run("pe", e)

            @block.scalar
            def _(e):
                run("act", e)

            @block.vector
            def _(e):
                run("dve", e)

            @block.gpsimd
            def _(e):
                run("pool", e)

            @block.sync
            def _(e):
                run("sp", e)
                for sem, cnt in self.dma_count.items():
                    e.wait_ge(sem, cnt)


def build_nc(layers=(0, 1, 2, 3), ntiles=NTILES):
    nc = bass.Bass("TRN2", target_bir_lowering=False)
    S = Sched()

    def dram(name, shape, dt=F32, kind="ExternalInput"):
        return nc.dram_tensor(name, list(shape), dt, kind=kind).ap()

    xT = dram("xT", [D, NTOK])
    yT = dram("yT", [D, OWN], kind="ExternalOutput")
    a_w_in = dram("a_w_in", [2, D, 4096])
    a_w_out = dram("a_w_out", [2, 2048, D])
    b_w_in = dram("b_w_in", [2, D, D])
    b_w_grp = dram("b_w_grp", [2, 4, 256, 256])
    b_w_out = dram("b_w_out", [2, D, D])
    w_gate = dram("ffn_w_gate", [4, D, DFF])
    w_up = dram("ffn_w_up", [4, D, DFF])
    w_down = dram("ffn_w_down", [4, DFF, D])
    gvec_d = dram("gvec", [128, GC_N])
    lng_d = dram("a_ln_g", [2, 2048])
    wsT_d = dram("a_w_sT", [2, 128, 1024])
    bs_d = dram("a_b_s", [2, 1024])
    maskT_d = dram("maskT", [128, 128])
    pm_d = dram("poolm", [3, 128, 512])

    sb = nc.alloc_sbuf_tensor
    X = sb("X", [128, KC * T], F32)
    H = sb("H", [128, KC * T], BF16)
    MS = sb("MS", [128, KC * T], F32)
    SQ = sb("SQ", [128, KC * T], BF16)
    BIG = sb("BIG", [128, 28672], BF16)
    WR = sb("WR", [128, NWSLOT * WSLOT], BF16)
    RS = [sb("RS0", [128, T], F32)]
    SG = [sb("SG0", [128, T], BF16)]
    GV = sb("GV", [128, GC_N], F32)
    GBC = sb("GBC", [128, 2 * 2048], BF16)
    WSM = sb("WSM", [128, 2 * 1024], BF16)
    BIAS = sb("BIAS", [128, 2 * 2048], F32)
    PM = sb("PM", [128, 3 * 512], BF16)
    PH = sb("PH", [128, 2 * 1024], BF16)
    ONESM = sb("ONESM", [128, 128], BF16)
    ONES1 = sb("ONES1", [128, 128], BF16)
    EPSC = sb("EPSC", [128, 1], F32)
    ST = sb("STATS", [128, 64], F32)
    ps = nc.alloc_psum_tensor("ps", [128, 4096], F32)

    engsem = {e: nc.alloc_semaphore("sem_" + e) for e in ("pe", "act", "dve")}
    wsem = [nc.alloc_semaphore("sem_w%d" % i) for i in range(NWSLOT)]
    xsem = nc.alloc_semaphore("sem_x")
    ysem = nc.alloc_semaphore("sem_y")
    _cs = [0]

    def csem_new():
        _cs[0] += 1
        return nc.alloc_semaphore("sem_c%d" % _cs[0])

    class Half:
        def __init__(self, h, col, n):
            self.h, self.col, self.n = h, col, n
            self.c_lo, self.c_hi = col // 128, (col + n) // 128

    cur = {"hc": None}

    def rng():
        hc = cur["hc"]
        return hc.col, hc.n

    def on(hc, fn):
        def run():
            old = cur["hc"]
            cur["hc"] = hc
            try:
                fn()
            finally:
                cur["hc"] = old
        return run

    def fm(buf, fc):
        c, n = rng()
        return buf[:, fc * T + c: fc * T + c + n]

    def Xv(fc): return fm(X, fc)
    def Hv(fc): return fm(H, fc)
    def MSv(fc): return fm(MS, fc)
    def SQv(fc): return fm(SQ, fc)
    def kX(fc): return ("X", fc, cur["hc"].h)
    def kH(fc): return ("H", fc, cur["hc"].h)
    def kMS(fc): return ("MS", fc, cur["hc"].h)

    def blocks(name, lo, n):
        return tuple((name, b) for b in range(lo // 128, (lo + n + 127) // 128))

    def kSQ(fc):
        c, n = rng()
        return blocks("SQ", fc * T + c, n)

    def bigfm(off, fc):
        c, n = rng()
        return BIG[:, off + fc * T + c: off + fc * T + c + n], blocks("BIG", off + fc * T + c, n)

    VOFF = 16 * T
    PLOFF = 7 * 1024
    MXOFF = PLOFF + 8 * T
    def Uv(fc): return bigfm(0, fc)[0]
    def kU(fc): return bigfm(0, fc)[1]
    def Gv(j): return bigfm(0, j)[0]
    def kG(j): return bigfm(0, j)[1]
    def PLv(fc): return bigfm(PLOFF, fc)[0]
    def kPL(fc): return bigfm(PLOFF, fc)[1]
    def MXv(fc): return bigfm(MXOFF, fc)[0]
    def kMX(fc): return bigfm(MXOFF, fc)[1]
    def Vv(c, c0=0, n=2048): return BIG[:, VOFF + c * 2048 + c0: VOFF + c * 2048 + c0 + n]
    def kV(c, c0=0, n=2048): return blocks("BIG", VOFF + c * 2048 + c0, n)
    def PTv(c, c0=0, n=1024): return BIG[:, c * 1024 + c0: c * 1024 + c0 + n]
    def kPT(c, c0=0, n=1024): return blocks("BIG", c * 1024 + c0, n)

    def RSv():
        c, n = rng()
        return RS[0][:, c:c + n]
    def kRS(): return ("RS", cur["hc"].h)
    def SGv():
        c, n = rng()
        return SG[0][:, c:c + n]
    def kSG(): return ("SG", cur["hc"].h)

    def PSH(bank, c0=0, n=None):
        if n is None:
            n = rng()[1]
        return ps[:, bank * 512 + c0: bank * 512 + c0 + n]
    def kPSb(bank): return (("ps", bank),)
    def STATb(): return 6 + cur["hc"].h

    state = {"bank": 0, "w": 0}

    def next_bank():
        b = state["bank"]
        state["bank"] = (b + 1) % 6
        return b

    def pe_mm(out, lhsT, rhs, start, stop, reads, writes):
        S.add("pe", lambda e: e.matmul(out, lhsT=lhsT, rhs=rhs, start=start, stop=stop), reads, writes)

    def act(out, in_, func, reads, writes, scale=None, bias=None, accum=None):
        kw = {}
        if scale is not None:
            kw["scale"] = scale
        if bias is not None:
            kw["bias"] = bias
        if accum is not None:
            kw["accum_out"] = accum
        S.add("act", lambda e: e.activation(out=out, in_=in_, func=func, **kw), reads, writes)

    def dve(fn, reads, writes):
        S.add("dve", fn, reads, writes)

    def wload(parts):
        w = state["w"]
        state["w"] = (w + 1) % NWSLOT
        state["last_slot"] = w
        key = ("W", w)
        for src, off in parts:
            k, n = src.shape[1], src.shape[2]
            dst = WR[:, w * WSLOT + off: w * WSLOT + off + k * n].rearrange("p (k n) -> p k n", k=k)
            S.add("pool", lambda e, dst=dst, src=src: e.dma_start(out=dst, in_=src), (), (key,), dsem=wsem[w])
        base = w * WSLOT
        return (lambda off, n: WR[:, base + off: base + off + n]), key

    def wrows(w2d, r0, nk, c0, n):
        return w2d[r0 * 128:(r0 + nk) * 128, c0:c0 + n].rearrange("(k p) n -> p k n", p=128)

    S.add("sp", lambda e: e.dma_start(out=GV[:], in_=gvec_d), (), (("GV",),), dsem=csem_new())
    S.add("pool", lambda e: e.dma_start(out=PM[:].rearrange("p (a n) -> p a n", a=3),
                                        in_=pm_d.rearrange("a p n -> p a n")), (), (("PM",),), dsem=csem_new())
    S.add("pool", lambda e: e.dma_start(out=GBC[:].rearrange("p (a n) -> p a n", a=2),
                                        in_=lng_d.partition_broadcast(128)), (), (("GBC",),), dsem=csem_new())
    dve(lambda e: e.memset(ONESM[:], 1.0 / 1024.0), (), (("ONESM",),))
    dve(lambda e: e.memset(ONES1[:], 1.0), (), (("ONES1",),))
    dve(lambda e: e.memset(EPSC[:], EPS), (), (("EPSC",),))
    dve(lambda e: e.memset(PH[:], 0.0), (), (("PH", 0), ("PH", 1)))
    dve(lambda e: e.memset(ST[:], 0.0), (), (("ST", 0), ("ST", 1)))

    a_layers = sorted({l // 2 for l in layers if l % 2 == 0})
    if a_layers:
        kscr = tuple(("MS", f, h) for f in range(3) for h in range(2))
        kscr2 = tuple(("MS", f, h) for f in (3, 4) for h in range(2))
        S.add("sp", lambda e: e.dma_start(out=MS[:, 2048:2176], in_=maskT_d), (), kscr, dsem=csem_new())
        for j in a_layers:
            S.add("sp", lambda e, j=j: e.dma_start(out=MS[:, 0:1024], in_=wsT_d[j]), (), kscr, dsem=csem_new())
            S.add("sp", lambda e, j=j: e.dma_start(out=MS[:, 2688:3712], in_=bs_d[j].partition_broadcast(128)),
                  (), kscr2, dsem=csem_new())
            dve(lambda e, j=j: e.tensor_tensor(
                out=WSM[:, j * 1024:(j + 1) * 1024].rearrange("p (g t) -> p g t", g=8),
                in0=MS[:, 0:1024].rearrange("p (g t) -> p g t", g=8),
                in1=MS[:, 2048:2176].unsqueeze(1).broadcast_to([128, 8, 128]), op=ALU.mult),
                kscr, (("WSM", j),))
            for g in range(8):
                pe_mm(ps[:, g * 128:(g + 1) * 128], ONES1[:], WSM[:, j * 1024 + g * 128: j * 1024 + (g + 1) * 128],
                      True, True, (("ONES1",), ("WSM", j)), (("ps", 0), ("ps", 1)))
            for fc in range(16):
                g = fc // 2
                dve(lambda e, j=j, fc=fc, g=g: e.scalar_tensor_tensor(
                    out=BIAS[:, j * 2048 + fc * 128: j * 2048 + (fc + 1) * 128],
                    in0=ps[:, g * 128:(g + 1) * 128], scalar=GV[:, GC_LNB + j * 16 + fc: GC_LNB + j * 16 + fc + 1],
                    in1=MS[:, 2688 + g * 128: 2688 + (g + 1) * 128], op0=ALU.mult, op1=ALU.add),
                    (("ps", 0), ("ps", 1), ("GV",)) + kscr2, (("BIAS", j),))

    def stats_mm(fc, first, last):
        sb_ = STATb()
        pe_mm(PSH(sb_), ONESM[:], SQv(fc), first, last, kSQ(fc) + (("ONESM",),), kPSb(sb_))

    def rstd_ops():
        sb_ = STATb()
        rv = RSv()
        if USE_LNEXP:
            return [
                lambda: act(rv, PSH(sb_, 0, rv.shape[1]), AF.Ln, kPSb(sb_) + (("EPSC",),), (kRS(),), bias=EPSC[:]),
                lambda: act(rv, rv, AF.Exp, (kRS(),), (kRS(),), scale=-0.5),
            ]
        return [
            lambda: act(rv, PSH(sb_, 0, rv.shape[1]), AF.Sqrt, kPSb(sb_) + (("EPSC",),), (kRS(),), bias=EPSC[:]),
            lambda: dve(lambda e: e.reciprocal(out=rv, in_=rv), (kRS(),), (kRS(),)),
        ]

    def prenorm_thunks(gcol):
        th = []

        def sq(fc):
            act(SQv(fc), Xv(fc), AF.Square, (kX(fc),), kSQ(fc))
            stats_mm(fc, fc == 0, fc == KC - 1)

        def hh(fc):
            o, a, g, rr = Hv(fc), Xv(fc), GV[:, gcol + fc: gcol + fc + 1], RSv()
            dve(lambda e: e.scalar_tensor_tensor(out=o, in0=a, scalar=g, in1=rr, op0=ALU.mult, op1=ALU.mult),
                (kX(fc), ("GV",), kRS()), (kH(fc),))
        for fc in range(KC):
            th.append(lambda fc=fc: sq(fc))
        th.extend(rstd_ops())
        for fc in range(KC):
            th.append(lambda fc=fc: hh(fc))
        return th

    class PostNorm:
        def __init__(self, gcol):
            self.gcol = gcol
            self.pending = None
            self.nstat = 0

        def evac(self, bank, oc):
            gcol = self.gcol
            act(SQv(oc), PSH(bank), AF.Square, kPSb(bank), kSQ(oc))
            act(MSv(oc), PSH(bank), AF.Identity, kPSb(bank) + (("GV",),), (kMS(oc),),
                scale=GV[:, gcol + oc: gcol + oc + 1])
            if self.pending is not None:
                stats_mm(self.pending, self.nstat == 0, False)
                self.nstat += 1
            self.pending = oc

        def chain(self, final_tile, next_pre):
            hc = cur["hc"]
            th = [lambda: stats_mm(self.pending, self.nstat == 0, True)]
            th.extend(rstd_ops())
            ft = final_tile
            sqs = []
            if next_pre is not None:
                hcn, gcoln = next_pre
                pre = [on(hcn, t) for t in on_build(hcn, lambda: prenorm_thunks(gcoln))]
            else:
                pre = []

            def mult(fc):
                o, rr = MSv(fc), RSv()
                dve(lambda e: e.tensor_tensor(out=o, in0=o, in1=rr, op=ALU.mult), (kMS(fc), kRS()), (kMS(fc),))

            def add(fc):
                m, x = MSv(fc), Xv(fc)
                if ft is None:
                    dve(lambda e: e.tensor_tensor(out=x, in0=m, in1=x, op=ALU.add), (kMS(fc), kX(fc)), (kX(fc),))
                else:
                    dve(lambda e: e.tensor_tensor(out=m, in0=m, in1=x, op=ALU.add), (kMS(fc), kX(fc)), (kMS(fc),))
                    store_out(ft, fc)
                    if ft + 1 < ntiles:
                        load_x(ft + 1, fc)
            n = hc.n
            d = (n + 60) / 960.0 * 1e-3 * 1e3
            a = (n + 240) / 1200.0
            r1 = 1.0 + 1.3 + 2 * a + 0.5 + (0.0 if USE_LNEXP else 2.5)
            times = [0.8, 1.0, 1.0]
            th.append(lambda: mult(0))
            times.append(r1)
            i = 1
            for fc in range(1, KC):
                th.append(lambda fc=fc: mult(fc))
                times.append(r1 + i * d)
                i += 1
                th.append(lambda fc=fc: add(fc - 1))
                times.append(r1 + i * d)
                i += 1
                if fc - 1 < len(pre) and fc - 1 < KC:
                    th.append(pre[fc - 1])
                    times.append(r1 + i * d)
            th.append(lambda: add(KC - 1))
            times.append(r1 + i * d)
            i += 1
            tl = r1 + i * d
            rest = pre[KC - 1:]
            if rest:
                r2 = tl + a + 0.3 + 1.3 + 2 * a + 0.5 + (0.0 if USE_LNEXP else 2.5)
                rt = [tl, tl + a + 0.3, tl + a + 0.3] + [r2 + k * d for k in range(KC)]
                th.extend(rest)
                times.extend(rt[:len(rest)])
            assert len(times) == len(th), (len(times), len(th))
            return [(tm, on(hc, t)) for tm, t in zip(times, th)]

    def on_build(hc, fn):
        old = cur["hc"]
        cur["hc"] = hc
        try:
            return fn()
        finally:
            cur["hc"] = old

    def groups_fm(banks_woffs, wv, wkey, ks, kfirst, klast, rhs_fn, rhs_keys_fn, kloc=None):
        for k in ks:
            kk = k if kloc is None else kloc(k)
            for bank, woff_fn in banks_woffs:
                pe_mm(PSH(bank), wv(woff_fn(kk), 128), rhs_fn(k), k == kfirst, k == klast,
                      (wkey,) + rhs_keys_fn(k), kPSb(bank))
            tick(len(banks_woffs) * (rng()[1] + 10) / 2400.0)

    slot_owner = {}

    class Piece:
        def __init__(self, parts, work, group=None, ipart=0, npart=1):
            self.parts, self.work = parts, work
            self.w = None
            self.slot = None
            self.group, self.ipart, self.npart = group, ipart, npart

        def resident(self):
            return self.parts is not None and self.slot is not None and slot_owner.get(self.slot) is self

        def load(self):
            if self.parts is None:
                self.w = (None, None)
            elif not self.resident():
                self.w = wload(self.parts)
                self.slot = state["last_slot"]
                slot_owner[self.slot] = self

        def run(self, hc, first=True, last=True):
            on(hc, lambda: self.work(self.w[0], self.w[1], first, last))()

    def natural_order(tail):
        return [(p, p.ipart == 0, p.ipart == p.npart - 1) for p in tail]

    def reuse_order(tail):
        groups = []
        for p in tail:
            if p.group not in groups:
                groups.append(p.group)
        parts = {g: [p for p in tail if p.group == g] for g in groups}
        res = {id(p) for p in tail if p.resident()}
        comp = [g for g in groups if all(id(p) in res for p in parts[g])]
        part = [g for g in groups if g not in comp and any(id(p) in res for p in parts[g])]
        rest = [g for g in groups if g not in comp and g not in part]
        order = []
        for g in comp + part + rest:
            ps = sorted(parts[g], key=lambda p: (0 if id(p) in res else 1, p.ipart))
            for i, p in enumerate(ps):
                order.append((p, i == 0, i == len(ps) - 1))
        return order

    bg = []
    clock = {"t": 0.0}

    def tick(dt):
        clock["t"] += dt
        while bg and bg[0][0] <= clock["t"]:
            bg.pop(0)[1]()

    def drain_all():
        while bg:
            bg.pop(0)[1]()

    def add_chain(items):
        t0 = clock["t"]
        for tm, th in items:
            bg.append((t0 + tm, th))

    def run_sublayer(hA, hB, head, mid, tail, make_chain, reuse=False, _u=None):
        for p in head:
            p.load()
        for p in head:
            p.run(hA)
        if head or mid:
            drain_all()
        for p in head:
            p.run(hB)
        for p in mid:
            p.load()
            p.run(hA)
            p.run(hB)
        for p, first, last in natural_order(tail):
            p.load()
            p.run(hA, first, last)
        drain_all()
        add_chain(make_chain(hA))
        for p, first, last in (reuse_order(tail) if reuse else natural_order(tail)):
            p.load()
            p.run(hB, first, last)
        drain_all()
        add_chain(make_chain(hB))

    def ffn(l, hA, hB, final_tile, next_pre_fn):
        sgate = {}

        def gu_work(pi):
            def work(wv, wkey, first=True, last=True):
                for jj in range(2):
                    j = pi * 2 + jj
                    bg_ = next_bank()
                    bu_ = next_bank()
                    groups_fm([(bg_, lambda k: k * 256 + jj * 128), (bu_, lambda k: 2048 + k * 256 + jj * 128)],
                              wv, wkey, range(KC), 0, KC - 1, lambda k: Hv(k), lambda k: (kH(k),))
                    sgv = SGv()
                    act(sgv, PSH(bg_), AF.Silu, kPSb(bg_), (kSG(),))
                    o, a = Gv(j), PSH(bu_)
                    dve(lambda e, o=o, a=a, sgv=sgv: e.tensor_tensor(out=o, in0=a, in1=sgv, op=ALU.mult),
                        kPSb(bu_) + (kSG(),), kG(j))
            return work
        pieces = [Piece([(wrows(w_gate[l], 0, 8, pi * 256, 256), 0), (wrows(w_up[l], 0, 8, pi * 256, 256), 2048)],
                        gu_work(pi)) for pi in range(NJ // 2)]
        pns = {0: PostNorm(GC_FFNPOST + l * 8), 1: PostNorm(GC_FFNPOST + l * 8)}

        dbanks = {}

        def down_work(pair, kh):
            def work(wv, wkey, first=True, last=True):
                h = cur["hc"].h
                if first:
                    dbanks[(h, pair)] = (next_bank(), next_bank())
                b0, b1 = dbanks[(h, pair)]
                ks = range(kh * 11, kh * 11 + 11)
                groups_fm([(b0, lambda kk: kk * 256), (b1, lambda kk: kk * 256 + 128)], wv, wkey,
                          ks, ks[0] if first else None, ks[-1] if last else None,
                          lambda k: Gv(k), lambda k: kG(k), kloc=lambda k: k - kh * 11)
                if last:
                    pns[h].evac(b0, pair * 2)
                    pns[h].evac(b1, pair * 2 + 1)
            return work
        tail = [Piece([(wrows(w_down[l], kh * 11, 11, pair * 256, 256), 0)], down_work(pair, kh),
                      group=pair, ipart=kh, npart=2)
                for pair in range(KC // 2) for kh in range(2)]

        def make_chain(hc):
            return on_build(hc, lambda: pns[hc.h].chain(final_tile, next_pre_fn(hc.h)))
        run_sublayer(hA, hB, pieces[:3], pieces[3:], tail, make_chain, reuse=True)

    def mixer_a(l, hA, hB, next_pre_fn):
        j = l // 2

        def v_work(vb):
            def work(wv, wkey, first=True, last=True):
                hc = cur["hc"]
                cs_all = list(range(hc.c_lo, hc.c_hi))
                for i0 in range(0, len(cs_all), 2):
                    cs = cs_all[i0:i0 + 2]
                    bks = [next_bank() for _ in cs]
                    for k in range(KC):
                        for c, b in zip(cs, bks):
                            pe_mm(ps[:, b * 512:(b + 1) * 512], H[:, k * T + c * 128: k * T + (c + 1) * 128],
                                  wv(k * 512, 512), k == 0, k == KC - 1, (kH(k), wkey), kPSb(b))
                        tick(len(cs) * 522 / 2400.0)
                    for c, b in zip(cs, bks):
                        act(Vv(c, vb * 512, 512), ps[:, b * 512:(b + 1) * 512], AF.Gelu, kPSb(b),
                            kV(c, vb * 512, 512) + (("ST", hc.h),), accum=ST[:, c * 4 + vb: c * 4 + vb + 1])
            return work

        def ln_stats():
            hc = cur["hc"]
            lo, hi = hc.c_lo, hc.c_hi
            kst = (("ST", hc.h),)
            junk = SQ[:, 0:2048] if hc.h == 0 else SQ[:, 4096:6144]
            kjunk = blocks("SQ", 0 if hc.h == 0 else 4096, 2048)
            for c in range(lo, hi):
                act(junk, Vv(c), AF.Square, kV(c), kjunk + kst, accum=ST[:, 28 + c: 29 + c])
            dve(lambda e: e.tensor_reduce(out=ST[:, 35 + lo:35 + hi],
                                          in_=ST[:, 4 * lo:4 * hi].rearrange("p (c v) -> p c v", v=4),
                                          axis=AX.X, op=ALU.add), kst, kst)
            dve(lambda e: e.tensor_scalar(out=ST[:, 35 + lo:35 + hi], in0=ST[:, 35 + lo:35 + hi],
                                          scalar1=1.0 / 2048.0, scalar2=None, op0=ALU.mult), kst, kst)
            dve(lambda e: e.tensor_tensor(out=ST[:, 42 + lo:42 + hi], in0=ST[:, 35 + lo:35 + hi],
                                          in1=ST[:, 35 + lo:35 + hi], op=ALU.mult), kst, kst)
            dve(lambda e: e.scalar_tensor_tensor(out=ST[:, 42 + lo:42 + hi], in0=ST[:, 28 + lo:28 + hi],
                                                 scalar=1.0 / 2048.0, in1=ST[:, 42 + lo:42 + hi],
                                                 op0=ALU.mult, op1=ALU.subtract), kst, kst)
            dve(lambda e: e.tensor_scalar(out=ST[:, 42 + lo:42 + hi], in0=ST[:, 42 + lo:42 + hi], scalar1=0.0,
                                          scalar2=None, op0=ALU.max), kst, kst)
            if USE_LNEXP:
                act(ST[:, 49 + lo:49 + hi], ST[:, 42 + lo:42 + hi], AF.Ln, kst + (("EPSC",),), kst, bias=EPSC[:])
                act(ST[:, 49 + lo:49 + hi], ST[:, 49 + lo:49 + hi], AF.Exp, kst, kst, scale=-0.5)
            else:
                act(ST[:, 49 + lo:49 + hi], ST[:, 42 + lo:42 + hi], AF.Sqrt, kst + (("EPSC",),), kst, bias=EPSC[:])
                dve(lambda e: e.reciprocal(out=ST[:, 49 + lo:49 + hi], in_=ST[:, 49 + lo:49 + hi]), kst, kst)

        def ln_apply():
            hc = cur["hc"]
            kst = (("ST", hc.h),)
            for c in range(hc.c_lo, hc.c_hi):
                dve(lambda e, c=c: e.scalar_tensor_tensor(
                    out=Vv(c), in0=Vv(c), scalar=ST[:, 35 + c: 36 + c],
                    in1=GBC[:, j * 2048:(j + 1) * 2048], op0=ALU.subtract, op1=ALU.mult),
                    kV(c) + kst + (("GBC",),), kV(c))
                dve(lambda e, c=c: e.tensor_scalar(
                    out=SQ[:, c * 1024:(c + 1) * 1024], in0=WSM[:, j * 1024:(j + 1) * 1024],
                    scalar1=ST[:, 49 + c: 50 + c], scalar2=None, op0=ALU.mult),
                    (("WSM", j),) + kst, blocks("SQ", c * 1024, 1024))

        def u_work(ub):
            def work(wv, wkey, first=True, last=True):
                for qp in range(2):
                    bs_ = [next_bank(), next_bank()]
                    groups_fm([(bs_[i], lambda k, q=qp * 2 + i: k * 512 + q * 128) for i in range(2)],
                              wv, wkey, range(KC), 0, KC - 1, lambda k: Hv(k), lambda k: (kH(k),))
                    for i in range(2):
                        oc = ub * 4 + qp * 2 + i
                        act(Uv(oc), PSH(bs_[i]), AF.Gelu, kPSb(bs_[i]), kU(oc))
                if ub == 0:
                    ln_stats()
                if ub == 1:
                    ln_apply()
            return work

        def sp_work(fcs):
            def work(wv, wkey, first=True, last=True):
                hc = cur["hc"]
                lo, hi = hc.c_lo, hc.c_hi
                nl = hi - lo
                prev = None

                def gate_mul(pfc, ptmp):
                    u, m = Uv(pfc), MSv(ptmp)
                    dve(lambda e: e.tensor_tensor(out=u, in0=m, in1=u, op=ALU.mult), (kMS(ptmp),) + kU(pfc), kU(pfc))
                for fc in fcs:
                    g = fc // 2
                    b = next_bank()
                    for c in range(lo, hi):
                        pe_mm(PSH(b, (c - lo) * 128, 128), Vv(c, fc * 128, 128),
                              SQ[:, c * 1024 + g * 128: c * 1024 + (g + 1) * 128], True, True,
                              kV(c, fc * 128, 128) + blocks("SQ", c * 1024 + g * 128, 128), kPSb(b))
                    tick(nl * 0.1)
                    tmp = fc % 8
                    o = MSv(tmp).rearrange("p (c t) -> p c t", c=nl)
                    a = PSH(b).rearrange("p (c t) -> p c t", c=nl)
                    bb = BIAS[:, j * 2048 + fc * 128: j * 2048 + (fc + 1) * 128].unsqueeze(1).broadcast_to([128, nl, 128])
                    dve(lambda e, o=o, a=a, bb=bb: e.tensor_tensor(out=o, in0=a, in1=bb, op=ALU.add),
                        kPSb(b) + (("BIAS", j),), (kMS(tmp),))
                    if prev is not None:
                        gate_mul(*prev)
                    prev = (fc, tmp)
                gate_mul(*prev)
            return work

        def zero_st():
            dve(lambda e: e.memset(ST[:, 0:35], 0.0), (), (("ST", 0), ("ST", 1)))
        zero_st()
        vp = [Piece([(wrows(a_w_in[j], 0, 8, 2048 + vb * 512, 512), 0)], v_work(vb)) for vb in range(4)]
        up = [Piece([(wrows(a_w_in[j], 0, 8, ub * 512, 512), 0)], u_work(ub)) for ub in range(4)]
        spp = [Piece(None, sp_work(range(q * 4, q * 4 + 4))) for q in range(4)]
        pns = {0: PostNorm(GC_MIXPOST + l * 8), 1: PostNorm(GC_MIXPOST + l * 8)}

        def out_work(pi):
            def work(wv, wkey, first=True, last=True):
                bs_ = [next_bank(), next_bank()]
                groups_fm([(bs_[q], lambda k, q=q: k * 256 + q * 128) for q in range(2)],
                          wv, wkey, range(16), 0, 15, lambda k: Uv(k), lambda k: kU(k))
                for q in range(2):
                    pns[cur["hc"].h].evac(bs_[q], pi * 2 + q)
            return work
        tail = [Piece([(wrows(a_w_out[j], 0, 16, pi * 256, 256), 0)], out_work(pi), group=pi) for pi in range(4)]

        def make_chain(hc):
            return on_build(hc, lambda: pns[hc.h].chain(None, next_pre_fn(hc.h)))
        run_sublayer(hA, hB, vp[:3], vp[3:] + up + spp, tail, make_chain, reuse=True)

    def mixer_b(l, hA, hB, hAc, hBc, first_chunk, next_pre_fn):
        j = l // 2

        def p_work(nh):
            def work(wv, wkey, first=True, last=True):
                hc = cur["hc"]
                cs_all = list(range(hc.c_lo, hc.c_hi))
                for i0 in range(0, len(cs_all), 2):
                    cs = cs_all[i0:i0 + 2]
                    bks = [next_bank() for _ in cs]
                    for k in range(KC):
                        for c, b in zip(cs, bks):
                            pe_mm(ps[:, b * 512:(b + 1) * 512], H[:, k * T + c * 128: k * T + (c + 1) * 128],
                                  wv(k * 512, 512), k == 0, k == KC - 1, (kH(k), wkey), kPSb(b))
                        tick(len(cs) * 522 / 2400.0)
                    for c, b in zip(cs, bks):
                        act(PTv(c, nh * 512, 512), ps[:, b * 512:(b + 1) * 512], AF.Copy, kPSb(b),
                            kPT(c, nh * 512, 512))
            return work

        def cmap(hc):
            return hAc if hc.h == 0 else hBc

        def pool_work(wv, wkey, first=True, last=True):
            hc = cmap(cur["hc"])
            def inner():
                lo, hi = hc.c_lo, hc.c_hi
                for fc in range(KC):
                    g = fc // 2
                    b = next_bank()
                    for c in range(lo, hi):
                        pm = 2 if c == first_chunk else 0
                        if c == 0:
                            prev_ap, prev_key = PH[:, j * 1024 + fc * 128: j * 1024 + (fc + 1) * 128], (("PH", j),)
                        else:
                            prev_ap, prev_key = PTv(c - 1, fc * 128, 128), kPT(c - 1, fc * 128, 128)
                        pe_mm(PSH(b, (c - lo) * 128, 128), PTv(c, fc * 128, 128),
                              PM[:, pm * 512 + g * 128: pm * 512 + (g + 1) * 128], True, False,
                              kPT(c, fc * 128, 128) + (("PM",),), kPSb(b))
                        pe_mm(PSH(b, (c - lo) * 128, 128), prev_ap,
                              PM[:, 512 + g * 128: 512 + (g + 1) * 128], False, True,
                              prev_key + (("PM",),), kPSb(b))
                    tick((hi - lo) * 0.2)
                    act(PLv(fc), PSH(b), AF.Copy, kPSb(b), kPL(fc))
                if hc.h == 1:
                    dve(lambda e: e.tensor_copy(out=PH[:, j * 1024:(j + 1) * 1024], in_=PTv(NCH - 1)),
                        kPT(NCH - 1), (("PH", j),))
                for ec in range(KC):
                    g = ec // 2
                    b = next_bank()
                    for dc in range(2):
                        pe_mm(PSH(b), wv((g * 2 + dc) * 256 + (ec % 2) * 128, 128), PLv(2 * g + dc),
                              dc == 0, dc == 1, (wkey,) + kPL(2 * g + dc), kPSb(b))
                    tick(2 * (rng()[1] + 10) / 2400.0)
                    act(MXv(ec), PSH(b), AF.Identity, kPSb(b) + (("GV",),), kMX(ec),
                        scale=GV[:, GC_BSCALE + j * 8 + ec: GC_BSCALE + j * 8 + ec + 1])
            on(hc, inner)()

        pp = [Piece([(wrows(b_w_in[j], 0, 8, nh * 512, 512), 0)], p_work(nh)) for nh in range(2)]
        gp = Piece([(b_w_grp[j].rearrange("g (dc p) e -> p (g dc) e", p=128), 0)], pool_work)
        pns = {0: PostNorm(GC_MIXPOST + l * 8), 1: PostNorm(GC_MIXPOST + l * 8)}

        def out_work(pi):
            def work(wv, wkey, first=True, last=True):
                hc = cmap(cur["hc"])
                def inner():
                    for qp in range(2):
                        bs_ = [next_bank(), next_bank()]
                        groups_fm([(bs_[i], lambda k, q=qp * 2 + i: k * 512 + q * 128) for i in range(2)],
                                  wv, wkey, range(KC), 0, KC - 1, lambda k: MXv(k), lambda k: kMX(k))
                        for i in range(2):
                            pns[hc.h].evac(bs_[i], pi * 4 + qp * 2 + i)
                on(hc, inner)()
            return work
        tail = [Piece([(wrows(b_w_out[j], 0, 8, pi * 512, 512), 0)], out_work(pi)) for pi in range(2)]

        def make_chain(hc):
            hcc = cmap(hc)
            return on_build(hcc, lambda: pns[hcc.h].chain(None, next_pre_fn(hcc.h)))
        run_sublayer(hA, hB, [], [], pp + [gp] + tail, make_chain, reuse=False)

    xT3 = xT.rearrange("(fc p) t -> p fc t", p=128)
    yT3 = yT.rearrange("(fc p) t -> p fc t", p=128)
    xsem2 = [[nc.alloc_semaphore("sem_x%d_%d" % (i, h)) for h in range(2)] for i in range(KC)]
    ysem2 = [[nc.alloc_semaphore("sem_y%d_%d" % (i, h)) for h in range(2)] for i in range(KC)]

    def load_x(t, fc):
        hc = cur["hc"]
        col, n = (0, 512) if hc.h == 0 else (512, 384)
        dst = X[:, fc * T + col: fc * T + col + n]
        src = xT3[:, fc, t * T + col: t * T + col + n]
        S.add("sp", lambda e: e.dma_start(out=dst, in_=src), (), (("X", fc, hc.h),), dsem=xsem2[fc][hc.h])

    def store_out(t, fc):
        hc = cur["hc"]
        col, n = rng()
        if t == 0:
            lo = max(col, HALO * 128)
            if lo >= col + n:
                return
            src = MS[:, fc * T + lo: fc * T + col + n]
            dst = yT3[:, fc, lo - HALO * 128: col + n - HALO * 128]
        else:
            o0 = (NCH - HALO) * 128 + (t - 1) * T
            src = MS[:, fc * T + col: fc * T + col + n]
            dst = yT3[:, fc, o0 + col: o0 + col + n]
        S.add("sp", lambda e: e.dma_start(out=dst, in_=src), (kMS(fc),), (), dsem=ysem2[fc][hc.h])

    full = tuple(layers) == (0, 1, 2, 3)
    live0 = {0: 1, 1: 2, 2: 2, 3: 3} if full else {l: 0 for l in layers}

    def halves_for(t, l, norm=False):
        c0 = live0[l] if t == 0 else 0
        if norm and l % 2 == 1:
            c0 = max(c0 - 1, 0)
        return Half(0, c0 * 128, 512 - c0 * 128), Half(1, 512, 384)

    seq = [(t, l, kind) for t in range(ntiles) for l in layers for kind in ("mix", "ffn")]

    def pre_gcol(l, kind):
        return (GC_MIXPRE if kind == "mix" else GC_FFNPRE) + l * 8

    for h in range(2):
        for fc in range(KC):
            on_build(Half(h, 0 if h == 0 else 512, 512 if h == 0 else 384), lambda fc=fc: load_x(0, fc))
    t0, l0, k0 = seq[0]
    for hc in halves_for(t0, l0, norm=True):
        for th in on_build(hc, lambda: prenorm_thunks(pre_gcol(l0, k0))):
            on(hc, th)()

    for i, (t, l, kind) in enumerate(seq):
        nxt = seq[i + 1] if i + 1 < len(seq) else None

        def next_pre_fn(h, nxt=nxt):
            if nxt is None:
                return None
            tn, ln, kn = nxt
            hcs = halves_for(tn, ln, norm=(kn == "mix"))
            return (hcs[h], pre_gcol(ln, kn))
        final_tile = t if (kind == "ffn" and l == layers[-1]) else None
        if kind == "mix":
            if l % 2 == 0:
                hA, hB = halves_for(t, l)
                mixer_a(l, hA, hB, next_pre_fn)
            else:
                hA, hB = halves_for(t, l, norm=True)
                hAc, hBc = halves_for(t, l)
                mixer_b(l, hA, hB, hAc, hBc, HALO if t == 0 else -1, next_pre_fn)
        else:
            hA, hB = halves_for(t, l)
            ffn(l, hA, hB, final_tile, next_pre_fn)
    drain_all()
    S.emit(nc, engsem)
    return nc


def _host_consts():
    s = np.arange(128)[:, None]
    t = np.arange(128)[None, :]
    maskT = (s <= t).astype(np.float32)
    wins = (2, 4, 8, 16)
    pm = np.zeros((3, 128, 4, 128), np.float32)
    for g, w in enumerate(wins):
        band = ((s <= t) & (s > t - w)).astype(np.float32)
        pm[0, :, g, :] = band / w - np.eye(128, dtype=np.float32)
        bandp = ((s - 128) > (t - w)).astype(np.float32)
        pm[1, :, g, :] = bandp / w
        cnt = np.minimum(t + 1, w).astype(np.float32)
        pm[2, :, g, :] = band / cnt - np.eye(128, dtype=np.float32)
    return maskT, pm.reshape(3, 128, 512)


def _vec_cols(v):
    return np.ascontiguousarray(v.reshape(-1, 128).T)


_NC_CACHE = {}


def _get_nc(layers, ntiles):
    key = (tuple(layers), ntiles)
    if key not in _NC_CACHE:
        _NC_CACHE[key] = build_nc(layers, ntiles)
    return _NC_CACHE[key]


def _make_in_maps(x, a_w_in, a_ln_g, a_ln_b, a_w_s, a_b_s, a_w_out, b_w_in, b_w_grp, b_scale, b_w_out,
                  mix_pre_g, mix_post_g, ffn_pre_g, ffn_post_g, ffn_w_gate, ffn_w_up, ffn_w_down):
    f = np.float32
    B, Sq, _ = x.shape
    maskT, pm = _host_consts()
    gv = np.zeros((128, GC_N), f)
    for l in range(4):
        gv[:, GC_MIXPRE + l * 8: GC_MIXPRE + l * 8 + 8] = _vec_cols(np.asarray(mix_pre_g[l], f))
        gv[:, GC_MIXPOST + l * 8: GC_MIXPOST + l * 8 + 8] = _vec_cols(np.asarray(mix_post_g[l], f))
        gv[:, GC_FFNPRE + l * 8: GC_FFNPRE + l * 8 + 8] = _vec_cols(np.asarray(ffn_pre_g[l], f))
        gv[:, GC_FFNPOST + l * 8: GC_FFNPOST + l * 8 + 8] = _vec_cols(np.asarray(ffn_post_g[l], f))
    for j in range(2):
        gv[:, GC_BSCALE + j * 8: GC_BSCALE + j * 8 + 8] = _vec_cols(np.asarray(b_scale[j], f))
        gv[:, GC_LNB + j * 16: GC_LNB + j * 16 + 16] = _vec_cols(np.asarray(a_ln_b[j], f))
    wsT = np.ascontiguousarray(np.transpose(np.asarray(a_w_s, f), (0, 3, 1, 2))).reshape(2, 128, 1024)
    bs = np.ascontiguousarray(np.asarray(a_b_s, f)).reshape(2, 1024)
    shared = {
        "a_w_in": np.ascontiguousarray(a_w_in, f), "a_w_out": np.ascontiguousarray(a_w_out, f),
        "b_w_in": np.ascontiguousarray(b_w_in, f), "b_w_grp": np.ascontiguousarray(b_w_grp, f),
        "b_w_out": np.ascontiguousarray(b_w_out, f), "ffn_w_gate": np.ascontiguousarray(ffn_w_gate, f),
        "ffn_w_up": np.ascontiguousarray(ffn_w_up, f), "ffn_w_down": np.ascontiguousarray(ffn_w_down, f),
        "gvec": gv, "a_ln_g": np.ascontiguousarray(a_ln_g, f), "a_w_sT": wsT, "a_b_s": bs, "maskT": maskT,
    }
    pm_mid = pm.copy()
    pm_mid[2] = pm_mid[0]
    in_maps = []
    for core in range(NCORES):
        b, half = core // 2, core % 2
        start = half * OWN - HALO * 128
        xw = np.zeros((NTOK, D), f)
        lo = max(start, 0)
        xw[lo - start:, :] = x[b, lo:start + NTOK, :]
        m = dict(shared)
        m["xT"] = np.ascontiguousarray(xw.T)
        m["poolm"] = pm if half == 0 else pm_mid
        in_maps.append(m)
    return in_maps


def kernel(**inputs):
    inputs = {k: np.asarray(v) for k, v in inputs.items()}
    x = inputs["x"].astype(np.float32, copy=False)
    B, Sq, _ = x.shape
    in_maps = _make_in_maps(**inputs)
    nc = _get_nc((0, 1, 2, 3), NTILES)
    res = run_bass_kernel_spmd(nc, in_maps, core_ids=list(range(NCORES)))
    out = np.empty((B, Sq, D), np.float32)
    for core in range(NCORES):
        b, half = core // 2, core % 2
        out[b, half * OWN:(half + 1) * OWN, :] = res.results[core]["yT"].T
    return out
```

```python
import numpy as np
import concourse.bass as bass
import concourse.mybir as mybir
from concourse.bass_utils import run_bass_kernel_spmd

F32 = mybir.dt.float32
BF16 = mybir.dt.bfloat16
AF = mybir.ActivationFunctionType
ALU = mybir.AluOpType
AX = mybir.AxisListType

NCORES = 8
D = 1024
KC = 8
NCH = 7
T = NCH * 128
HALVES = ((0, 512), (512, 384))
NTILES = 5
HALO = 3
NCHUNK = NTILES * NCH
NTOK = NCHUNK * 128
OWN = 4096
DFF = 2816
NJ = DFF // 128
EPS = 1e-6
WSLOT = 4096
NWSLOT = 3
USE_LNEXP = True

GC_MIXPRE, GC_MIXPOST, GC_FFNPRE, GC_FFNPOST, GC_BSCALE, GC_LNB, GC_N = 0, 32, 64, 96, 128, 144, 176


class Op:
    __slots__ = ("eng", "fn", "deps", "sig", "sigidx", "dsem", "dval", "pos", "gidx", "clk", "ckey")

    def __init__(self, eng, fn, dsem=None, dval=0):
        self.eng = eng
        self.fn = fn
        self.deps = []
        self.sig = False
        self.sigidx = 0
        self.dsem = dsem
        self.dval = dval


class Sched:
    ENGS = ("pe", "act", "dve", "pool", "sp")

    def __init__(self):
        self.ops = {e: [] for e in self.ENGS}
        self.res = {}
        self.dma_count = {}
        self.eclk = {e: {} for e in self.ENGS}
        self.gcount = 0

    def add(self, eng, fn, reads=(), writes=(), dsem=None):
        if dsem is not None:
            self.dma_count[dsem] = self.dma_count.get(dsem, 0) + 16
            op = Op(eng, fn, dsem, self.dma_count[dsem])
            op.ckey = ("dma", id(dsem))
            op.pos = op.dval
        else:
            op = Op(eng, fn)
            op.ckey = eng
            op.pos = len(self.ops[eng]) + 1
        self.gcount += 1
        op.gidx = self.gcount
        deps = {}
        res = self.res
        for k in reads:
            rec = res.get(k)
            if rec is not None and rec[0] is not None:
                deps[id(rec[0])] = (rec[0], "RAW")
        for k in writes:
            rec = res.get(k)
            if rec is not None:
                if rec[0] is not None and id(rec[0]) not in deps:
                    deps[id(rec[0])] = (rec[0], "WAW")
                for r in rec[1]:
                    if id(r) not in deps:
                        deps[id(r)] = (r, "WAR")
        cand = []
        for d, kind in deps.values():
            if d is op:
                continue
            if d.dsem is None and op.dsem is None and d.eng == eng:
                if eng == "pe":
                    continue
            cand.append(d)
        ek = self.eclk[eng]
        cand.sort(key=lambda d: -d.gidx)
        for d in cand:
            if ek.get(d.ckey, 0) >= d.pos:
                continue
            op.deps.append(d)
            for k, v in d.clk.items():
                if ek.get(k, 0) < v:
                    ek[k] = v
        clk = dict(ek)
        if op.dsem is None:
            clk[op.ckey] = op.pos
        else:
            clk[op.ckey] = op.pos
        op.clk = clk
        for k in reads:
            rec = res.get(k)
            if rec is None:
                res[k] = [None, [op]]
            else:
                rec[1].append(op)
        for k in writes:
            res[k] = [op, []]
        self.ops[eng].append(op)
        return op

    def emit(self, nc, engsem):
        for e in self.ENGS:
            for op in self.ops[e]:
                for d in op.deps:
                    if d.dsem is None:
                        d.sig = True
        for e in self.ENGS:
            n = 0
            for op in self.ops[e]:
                if op.dsem is None and op.sig:
                    n += 1
                    op.sigidx = n
        ops = self.ops

        def run(eng_name, eng):
            seen = {}
            for op in ops[eng_name]:
                need = {}
                for d in op.deps:
                    if d.dsem is not None:
                        sem, val = d.dsem, d.dval
                    else:
                        sem, val = engsem[d.eng], d.sigidx
                    k = id(sem)
                    if k not in need or need[k][1] < val:
                        need[k] = (sem, val)
                for k, (sem, val) in need.items():
                    if seen.get(k, 0) < val:
                        eng.wait_ge(sem, val)
                        seen[k] = val
                inst = op.fn(eng)
                if op.dsem is not None:
                    inst.then_inc(op.dsem, 16)
                elif op.sig:
                    inst.then_inc(engsem[eng_name], 1)

        with nc.Block() as block:
            @block.tensor
            def _(e):
                run("pe", e)

            @block.scalar
            def _(e):
                run("act", e)

            @block.vector
            def _(e):
                run("dve", e)

            @block.gpsimd
            def _(e):
                run("pool", e)

            @block.sync
            def _(e):
                run("sp", e)
                for sem, cnt in self.dma_count.items():
                    e.wait_ge(sem, cnt)


def build_nc(layers=(0, 1, 2, 3), ntiles=NTILES):
    nc = bass.Bass("TRN2", target_bir_lowering=False)
    S = Sched()

    def dram(name, shape, dt=F32, kind="ExternalInput"):
        return nc.dram_tensor(name, list(shape), dt, kind=kind).ap()

    xT = dram("xT", [D, NTOK])
    yT = dram("yT", [D, OWN], kind="ExternalOutput")
    a_w_in = dram("a_w_in", [2, D, 4096])
    a_w_out = dram("a_w_out", [2, 2048, D])
    b_w_in = dram("b_w_in", [2, D, D])
    b_w_grp = dram("b_w_grp", [2, 4, 256, 256])
    b_w_out = dram("b_w_out", [2, D, D])
    w_gate = dram("ffn_w_gate", [4, D, DFF])
    w_up = dram("ffn_w_up", [4, D, DFF])
    w_down = dram("ffn_w_down", [4, DFF, D])
    gvec_d = dram("gvec", [128, GC_N])
    lng_d = dram("a_ln_g", [2, 2048])
    wsT_d = dram("a_w_sT", [2, 128, 1024])
    bs_d = dram("a_b_s", [2, 1024])
    maskT_d = dram("maskT", [128, 128])
    pm_d = dram("poolm", [3, 128, 512])

    sb = nc.alloc_sbuf_tensor
    X = sb("X", [128, KC * T], F32)
    H = sb("H", [128, KC * T], BF16)
    MS = sb("MS", [128, KC * T], F32)
    SQ = sb("SQ", [128, KC * T], BF16)
    BIG = sb("BIG", [128, 28672], BF16)
    WR = sb("WR", [128, NWSLOT * WSLOT], BF16)
    RS = [sb("RS0", [128, T], F32)]
    SG = [sb("SG0", [128, T], BF16)]
    GV = sb("GV", [128, GC_N], F32)
    GBC = sb("GBC", [128, 2 * 2048], BF16)
    WSM = sb("WSM", [128, 2 * 1024], BF16)
    BIAS = sb("BIAS", [128, 2 * 2048], F32)
    PM = sb("PM", [128, 3 * 512], BF16)
    PH = sb("PH", [128, 2 * 1024], BF16)
    ONESM = sb("ONESM", [128, 128], BF16)
    ONES1 = sb("ONES1", [128, 128], BF16)
    EPSC = sb("EPSC", [128, 1], F32)
    ST = sb("STATS", [128, 64], F32)
    ps = nc.alloc_psum_tensor("ps", [128, 4096], F32)

    engsem = {e: nc.alloc_semaphore("sem_" + e) for e in ("pe", "act", "dve")}
    wsem = [nc.alloc_semaphore("sem_w%d" % i) for i in range(NWSLOT)]
    xsem = nc.alloc_semaphore("sem_x")
    ysem = nc.alloc_semaphore("sem_y")
    _cs = [0]

    def csem_new():
        _cs[0] += 1
        return nc.alloc_semaphore("sem_c%d" % _cs[0])

    class Half:
        def __init__(self, h, col, n):
            self.h, self.col, self.n = h, col, n
            self.c_lo, self.c_hi = col // 128, (col + n) // 128

    cur = {"hc": None}

    def rng():
        hc = cur["hc"]
        return hc.col, hc.n

    def on(hc, fn):
        def run():
            old = cur["hc"]
            cur["hc"] = hc
            try:
                fn()
            finally:
                cur["hc"] = old
        return run

    def fm(buf, fc):
        c, n = rng()
        return buf[:, fc * T + c: fc * T + c + n]

    def Xv(fc): return fm(X, fc)
    def Hv(fc): return fm(H, fc)
    def MSv(fc): return fm(MS, fc)
    def SQv(fc): return fm(SQ, fc)
    def kX(fc): return ("X", fc, cur["hc"].h)
    def kH(fc): return ("H", fc, cur["hc"].h)
    def kMS(fc): return ("MS", fc, cur["hc"].h)

    def blocks(name, lo, n):
        return tuple((name, b) for b in range(lo // 128, (lo + n + 127) // 128))

    def kSQ(fc):
        c, n = rng()
        return blocks("SQ", fc * T + c, n)

    def bigfm(off, fc):
        c, n = rng()
        return BIG[:, off + fc * T + c: off + fc * T + c + n], blocks("BIG", off + fc * T + c, n)

    VOFF = 16 * T
    PLOFF = 7 * 1024
    MXOFF = PLOFF + 8 * T
    def Uv(fc): return bigfm(0, fc)[0]
    def kU(fc): return bigfm(0, fc)[1]
    def Gv(j): return bigfm(0, j)[0]
    def kG(j): return bigfm(0, j)[1]
    def PLv(fc): return bigfm(PLOFF, fc)[0]
    def kPL(fc): return bigfm(PLOFF, fc)[1]
    def MXv(fc): return bigfm(MXOFF, fc)[0]
    def kMX(fc): return bigfm(MXOFF, fc)[1]
    def Vv(c, c0=0, n=2048): return BIG[:, VOFF + c * 2048 + c0: VOFF + c * 2048 + c0 + n]
    def kV(c, c0=0, n=2048): return blocks("BIG", VOFF + c * 2048 + c0, n)
    def PTv(c, c0=0, n=1024): return BIG[:, c * 1024 + c0: c * 1024 + c0 + n]
    def kPT(c, c0=0, n=1024): return blocks("BIG", c * 1024 + c0, n)

    def RSv():
        c, n = rng()
        return RS[0][:, c:c + n]
    def kRS(): return ("RS", cur["hc"].h)
    def SGv():
        c, n = rng()
        return SG[0][:, c:c + n]
    def kSG(): return ("SG", cur["hc"].h)

    def PSH(bank, c0=0, n=None):
        if n is None:
            n = rng()[1]
        return ps[:, bank * 512 + c0: bank * 512 + c0 + n]
    def kPSb(bank): return (("ps", bank),)
    def STATb(): return 6 + cur["hc"].h

    state = {"bank": 0, "w": 0}

    def next_bank():
        b = state["bank"]
        state["bank"] = (b + 1) % 6
        return b

    def pe_mm(out, lhsT, rhs, start, stop, reads, writes):
        S.add("pe", lambda e: e.matmul(out, lhsT=lhsT, rhs=rhs, start=start, stop=stop), reads, writes)

    def act(out, in_, func, reads, writes, scale=None, bias=None, accum=None):
        kw = {}
        if scale is not None:
            kw["scale"] = scale
        if bias is not None:
            kw["bias"] = bias
        if accum is not None:
            kw["accum_out"] = accum
        S.add("act", lambda e: e.activation(out=out, in_=in_, func=func, **kw), reads, writes)

    def dve(fn, reads, writes):
        S.add("dve", fn, reads, writes)

    def wload(parts):
        w = state["w"]
        state["w"] = (w + 1) % NWSLOT
        state["last_slot"] = w
        key = tuple(("W", w, i) for i in range(2))
        for i, (src, off) in enumerate(parts):
            k, n = src.shape[1], src.shape[2]
            dst = WR[:, w * WSLOT + off: w * WSLOT + off + k * n].rearrange("p (k n) -> p k n", k=k)
            wk = (key[i],) if len(parts) > 1 else key
            S.add("pool", lambda e, dst=dst, src=src: e.dma_start(out=dst, in_=src), (), wk, dsem=wsem[w])
        base = w * WSLOT
        return (lambda off, n: WR[:, base + off: base + off + n]), key

    def wrows(w2d, r0, nk, c0, n):
        return w2d[r0 * 128:(r0 + nk) * 128, c0:c0 + n].rearrange("(k p) n -> p k n", p=128)

    S.add("sp", lambda e: e.dma_start(out=GV[:], in_=gvec_d), (), (("GV",),), dsem=csem_new())
    S.add("pool", lambda e: e.dma_start(out=PM[:].rearrange("p (a n) -> p a n", a=3),
                                        in_=pm_d.rearrange("a p n -> p a n")), (), (("PM",),), dsem=csem_new())
    S.add("pool", lambda e: e.dma_start(out=GBC[:].rearrange("p (a n) -> p a n", a=2),
                                        in_=lng_d.partition_broadcast(128)), (), (("GBC",),), dsem=csem_new())
    dve(lambda e: e.memset(ONESM[:], 1.0 / 1024.0), (), (("ONESM",),))
    dve(lambda e: e.memset(ONES1[:], 1.0), (), (("ONES1",),))
    dve(lambda e: e.memset(EPSC[:], EPS), (), (("EPSC",),))
    dve(lambda e: e.memset(PH[:], 0.0), (), (("PH", 0), ("PH", 1)))
    dve(lambda e: e.memset(ST[:], 0.0), (), (("ST", 0), ("ST", 1)))

    a_layers = sorted({l // 2 for l in layers if l % 2 == 0})
    if a_layers:
        kscr = tuple(("MS", f, h) for f in range(3) for h in range(2))
        kscr2 = tuple(("MS", f, h) for f in (3, 4) for h in range(2))
        S.add("sp", lambda e: e.dma_start(out=MS[:, 2048:2176], in_=maskT_d), (), kscr, dsem=csem_new())
        for j in a_layers:
            S.add("sp", lambda e, j=j: e.dma_start(out=MS[:, 0:1024], in_=wsT_d[j]), (), kscr, dsem=csem_new())
            S.add("sp", lambda e, j=j: e.dma_start(out=MS[:, 2688:3712], in_=bs_d[j].partition_broadcast(128)),
                  (), kscr2, dsem=csem_new())
            dve(lambda e, j=j: e.tensor_tensor(
                out=WSM[:, j * 1024:(j + 1) * 1024].rearrange("p (g t) -> p g t", g=8),
                in0=MS[:, 0:1024].rearrange("p (g t) -> p g t", g=8),
                in1=MS[:, 2048:2176].unsqueeze(1).broadcast_to([128, 8, 128]), op=ALU.mult),
                kscr, (("WSM", j),))
            for g in range(8):
                pe_mm(ps[:, g * 128:(g + 1) * 128], ONES1[:], WSM[:, j * 1024 + g * 128: j * 1024 + (g + 1) * 128],
                      True, True, (("ONES1",), ("WSM", j)), (("ps", 0), ("ps", 1)))
            for fc in range(16):
                g = fc // 2
                dve(lambda e, j=j, fc=fc, g=g: e.scalar_tensor_tensor(
                    out=BIAS[:, j * 2048 + fc * 128: j * 2048 + (fc + 1) * 128],
                    in0=ps[:, g * 128:(g + 1) * 128], scalar=GV[:, GC_LNB + j * 16 + fc: GC_LNB + j * 16 + fc + 1],
                    in1=MS[:, 2688 + g * 128: 2688 + (g + 1) * 128], op0=ALU.mult, op1=ALU.add),
                    (("ps", 0), ("ps", 1), ("GV",)) + kscr2, (("BIAS", j),))

    def stats_mm(fc, first, last):
        sb_ = STATb()
        pe_mm(PSH(sb_), ONESM[:], SQv(fc), first, last, kSQ(fc) + (("ONESM",),), kPSb(sb_))

    def rstd_ops():
        sb_ = STATb()
        rv = RSv()
        if USE_LNEXP:
            return [
                lambda: act(rv, PSH(sb_, 0, rv.shape[1]), AF.Ln, kPSb(sb_) + (("EPSC",),), (kRS(),), bias=EPSC[:]),
                lambda: act(rv, rv, AF.Exp, (kRS(),), (kRS(),), scale=-0.5),
            ]
        return [
            lambda: act(rv, PSH(sb_, 0, rv.shape[1]), AF.Sqrt, kPSb(sb_) + (("EPSC",),), (kRS(),), bias=EPSC[:]),
            lambda: dve(lambda e: e.reciprocal(out=rv, in_=rv), (kRS(),), (kRS(),)),
        ]

    def prenorm_thunks(gcol):
        th = []

        def sq(fc):
            act(SQv(fc), Xv(fc), AF.Square, (kX(fc),), kSQ(fc))
            stats_mm(fc, fc == 0, fc == KC - 1)

        def hh(fc):
            o, a, g, rr = Hv(fc), Xv(fc), GV[:, gcol + fc: gcol + fc + 1], RSv()
            dve(lambda e: e.scalar_tensor_tensor(out=o, in0=a, scalar=g, in1=rr, op0=ALU.mult, op1=ALU.mult),
                (kX(fc), ("GV",), kRS()), (kH(fc),))
        for fc in range(KC):
            th.append(lambda fc=fc: sq(fc))
        th.extend(rstd_ops())
        for fc in range(KC):
            th.append(lambda fc=fc: hh(fc))
        return th

    class PostNorm:
        def __init__(self, gcol):
            self.gcol = gcol
            self.pending = None
            self.nstat = 0

        def evac(self, bank, oc):
            gcol = self.gcol
            act(SQv(oc), PSH(bank), AF.Square, kPSb(bank), kSQ(oc))
            act(MSv(oc), PSH(bank), AF.Identity, kPSb(bank) + (("GV",),), (kMS(oc),),
                scale=GV[:, gcol + oc: gcol + oc + 1])
            if self.pending is not None:
                stats_mm(self.pending, self.nstat == 0, False)
                self.nstat += 1
            self.pending = oc

        def chain(self, final_tile, next_pre):
            hc = cur["hc"]
            th = [lambda: stats_mm(self.pending, self.nstat == 0, True)]
            th.extend(rstd_ops())
            ft = final_tile
            sqs = []
            if next_pre is not None:
                hcn, gcoln = next_pre
                pre = [on(hcn, t) for t in on_build(hcn, lambda: prenorm_thunks(gcoln))]
            else:
                pre = []

            def mult(fc):
                o, rr = MSv(fc), RSv()
                dve(lambda e: e.tensor_tensor(out=o, in0=o, in1=rr, op=ALU.mult), (kMS(fc), kRS()), (kMS(fc),))

            def add(fc):
                m, x = MSv(fc), Xv(fc)
                if ft is None:
                    dve(lambda e: e.tensor_tensor(out=x, in0=m, in1=x, op=ALU.add), (kMS(fc), kX(fc)), (kX(fc),))
                else:
                    dve(lambda e: e.tensor_tensor(out=m, in0=m, in1=x, op=ALU.add), (kMS(fc), kX(fc)), (kMS(fc),))
                    store_out(ft, fc)
                    if ft + 1 < ntiles:
                        load_x(ft + 1, fc)
            n = hc.n
            d = (n + 60) / 960.0 * 1e-3 * 1e3
            a = (n + 240) / 1200.0
            r1 = 1.0 + 1.3 + 2 * a + 0.5 + (0.0 if USE_LNEXP else 2.5)
            times = [0.8, 1.0, 1.0]
            th.append(lambda: mult(0))
            times.append(r1)
            i = 1
            for fc in range(1, KC):
                th.append(lambda fc=fc: mult(fc))
                times.append(r1 + i * d)
                i += 1
                th.append(lambda fc=fc: add(fc - 1))
                times.append(r1 + i * d)
                i += 1
                if fc - 1 < len(pre) and fc - 1 < KC:
                    th.append(pre[fc - 1])
                    times.append(r1 + i * d)
            th.append(lambda: add(KC - 1))
            times.append(r1 + i * d)
            i += 1
            tl = r1 + i * d
            rest = pre[KC - 1:]
            if rest:
                r2 = tl + a + 0.3 + 1.3 + 2 * a + 0.5 + (0.0 if USE_LNEXP else 2.5)
                rt = [tl, tl + a + 0.3, tl + a + 0.3] + [r2 + k * d for k in range(KC)]
                th.extend(rest)
                times.extend(rt[:len(rest)])
            assert len(times) == len(th), (len(times), len(th))
            return [(tm, on(hc, t)) for tm, t in zip(times, th)]

    def on_build(hc, fn):
        old = cur["hc"]
        cur["hc"] = hc
        try:
            return fn()
        finally:
            cur["hc"] = old

    def groups_fm(banks_woffs, wv, wkey, ks, kfirst, klast, rhs_fn, rhs_keys_fn, kloc=None):
        for k in ks:
            kk = k if kloc is None else kloc(k)
            for bank, woff_fn in banks_woffs:
                pe_mm(PSH(bank), wv(woff_fn(kk), 128), rhs_fn(k), k == kfirst, k == klast,
                      wkey + rhs_keys_fn(k), kPSb(bank))
            tick(len(banks_woffs) * (rng()[1] + 10) / 2400.0)

    slot_owner = {}

    class Piece:
        def __init__(self, parts, work, group=None, ipart=0, npart=1):
            self.parts, self.work = parts, work
            self.w = None
            self.slot = None
            self.group, self.ipart, self.npart = group, ipart, npart

        def resident(self):
            return self.parts is not None and self.slot is not None and slot_owner.get(self.slot) is self

        def load(self):
            if self.parts is None:
                self.w = (None, None)
            elif not self.resident():
                self.w = wload(self.parts)
                self.slot = state["last_slot"]
                slot_owner[self.slot] = self

        def run(self, hc, first=True, last=True):
            on(hc, lambda: self.work(self.w[0], self.w[1], first, last))()

    def natural_order(tail):
        return [(p, p.ipart == 0, p.ipart == p.npart - 1) for p in tail]

    def reuse_order(tail):
        groups = []
        for p in tail:
            if p.group not in groups:
                groups.append(p.group)
        parts = {g: [p for p in tail if p.group == g] for g in groups}
        res = {id(p) for p in tail if p.resident()}
        comp = [g for g in groups if all(id(p) in res for p in parts[g])]
        part = [g for g in groups if g not in comp and any(id(p) in res for p in parts[g])]
        rest = [g for g in groups if g not in comp and g not in part]
        order = []
        for g in comp + part + rest:
            ps = sorted(parts[g], key=lambda p: (0 if id(p) in res else 1, p.ipart))
            for i, p in enumerate(ps):
                order.append((p, i == 0, i == len(ps) - 1))
        return order

    bg = []
    clock = {"t": 0.0}

    def tick(dt):
        clock["t"] += dt
        while bg and bg[0][0] <= clock["t"]:
            bg.pop(0)[1]()

    def drain_all():
        while bg:
            bg.pop(0)[1]()

    def add_chain(items):
        t0 = clock["t"]
        for tm, th in items:
            bg.append((t0 + tm, th))

    def run_sublayer(hA, hB, head, mid, tail, make_chain, reuse=False, _u=None):
        for p in head:
            p.load()
        for p in head:
            p.run(hA)
        if head or mid:
            drain_all()
        for p in head:
            p.run(hB)
        for p in mid:
            p.load()
            p.run(hA)
            p.run(hB)
        for p, first, last in natural_order(tail):
            p.load()
            p.run(hA, first, last)
        drain_all()
        add_chain(make_chain(hA))
        for p, first, last in (reuse_order(tail) if reuse else natural_order(tail)):
            p.load()
            p.run(hB, first, last)
        drain_all()
        add_chain(make_chain(hB))

    def ffn(l, hA, hB, final_tile, next_pre_fn):
        sgate = {}

        def gu_work(pi):
            def work(wv, wkey, first=True, last=True):
                for jj in range(2):
                    j = pi * 2 + jj
                    bg_ = next_bank()
                    bu_ = next_bank()
                    groups_fm([(bg_, lambda k: k * 256 + jj * 128), (bu_, lambda k: 2048 + k * 256 + jj * 128)],
                              wv, wkey, range(KC), 0, KC - 1, lambda k: Hv(k), lambda k: (kH(k),))
                    sgv = SGv()
                    act(sgv, PSH(bg_), AF.Silu, kPSb(bg_), (kSG(),))
                    o, a = Gv(j), PSH(bu_)
                    dve(lambda e, o=o, a=a, sgv=sgv: e.tensor_tensor(out=o, in0=a, in1=sgv, op=ALU.mult),
                        kPSb(bu_) + (kSG(),), kG(j))
            return work
        pieces = [Piece([(wrows(w_gate[l], 0, 8, pi * 256, 256), 0), (wrows(w_up[l], 0, 8, pi * 256, 256), 2048)],
                        gu_work(pi)) for pi in range(NJ // 2)]
        pns = {0: PostNorm(GC_FFNPOST + l * 8), 1: PostNorm(GC_FFNPOST + l * 8)}

        dbanks = {}

        def down_work(pair, kh):
            def work(wv, wkey, first=True, last=True):
                h = cur["hc"].h
                if first:
                    dbanks[(h, pair)] = (next_bank(), next_bank())
                b0, b1 = dbanks[(h, pair)]
                ks = range(kh * 11, kh * 11 + 11)
                groups_fm([(b0, lambda kk: kk * 256), (b1, lambda kk: kk * 256 + 128)], wv, wkey,
                          ks, ks[0] if first else None, ks[-1] if last else None,
                          lambda k: Gv(k), lambda k: kG(k), kloc=lambda k: k - kh * 11)
                if last:
                    pns[h].evac(b0, pair * 2)
                    pns[h].evac(b1, pair * 2 + 1)
            return work
        tail = [Piece([(wrows(w_down[l], kh * 11, 11, pair * 256, 256), 0)], down_work(pair, kh),
                      group=pair, ipart=kh, npart=2)
                for pair in range(KC // 2) for kh in range(2)]

        def make_chain(hc):
            return on_build(hc, lambda: pns[hc.h].chain(final_tile, next_pre_fn(hc.h)))
        run_sublayer(hA, hB, pieces[:3], pieces[3:], tail, make_chain, reuse=True)

    def mixer_a(l, hA, hB, next_pre_fn):
        j = l // 2

        def v_work(vb):
            def work(wv, wkey, first=True, last=True):
                hc = cur["hc"]
                cs_all = list(range(hc.c_lo, hc.c_hi))
                for i0 in range(0, len(cs_all), 2):
                    cs = cs_all[i0:i0 + 2]
                    bks = [next_bank() for _ in cs]
                    for k in range(KC):
                        for c, b in zip(cs, bks):
                            pe_mm(ps[:, b * 512:(b + 1) * 512], H[:, k * T + c * 128: k * T + (c + 1) * 128],
                                  wv(k * 512, 512), k == 0, k == KC - 1, (kH(k),) + wkey, kPSb(b))
                        tick(len(cs) * 522 / 2400.0)
                    for c, b in zip(cs, bks):
                        act(Vv(c, vb * 512, 512), ps[:, b * 512:(b + 1) * 512], AF.Gelu, kPSb(b),
                            kV(c, vb * 512, 512) + (("ST", hc.h),), accum=ST[:, c * 4 + vb: c * 4 + vb + 1])
            return work

        def ln_square(idx):
            hc = cur["hc"]
            c = hc.c_lo + idx
            if c >= hc.c_hi:
                return
            kst = (("ST", hc.h),)
            off = (0 if hc.h == 0 else 4096) + (idx % 2) * 1024
            junk = SQ[:, off:off + 2048]
            act(junk, Vv(c), AF.Square, kV(c), blocks("SQ", off, 2048) + kst, accum=ST[:, 28 + c: 29 + c])

        def ln_stats():
            hc = cur["hc"]
            lo, hi = hc.c_lo, hc.c_hi
            kst = (("ST", hc.h),)
            junk = SQ[:, 0:2048] if hc.h == 0 else SQ[:, 4096:6144]
            kjunk = blocks("SQ", 0 if hc.h == 0 else 4096, 2048)
            dve(lambda e: e.tensor_reduce(out=ST[:, 35 + lo:35 + hi],
                                          in_=ST[:, 4 * lo:4 * hi].rearrange("p (c v) -> p c v", v=4),
                                          axis=AX.X, op=ALU.add), kst, kst)
            dve(lambda e: e.tensor_scalar(out=ST[:, 35 + lo:35 + hi], in0=ST[:, 35 + lo:35 + hi],
                                          scalar1=1.0 / 2048.0, scalar2=None, op0=ALU.mult), kst, kst)
            dve(lambda e: e.tensor_tensor(out=ST[:, 42 + lo:42 + hi], in0=ST[:, 35 + lo:35 + hi],
                                          in1=ST[:, 35 + lo:35 + hi], op=ALU.mult), kst, kst)
            dve(lambda e: e.scalar_tensor_tensor(out=ST[:, 42 + lo:42 + hi], in0=ST[:, 28 + lo:28 + hi],
                                                 scalar=1.0 / 2048.0, in1=ST[:, 42 + lo:42 + hi],
                                                 op0=ALU.mult, op1=ALU.subtract), kst, kst)
            dve(lambda e: e.tensor_scalar(out=ST[:, 42 + lo:42 + hi], in0=ST[:, 42 + lo:42 + hi], scalar1=0.0,
                                          scalar2=None, op0=ALU.max), kst, kst)
            if USE_LNEXP:
                act(ST[:, 49 + lo:49 + hi], ST[:, 42 + lo:42 + hi], AF.Ln, kst + (("EPSC",),), kst, bias=EPSC[:])
                act(ST[:, 49 + lo:49 + hi], ST[:, 49 + lo:49 + hi], AF.Exp, kst, kst, scale=-0.5)
            else:
                act(ST[:, 49 + lo:49 + hi], ST[:, 42 + lo:42 + hi], AF.Sqrt, kst + (("EPSC",),), kst, bias=EPSC[:])
                dve(lambda e: e.reciprocal(out=ST[:, 49 + lo:49 + hi], in_=ST[:, 49 + lo:49 + hi]), kst, kst)

        def ln_apply():
            hc = cur["hc"]
            kst = (("ST", hc.h),)
            for c in range(hc.c_lo, hc.c_hi):
                dve(lambda e, c=c: e.scalar_tensor_tensor(
                    out=Vv(c), in0=Vv(c), scalar=ST[:, 35 + c: 36 + c],
                    in1=GBC[:, j * 2048:(j + 1) * 2048], op0=ALU.subtract, op1=ALU.mult),
                    kV(c) + kst + (("GBC",),), kV(c))
                dve(lambda e, c=c: e.tensor_scalar(
                    out=SQ[:, c * 1024:(c + 1) * 1024], in0=WSM[:, j * 1024:(j + 1) * 1024],
                    scalar1=ST[:, 49 + c: 50 + c], scalar2=None, op0=ALU.mult),
                    (("WSM", j),) + kst, blocks("SQ", c * 1024, 1024))

        def u_work(ub):
            def work(wv, wkey, first=True, last=True):
                for qp in range(2):
                    bs_ = [next_bank(), next_bank()]
                    groups_fm([(bs_[i], lambda k, q=qp * 2 + i: k * 512 + q * 128) for i in range(2)],
                              wv, wkey, range(KC), 0, KC - 1, lambda k: Hv(k), lambda k: (kH(k),))
                    for i in range(2):
                        oc = ub * 4 + qp * 2 + i
                        act(Uv(oc), PSH(bs_[i]), AF.Gelu, kPSb(bs_[i]), kU(oc))
                    if ub < 2:
                        ln_square(ub * 2 + qp)
                if ub == 1:
                    ln_stats()
                if ub == 2:
                    ln_apply()
            return work

        def sp_work(fcs):
            def work(wv, wkey, first=True, last=True):
                hc = cur["hc"]
                lo, hi = hc.c_lo, hc.c_hi
                nl = hi - lo
                prev = None

                def gate_mul(pfc, ptmp):
                    u, m = Uv(pfc), MSv(ptmp)
                    dve(lambda e: e.tensor_tensor(out=u, in0=m, in1=u, op=ALU.mult), (kMS(ptmp),) + kU(pfc), kU(pfc))
                for fc in fcs:
                    g = fc // 2
                    b = next_bank()
                    for c in range(lo, hi):
                        pe_mm(PSH(b, (c - lo) * 128, 128), Vv(c, fc * 128, 128),
                              SQ[:, c * 1024 + g * 128: c * 1024 + (g + 1) * 128], True, True,
                              kV(c, fc * 128, 128) + blocks("SQ", c * 1024 + g * 128, 128), kPSb(b))
                    tick(nl * 0.1)
                    tmp = fc % 8
                    o = MSv(tmp).rearrange("p (c t) -> p c t", c=nl)
                    a = PSH(b).rearrange("p (c t) -> p c t", c=nl)
                    bb = BIAS[:, j * 2048 + fc * 128: j * 2048 + (fc + 1) * 128].unsqueeze(1).broadcast_to([128, nl, 128])
                    dve(lambda e, o=o, a=a, bb=bb: e.tensor_tensor(out=o, in0=a, in1=bb, op=ALU.add),
                        kPSb(b) + (("BIAS", j),), (kMS(tmp),))
                    if prev is not None:
                        gate_mul(*prev)
                    prev = (fc, tmp)
                gate_mul(*prev)
            return work

        def zero_st():
            dve(lambda e: e.memset(ST[:, 0:35], 0.0), (), (("ST", 0), ("ST", 1)))
        zero_st()
        vp = [Piece([(wrows(a_w_in[j], 0, 8, 2048 + vb * 512, 512), 0)], v_work(vb)) for vb in range(4)]
        up = [Piece([(wrows(a_w_in[j], 0, 8, ub * 512, 512), 0)], u_work(ub)) for ub in range(4)]
        spp = [Piece(None, sp_work(range(q * 4, q * 4 + 4))) for q in range(4)]
        pns = {0: PostNorm(GC_MIXPOST + l * 8), 1: PostNorm(GC_MIXPOST + l * 8)}

        def out_work(pi):
            def work(wv, wkey, first=True, last=True):
                bs_ = [next_bank(), next_bank()]
                groups_fm([(bs_[q], lambda k, q=q: k * 256 + q * 128) for q in range(2)],
                          wv, wkey, range(16), 0, 15, lambda k: Uv(k), lambda k: kU(k))
                for q in range(2):
                    pns[cur["hc"].h].evac(bs_[q], pi * 2 + q)
            return work
        tail = [Piece([(wrows(a_w_out[j], 0, 16, pi * 256, 256), 0)], out_work(pi), group=pi) for pi in range(4)]

        def make_chain(hc):
            return on_build(hc, lambda: pns[hc.h].chain(None, next_pre_fn(hc.h)))
        run_sublayer(hA, hB, vp[:3], vp[3:] + up + spp, tail, make_chain, reuse=True)

    def mixer_b(l, hA, hB, hAc, hBc, first_chunk, next_pre_fn):
        j = l // 2

        def p_work(nh):
            def work(wv, wkey, first=True, last=True):
                hc = cur["hc"]
                cs_all = list(range(hc.c_lo, hc.c_hi))
                for i0 in range(0, len(cs_all), 2):
                    cs = cs_all[i0:i0 + 2]
                    bks = [next_bank() for _ in cs]
                    for k in range(KC):
                        for c, b in zip(cs, bks):
                            pe_mm(ps[:, b * 512:(b + 1) * 512], H[:, k * T + c * 128: k * T + (c + 1) * 128],
                                  wv(k * 512, 512), k == 0, k == KC - 1, (kH(k),) + wkey, kPSb(b))
                        tick(len(cs) * 522 / 2400.0)
                    for c, b in zip(cs, bks):
                        act(PTv(c, nh * 512, 512), ps[:, b * 512:(b + 1) * 512], AF.Copy, kPSb(b),
                            kPT(c, nh * 512, 512))
            return work

        def cmap(hc):
            return hAc if hc.h == 0 else hBc

        def pool_work(wv, wkey, first=True, last=True):
            hc = cmap(cur["hc"])
            def inner():
                lo, hi = hc.c_lo, hc.c_hi
                for fc in range(KC):
                    g = fc // 2
                    b = next_bank()
                    for c in range(lo, hi):
                        pm = 2 if c == first_chunk else 0
                        if c == 0:
                            prev_ap, prev_key = PH[:, j * 1024 + fc * 128: j * 1024 + (fc + 1) * 128], (("PH", j),)
                        else:
                            prev_ap, prev_key = PTv(c - 1, fc * 128, 128), kPT(c - 1, fc * 128, 128)
                        pe_mm(PSH(b, (c - lo) * 128, 128), PTv(c, fc * 128, 128),
                              PM[:, pm * 512 + g * 128: pm * 512 + (g + 1) * 128], True, False,
                              kPT(c, fc * 128, 128) + (("PM",),), kPSb(b))
                        pe_mm(PSH(b, (c - lo) * 128, 128), prev_ap,
                              PM[:, 512 + g * 128: 512 + (g + 1) * 128], False, True,
                              prev_key + (("PM",),), kPSb(b))
                    tick((hi - lo) * 0.2)
                    act(PLv(fc), PSH(b), AF.Copy, kPSb(b), kPL(fc))
                if hc.h == 1:
                    dve(lambda e: e.tensor_copy(out=PH[:, j * 1024:(j + 1) * 1024], in_=PTv(NCH - 1)),
                        kPT(NCH - 1), (("PH", j),))
                for ec in range(KC):
                    g = ec // 2
                    b = next_bank()
                    for dc in range(2):
                        pe_mm(PSH(b), wv((g * 2 + dc) * 256 + (ec % 2) * 128, 128), PLv(2 * g + dc),
                              dc == 0, dc == 1, wkey + kPL(2 * g + dc), kPSb(b))
                    tick(2 * (rng()[1] + 10) / 2400.0)
                    act(MXv(ec), PSH(b), AF.Identity, kPSb(b) + (("GV",),), kMX(ec),
                        scale=GV[:, GC_BSCALE + j * 8 + ec: GC_BSCALE + j * 8 + ec + 1])
            on(hc, inner)()

        pp = [Piece([(wrows(b_w_in[j], 0, 8, nh * 512, 512), 0)], p_work(nh)) for nh in range(2)]
        gp = Piece([(b_w_grp[j].rearrange("g (dc p) e -> p (g dc) e", p=128), 0)], pool_work)
        pns = {0: PostNorm(GC_MIXPOST + l * 8), 1: PostNorm(GC_MIXPOST + l * 8)}

        def out_work(pi):
            def work(wv, wkey, first=True, last=True):
                hc = cmap(cur["hc"])
                def inner():
                    for qp in range(2):
                        bs_ = [next_bank(), next_bank()]
                        groups_fm([(bs_[i], lambda k, q=qp * 2 + i: k * 512 + q * 128) for i in range(2)],
                                  wv, wkey, range(KC), 0, KC - 1, lambda k: MXv(k), lambda k: kMX(k))
                        for i in range(2):
                            pns[hc.h].evac(bs_[i], pi * 4 + qp * 2 + i)
                on(hc, inner)()
            return work
        tail = [Piece([(wrows(b_w_out[j], 0, 8, pi * 512, 512), 0)], out_work(pi)) for pi in range(2)]

        def make_chain(hc):
            hcc = cmap(hc)
            return on_build(hcc, lambda: pns[hcc.h].chain(None, next_pre_fn(hcc.h)))
        run_sublayer(hA, hB, [], [], pp + [gp] + tail, make_chain, reuse=False)

    xT3 = xT.rearrange("(fc p) t -> p fc t", p=128)
    yT3 = yT.rearrange("(fc p) t -> p fc t", p=128)
    xsem2 = [[nc.alloc_semaphore("sem_x%d_%d" % (i, h)) for h in range(2)] for i in range(KC)]
    ysem2 = [[nc.alloc_semaphore("sem_y%d_%d" % (i, h)) for h in range(2)] for i in range(KC)]

    def load_x(t, fc):
        hc = cur["hc"]
        col, n = (0, 512) if hc.h == 0 else (512, 384)
        dst = X[:, fc * T + col: fc * T + col + n]
        src = xT3[:, fc, t * T + col: t * T + col + n]
        S.add("sp", lambda e: e.dma_start(out=dst, in_=src), (), (("X", fc, hc.h),), dsem=xsem2[fc][hc.h])

    def store_out(t, fc):
        hc = cur["hc"]
        col, n = rng()
        if t == 0:
            lo = max(col, HALO * 128)
            if lo >= col + n:
                return
            src = MS[:, fc * T + lo: fc * T + col + n]
            dst = yT3[:, fc, lo - HALO * 128: col + n - HALO * 128]
        else:
            o0 = (NCH - HALO) * 128 + (t - 1) * T
            src = MS[:, fc * T + col: fc * T + col + n]
            dst = yT3[:, fc, o0 + col: o0 + col + n]
        S.add("sp", lambda e: e.dma_start(out=dst, in_=src), (kMS(fc),), (), dsem=ysem2[fc][hc.h])

    full = tuple(layers) == (0, 1, 2, 3)
    live0 = {0: 1, 1: 2, 2: 2, 3: 3} if full else {l: 0 for l in layers}

    def halves_for(t, l, norm=False):
        c0 = live0[l] if t == 0 else 0
        if norm and l % 2 == 1:
            c0 = max(c0 - 1, 0)
        return Half(0, c0 * 128, 512 - c0 * 128), Half(1, 512, 384)

    seq = [(t, l, kind) for t in range(ntiles) for l in layers for kind in ("mix", "ffn")]

    def pre_gcol(l, kind):
        return (GC_MIXPRE if kind == "mix" else GC_FFNPRE) + l * 8

    for h in range(2):
        for fc in range(KC):
            on_build(Half(h, 0 if h == 0 else 512, 512 if h == 0 else 384), lambda fc=fc: load_x(0, fc))
    t0, l0, k0 = seq[0]
    for hc in halves_for(t0, l0, norm=True):
        for th in on_build(hc, lambda: prenorm_thunks(pre_gcol(l0, k0))):
            on(hc, th)()

    for i, (t, l, kind) in enumerate(seq):
        nxt = seq[i + 1] if i + 1 < len(seq) else None

        def next_pre_fn(h, nxt=nxt):
            if nxt is None:
                return None
            tn, ln, kn = nxt
            hcs = halves_for(tn, ln, norm=(kn == "mix"))
            return (hcs[h], pre_gcol(ln, kn))
        final_tile = t if (kind == "ffn" and l == layers[-1]) else None
        if kind == "mix":
            if l % 2 == 0:
                hA, hB = halves_for(t, l)
                mixer_a(l, hA, hB, next_pre_fn)
            else:
                hA, hB = halves_for(t, l, norm=True)
                hAc, hBc = halves_for(t, l)
                mixer_b(l, hA, hB, hAc, hBc, HALO if t == 0 else -1, next_pre_fn)
        else:
            hA, hB = halves_for(t, l)
            ffn(l, hA, hB, final_tile, next_pre_fn)
    drain_all()
    S.emit(nc, engsem)
    return nc


def _host_consts():
    s = np.arange(128)[:, None]
    t = np.arange(128)[None, :]
    maskT = (s <= t).astype(np.float32)
    wins = (2, 4, 8, 16)
    pm = np.zeros((3, 128, 4, 128), np.float32)
    for g, w in enumerate(wins):
        band = ((s <= t) & (s > t - w)).astype(np.float32)
        pm[0, :, g, :] = band / w - np.eye(128, dtype=np.float32)
        bandp = ((s - 128) > (t - w)).astype(np.float32)
        pm[1, :, g, :] = bandp / w
        cnt = np.minimum(t + 1, w).astype(np.float32)
        pm[2, :, g, :] = band / cnt - np.eye(128, dtype=np.float32)
    return maskT, pm.reshape(3, 128, 512)


def _vec_cols(v):
    return np.ascontiguousarray(v.reshape(-1, 128).T)


_NC_CACHE = {}


def _get_nc(layers, ntiles):
    key = (tuple(layers), ntiles)
    if key not in _NC_CACHE:
        _NC_CACHE[key] = build_nc(layers, ntiles)
    return _NC_CACHE[key]


def _make_in_maps(x, a_w_in, a_ln_g, a_ln_b, a_w_s, a_b_s, a_w_out, b_w_in, b_w_grp, b_scale, b_w_out,
                  mix_pre_g, mix_post_g, ffn_pre_g, ffn_post_g, ffn_w_gate, ffn_w_up, ffn_w_down):
    f = np.float32
    B, Sq, _ = x.shape
    maskT, pm = _host_consts()
    gv = np.zeros((128, GC_N), f)
    for l in range(4):
        gv[:, GC_MIXPRE + l * 8: GC_MIXPRE + l * 8 + 8] = _vec_cols(np.asarray(mix_pre_g[l], f))
        gv[:, GC_MIXPOST + l * 8: GC_MIXPOST + l * 8 + 8] = _vec_cols(np.asarray(mix_post_g[l], f))
        gv[:, GC_FFNPRE + l * 8: GC_FFNPRE + l * 8 + 8] = _vec_cols(np.asarray(ffn_pre_g[l], f))
        gv[:, GC_FFNPOST + l * 8: GC_FFNPOST + l * 8 + 8] = _vec_cols(np.asarray(ffn_post_g[l], f))
    for j in range(2):
        gv[:, GC_BSCALE + j * 8: GC_BSCALE + j * 8 + 8] = _vec_cols(np.asarray(b_scale[j], f))
        gv[:, GC_LNB + j * 16: GC_LNB + j * 16 + 16] = _vec_cols(np.asarray(a_ln_b[j], f))
    wsT = np.ascontiguousarray(np.transpose(np.asarray(a_w_s, f), (0, 3, 1, 2))).reshape(2, 128, 1024)
    bs = np.ascontiguousarray(np.asarray(a_b_s, f)).reshape(2, 1024)
    shared = {
        "a_w_in": np.ascontiguousarray(a_w_in, f), "a_w_out": np.ascontiguousarray(a_w_out, f),
        "b_w_in": np.ascontiguousarray(b_w_in, f), "b_w_grp": np.ascontiguousarray(b_w_grp, f),
        "b_w_out": np.ascontiguousarray(b_w_out, f), "ffn_w_gate": np.ascontiguousarray(ffn_w_gate, f),
        "ffn_w_up": np.ascontiguousarray(ffn_w_up, f), "ffn_w_down": np.ascontiguousarray(ffn_w_down, f),
        "gvec": gv, "a_ln_g": np.ascontiguousarray(a_ln_g, f), "a_w_sT": wsT, "a_b_s": bs, "maskT": maskT,
    }
    pm_mid = pm.copy()
    pm_mid[2] = pm_mid[0]
    in_maps = []
    for core in range(NCORES):
        b, half = core // 2, core % 2
        start = half * OWN - HALO * 128
        xw = np.zeros((NTOK, D), f)
        lo = max(start, 0)
        xw[lo - start:, :] = x[b, lo:start + NTOK, :]
        m = dict(shared)
        m["xT"] = np.ascontiguousarray(xw.T)
        m["poolm"] = pm if half == 0 else pm_mid
        in_maps.append(m)
    return in_maps


def kernel(**inputs):
    inputs = {k: np.asarray(v) for k, v in inputs.items()}
    x = inputs["x"].astype(np.float32, copy=False)
    B, Sq, _ = x.shape
    in_maps = _make_in_maps(**inputs)
    nc = _get_nc((0, 1, 2, 3), NTILES)
    res = run_bass_kernel_spmd(nc, in_maps, core_ids=list(range(NCORES)))
    out = np.empty((B, Sq, D), np.float32)
    for core in range(NCORES):
        b, half = core // 2, core % 2
        out[b, half * OWN:(half + 1) * OWN, :] = res.results[core]["yT"].T
    return out
```

```python
import numpy as np
import concourse.bass as bass
import concourse.mybir as mybir
from concourse.bass_utils import run_bass_kernel_spmd

F32 = mybir.dt.float32
BF16 = mybir.dt.bfloat16
AF = mybir.ActivationFunctionType
ALU = mybir.AluOpType
AX = mybir.AxisListType

NCORES = 8
D = 1024
KC = 8
NCH = 7
T = NCH * 128
HALVES = ((0, 512), (512, 384))
NTILES = 5
HALO = 3
NCHUNK = NTILES * NCH
NTOK = NCHUNK * 128
OWN = 4096
DFF = 2816
NJ = DFF // 128
EPS = 1e-6
WSLOT = 4096
NWSLOT = 3
USE_LNEXP = True

GC_MIXPRE, GC_MIXPOST, GC_FFNPRE, GC_FFNPOST, GC_BSCALE, GC_LNB, GC_N = 0, 32, 64, 96, 128, 144, 176


class Op:
    __slots__ = ("eng", "fn", "deps", "sig", "sigidx", "dsem", "dval", "pos", "gidx", "clk", "ckey")

    def __init__(self, eng, fn, dsem=None, dval=0):
        self.eng = eng
        self.fn = fn
        self.deps = []
        self.sig = False
        self.sigidx = 0
        self.dsem = dsem
        self.dval = dval


class Sched:
    ENGS = ("pe", "act", "dve", "pool", "sp")

    def __init__(self):
        self.ops = {e: [] for e in self.ENGS}
        self.res = {}
        self.dma_count = {}
        self.eclk = {e: {} for e in self.ENGS}
        self.gcount = 0

    def add(self, eng, fn, reads=(), writes=(), dsem=None):
        if dsem is not None:
            self.dma_count[dsem] = self.dma_count.get(dsem, 0) + 16
            op = Op(eng, fn, dsem, self.dma_count[dsem])
            op.ckey = ("dma", id(dsem))
            op.pos = op.dval
        else:
            op = Op(eng, fn)
            op.ckey = eng
            op.pos = len(self.ops[eng]) + 1
        self.gcount += 1
        op.gidx = self.gcount
        deps = {}
        res = self.res
        for k in reads:
            rec = res.get(k)
            if rec is not None and rec[0] is not None:
                deps[id(rec[0])] = (rec[0], "RAW")
        for k in writes:
            rec = res.get(k)
            if rec is not None:
                if rec[0] is not None and id(rec[0]) not in deps:
                    deps[id(rec[0])] = (rec[0], "WAW")
                for r in rec[1]:
                    if id(r) not in deps:
                        deps[id(r)] = (r, "WAR")
        cand = []
        for d, kind in deps.values():
            if d is op:
                continue
            if d.dsem is None and op.dsem is None and d.eng == eng:
                if eng == "pe":
                    continue
            cand.append(d)
        ek = self.eclk[eng]
        cand.sort(key=lambda d: -d.gidx)
        for d in cand:
            if ek.get(d.ckey, 0) >= d.pos:
                continue
            op.deps.append(d)
            for k, v in d.clk.items():
                if ek.get(k, 0) < v:
                    ek[k] = v
        clk = dict(ek)
        if op.dsem is None:
            clk[op.ckey] = op.pos
        else:
            clk[op.ckey] = op.pos
        op.clk = clk
        for k in reads:
            rec = res.get(k)
            if rec is None:
                res[k] = [None, [op]]
            else:
                rec[1].append(op)
        for k in writes:
            res[k] = [op, []]
        self.ops[eng].append(op)
        return op

    def emit(self, nc, engsem):
        for e in self.ENGS:
            for op in self.ops[e]:
                for d in op.deps:
                    if d.dsem is None:
                        d.sig = True
        for e in self.ENGS:
            n = 0
            for op in self.ops[e]:
                if op.dsem is None and op.sig:
                    n += 1
                    op.sigidx = n
        ops = self.ops

        def run(eng_name, eng):
            seen = {}
            for op in ops[eng_name]:
                need = {}
                for d in op.deps:
                    if d.dsem is not None:
                        sem, val = d.dsem, d.dval
                    else:
                        sem, val = engsem[d.eng], d.sigidx
                    k = id(sem)
                    if k not in need or need[k][1] < val:
                        need[k] = (sem, val)
                for k, (sem, val) in need.items():
                    if seen.get(k, 0) < val:
                        eng.wait_ge(sem, val)
                        seen[k] = val
                inst = op.fn(eng)
                if op.dsem is not None:
                    inst.then_inc(op.dsem, 16)
                elif op.sig:
                    inst.then_inc(engsem[eng_name], 1)

        with nc.Block() as block:
            @block.tensor
            def _(e):
                run("pe", e)

            @block.scalar
            def _(e):
                run("act", e)

            @block.vector
            def _(e):
                run("dve", e)

            @block.gpsimd
            def _(e):
                run("pool", e)

            @block.sync
            def _(e):
                run("sp", e)
                for sem, cnt in self.dma_count.items():
                    e.wait_ge(sem, cnt)


def build_nc(layers=(0, 1, 2, 3), ntiles=NTILES):
    nc = bass.Bass("TRN2", target_bir_lowering=False)
    S = Sched()

    def dram(name, shape, dt=F32, kind="ExternalInput"):
        return nc.dram_tensor(name, list(shape), dt, kind=kind).ap()

    xT = dram("xT", [D, NTOK])
    yT = dram("yT", [D, OWN], kind="ExternalOutput")
    a_w_in = dram("a_w_in", [2, D, 4096])
    a_w_out = dram("a_w_out", [2, 2048, D])
    b_w_in = dram("b_w_in", [2, D, D])
    b_w_grp = dram("b_w_grp", [2, 4, 256, 256])
    b_w_out = dram("b_w_out", [2, D, D])
    w_gate = dram("ffn_w_gate", [4, D, DFF])
    w_up = dram("ffn_w_up", [4, D, DFF])
    w_down = dram("ffn_w_down", [4, DFF, D])
    gvec_d = dram("gvec", [128, GC_N])
    lng_d = dram("a_ln_g", [2, 2048])
    wsT_d = dram("a_w_sT", [2, 128, 1024])
    bs_d = dram("a_b_s", [2, 1024])
    maskT_d = dram("maskT", [128, 128])
    pm_d = dram("poolm", [3, 128, 512])

    sb = nc.alloc_sbuf_tensor
    X = sb("X", [128, KC * T], F32)
    H = sb("H", [128, KC * T], BF16)
    MS = sb("MS", [128, KC * T], F32)
    SQ = sb("SQ", [128, KC * T], BF16)
    BIG = sb("BIG", [128, 28672], BF16)
    WR = sb("WR", [128, NWSLOT * WSLOT], BF16)
    RS = [sb("RS0", [128, T], F32)]
    SG = [sb("SG0", [128, T], BF16)]
    GV = sb("GV", [128, GC_N], F32)
    GBC = sb("GBC", [128, 2 * 2048], BF16)
    WSM = sb("WSM", [128, 2 * 1024], BF16)
    BIAS = sb("BIAS", [128, 2 * 2048], F32)
    PM = sb("PM", [128, 3 * 512], BF16)
    PH = sb("PH", [128, 2 * 1024], BF16)
    ONESM = sb("ONESM", [128, 128], BF16)
    ONES1 = sb("ONES1", [128, 128], BF16)
    EPSC = sb("EPSC", [128, 1], F32)
    ST = sb("STATS", [128, 64], F32)
    ps = nc.alloc_psum_tensor("ps", [128, 4096], F32)

    engsem = {e: nc.alloc_semaphore("sem_" + e) for e in ("pe", "act", "dve")}
    wsem = [nc.alloc_semaphore("sem_w%d" % i) for i in range(NWSLOT)]
    xsem = nc.alloc_semaphore("sem_x")
    ysem = nc.alloc_semaphore("sem_y")
    _cs = [0]

    def csem_new():
        _cs[0] += 1
        return nc.alloc_semaphore("sem_c%d" % _cs[0])

    class Half:
        def __init__(self, h, col, n):
            self.h, self.col, self.n = h, col, n
            self.c_lo, self.c_hi = col // 128, (col + n) // 128

    cur = {"hc": None}

    def rng():
        hc = cur["hc"]
        return hc.col, hc.n

    def on(hc, fn):
        def run():
            old = cur["hc"]
            cur["hc"] = hc
            try:
                fn()
            finally:
                cur["hc"] = old
        return run

    def fm(buf, fc):
        c, n = rng()
        return buf[:, fc * T + c: fc * T + c + n]

    def Xv(fc): return fm(X, fc)
    def Hv(fc): return fm(H, fc)
    def MSv(fc): return fm(MS, fc)
    def SQv(fc): return fm(SQ, fc)
    def kX(fc): return ("X", fc, cur["hc"].h)
    def kH(fc): return ("H", fc, cur["hc"].h)
    def kMS(fc): return ("MS", fc, cur["hc"].h)

    def blocks(name, lo, n):
        return tuple((name, b) for b in range(lo // 128, (lo + n + 127) // 128))

    def kSQ(fc):
        c, n = rng()
        return blocks("SQ", fc * T + c, n)

    def bigfm(off, fc):
        c, n = rng()
        return BIG[:, off + fc * T + c: off + fc * T + c + n], blocks("BIG", off + fc * T + c, n)

    VOFF = 16 * T
    PLOFF = 7 * 1024
    MXOFF = PLOFF + 8 * T
    def Uv(fc): return bigfm(0, fc)[0]
    def kU(fc): return bigfm(0, fc)[1]
    def Gv(j): return bigfm(0, j)[0]
    def kG(j): return bigfm(0, j)[1]
    def PLv(fc): return bigfm(PLOFF, fc)[0]
    def kPL(fc): return bigfm(PLOFF, fc)[1]
    def MXv(fc): return bigfm(MXOFF, fc)[0]
    def kMX(fc): return bigfm(MXOFF, fc)[1]
    def Vv(c, c0=0, n=2048): return BIG[:, VOFF + c * 2048 + c0: VOFF + c * 2048 + c0 + n]
    def kV(c, c0=0, n=2048): return blocks("BIG", VOFF + c * 2048 + c0, n)
    def PTv(c, c0=0, n=1024): return BIG[:, c * 1024 + c0: c * 1024 + c0 + n]
    def kPT(c, c0=0, n=1024): return blocks("BIG", c * 1024 + c0, n)

    def RSv():
        c, n = rng()
        return RS[0][:, c:c + n]
    def kRS(): return ("RS", cur["hc"].h)
    def SGv():
        c, n = rng()
        return SG[0][:, c:c + n]
    def kSG(): return ("SG", cur["hc"].h)

    def PSH(bank, c0=0, n=None):
        if n is None:
            n = rng()[1]
        return ps[:, bank * 512 + c0: bank * 512 + c0 + n]
    def kPSb(bank): return (("ps", bank),)
    def STATb(): return 6 + cur["hc"].h

    state = {"bank": 0, "w": 0, "use": 0}
    slot_use = [0] * NWSLOT

    def next_bank():
        b = state["bank"]
        state["bank"] = (b + 1) % 6
        return b

    def pe_mm(out, lhsT, rhs, start, stop, reads, writes):
        S.add("pe", lambda e: e.matmul(out, lhsT=lhsT, rhs=rhs, start=start, stop=stop), reads, writes)

    def act(out, in_, func, reads, writes, scale=None, bias=None, accum=None):
        kw = {}
        if scale is not None:
            kw["scale"] = scale
        if bias is not None:
            kw["bias"] = bias
        if accum is not None:
            kw["accum_out"] = accum
        S.add("act", lambda e: e.activation(out=out, in_=in_, func=func, **kw), reads, writes)

    def dve(fn, reads, writes):
        S.add("dve", fn, reads, writes)

    def wload(parts):
        w = min(range(NWSLOT), key=lambda i: slot_use[i])
        state["use"] += 1
        slot_use[w] = state["use"]
        state["last_slot"] = w
        key = tuple(("W", w, i) for i in range(2))
        for i, (src, off) in enumerate(parts):
            k, n = src.shape[1], src.shape[2]
            dst = WR[:, w * WSLOT + off: w * WSLOT + off + k * n].rearrange("p (k n) -> p k n", k=k)
            wk = (key[i],) if len(parts) > 1 else key
            S.add("pool", lambda e, dst=dst, src=src: e.dma_start(out=dst, in_=src), (), wk, dsem=wsem[w])
        base = w * WSLOT
        return (lambda off, n: WR[:, base + off: base + off + n]), key

    def wrows(w2d, r0, nk, c0, n):
        return w2d[r0 * 128:(r0 + nk) * 128, c0:c0 + n].rearrange("(k p) n -> p k n", p=128)

    S.add("sp", lambda e: e.dma_start(out=GV[:], in_=gvec_d), (), (("GV",),), dsem=csem_new())
    S.add("pool", lambda e: e.dma_start(out=PM[:].rearrange("p (a n) -> p a n", a=3),
                                        in_=pm_d.rearrange("a p n -> p a n")), (), (("PM",),), dsem=csem_new())
    S.add("pool", lambda e: e.dma_start(out=GBC[:].rearrange("p (a n) -> p a n", a=2),
                                        in_=lng_d.partition_broadcast(128)), (), (("GBC",),), dsem=csem_new())
    dve(lambda e: e.memset(ONESM[:], 1.0 / 1024.0), (), (("ONESM",),))
    dve(lambda e: e.memset(ONES1[:], 1.0), (), (("ONES1",),))
    dve(lambda e: e.memset(EPSC[:], EPS), (), (("EPSC",),))
    dve(lambda e: e.memset(PH[:], 0.0), (), (("PH", 0), ("PH", 1)))
    dve(lambda e: e.memset(ST[:], 0.0), (), (("ST", 0), ("ST", 1)))

    a_layers = sorted({l // 2 for l in layers if l % 2 == 0})
    if a_layers:
        kscr = tuple(("MS", f, h) for f in range(3) for h in range(2))
        kscr2 = tuple(("MS", f, h) for f in (3, 4) for h in range(2))
        S.add("sp", lambda e: e.dma_start(out=MS[:, 2048:2176], in_=maskT_d), (), kscr, dsem=csem_new())
        for j in a_layers:
            S.add("sp", lambda e, j=j: e.dma_start(out=MS[:, 0:1024], in_=wsT_d[j]), (), kscr, dsem=csem_new())
            S.add("sp", lambda e, j=j: e.dma_start(out=MS[:, 2688:3712], in_=bs_d[j].partition_broadcast(128)),
                  (), kscr2, dsem=csem_new())
            dve(lambda e, j=j: e.tensor_tensor(
                out=WSM[:, j * 1024:(j + 1) * 1024].rearrange("p (g t) -> p g t", g=8),
                in0=MS[:, 0:1024].rearrange("p (g t) -> p g t", g=8),
                in1=MS[:, 2048:2176].unsqueeze(1).broadcast_to([128, 8, 128]), op=ALU.mult),
                kscr, (("WSM", j),))
            for g in range(8):
                pe_mm(ps[:, g * 128:(g + 1) * 128], ONES1[:], WSM[:, j * 1024 + g * 128: j * 1024 + (g + 1) * 128],
                      True, True, (("ONES1",), ("WSM", j)), (("ps", 0), ("ps", 1)))
            for fc in range(16):
                g = fc // 2
                dve(lambda e, j=j, fc=fc, g=g: e.scalar_tensor_tensor(
                    out=BIAS[:, j * 2048 + fc * 128: j * 2048 + (fc + 1) * 128],
                    in0=ps[:, g * 128:(g + 1) * 128], scalar=GV[:, GC_LNB + j * 16 + fc: GC_LNB + j * 16 + fc + 1],
                    in1=MS[:, 2688 + g * 128: 2688 + (g + 1) * 128], op0=ALU.mult, op1=ALU.add),
                    (("ps", 0), ("ps", 1), ("GV",)) + kscr2, (("BIAS", j),))

    def stats_mm(fc, first, last):
        sb_ = STATb()
        pe_mm(PSH(sb_), ONESM[:], SQv(fc), first, last, kSQ(fc) + (("ONESM",),), kPSb(sb_))

    def rstd_ops():
        sb_ = STATb()
        rv = RSv()
        if USE_LNEXP:
            return [
                lambda: act(rv, PSH(sb_, 0, rv.shape[1]), AF.Ln, kPSb(sb_) + (("EPSC",),), (kRS(),), bias=EPSC[:]),
                lambda: act(rv, rv, AF.Exp, (kRS(),), (kRS(),), scale=-0.5),
            ]
        return [
            lambda: act(rv, PSH(sb_, 0, rv.shape[1]), AF.Sqrt, kPSb(sb_) + (("EPSC",),), (kRS(),), bias=EPSC[:]),
            lambda: dve(lambda e: e.reciprocal(out=rv, in_=rv), (kRS(),), (kRS(),)),
        ]

    def prenorm_thunks(gcol):
        th = []

        def sq(fc):
            act(SQv(fc), Xv(fc), AF.Square, (kX(fc),), kSQ(fc))
            stats_mm(fc, fc == 0, fc == KC - 1)

        def hh(fc):
            o, a, g, rr = Hv(fc), Xv(fc), GV[:, gcol + fc: gcol + fc + 1], RSv()
            dve(lambda e: e.scalar_tensor_tensor(out=o, in0=a, scalar=g, in1=rr, op0=ALU.mult, op1=ALU.mult),
                (kX(fc), ("GV",), kRS()), (kH(fc),))
        for fc in range(KC):
            th.append(lambda fc=fc: sq(fc))
        th.extend(rstd_ops())
        for fc in range(KC):
            th.append(lambda fc=fc: hh(fc))
        return th

    class PostNorm:
        def __init__(self, gcol):
            self.gcol = gcol
            self.pending = None
            self.nstat = 0

        def evac(self, bank, oc):
            gcol = self.gcol
            act(SQv(oc), PSH(bank), AF.Square, kPSb(bank), kSQ(oc))
            act(MSv(oc), PSH(bank), AF.Identity, kPSb(bank) + (("GV",),), (kMS(oc),),
                scale=GV[:, gcol + oc: gcol + oc + 1])
            if self.pending is not None:
                stats_mm(self.pending, self.nstat == 0, False)
                self.nstat += 1
            self.pending = oc

        def chain(self, final_tile, next_pre):
            hc = cur["hc"]
            th = [lambda: stats_mm(self.pending, self.nstat == 0, True)]
            th.extend(rstd_ops())
            ft = final_tile
            sqs = []
            if next_pre is not None:
                hcn, gcoln = next_pre
                pre = [on(hcn, t) for t in on_build(hcn, lambda: prenorm_thunks(gcoln))]
            else:
                pre = []

            def mult(fc):
                o, rr = MSv(fc), RSv()
                dve(lambda e: e.tensor_tensor(out=o, in0=o, in1=rr, op=ALU.mult), (kMS(fc), kRS()), (kMS(fc),))

            def add(fc):
                m, x = MSv(fc), Xv(fc)
                if ft is None:
                    dve(lambda e: e.tensor_tensor(out=x, in0=m, in1=x, op=ALU.add), (kMS(fc), kX(fc)), (kX(fc),))
                else:
                    dve(lambda e: e.tensor_tensor(out=m, in0=m, in1=x, op=ALU.add), (kMS(fc), kX(fc)), (kMS(fc),))
                    store_out(ft, fc)
                    if ft + 1 < ntiles:
                        load_x(ft + 1, fc)
            n = hc.n
            d = (n + 60) / 960.0 * 1e-3 * 1e3
            a = (n + 240) / 1200.0
            r1 = 1.0 + 1.3 + 2 * a + 0.5 + (0.0 if USE_LNEXP else 2.5)
            times = [0.8, 1.0, 1.0]
            th.append(lambda: mult(0))
            times.append(r1)
            i = 1
            for fc in range(1, KC):
                th.append(lambda fc=fc: mult(fc))
                times.append(r1 + i * d)
                i += 1
                th.append(lambda fc=fc: add(fc - 1))
                times.append(r1 + i * d)
                i += 1
                if fc - 1 < len(pre) and fc - 1 < KC:
                    th.append(pre[fc - 1])
                    times.append(r1 + i * d)
            th.append(lambda: add(KC - 1))
            times.append(r1 + i * d)
            i += 1
            tl = r1 + i * d
            rest = pre[KC - 1:]
            if rest:
                r2 = tl + a + 0.3 + 1.3 + 2 * a + 0.5 + (0.0 if USE_LNEXP else 2.5)
                rt = [tl, tl + a + 0.3, tl + a + 0.3] + [r2 + k * d for k in range(KC)]
                th.extend(rest)
                times.extend(rt[:len(rest)])
            assert len(times) == len(th), (len(times), len(th))
            return [(tm, on(hc, t)) for tm, t in zip(times, th)]

    def on_build(hc, fn):
        old = cur["hc"]
        cur["hc"] = hc
        try:
            return fn()
        finally:
            cur["hc"] = old

    def groups_fm(banks_woffs, wv, wkey, ks, kfirst, klast, rhs_fn, rhs_keys_fn, kloc=None):
        for k in ks:
            kk = k if kloc is None else kloc(k)
            for bank, woff_fn in banks_woffs:
                pe_mm(PSH(bank), wv(woff_fn(kk), 128), rhs_fn(k), k == kfirst, k == klast,
                      wkey + rhs_keys_fn(k), kPSb(bank))
            tick(len(banks_woffs) * (rng()[1] + 10) / 2400.0)

    slot_owner = {}

    class Piece:
        def __init__(self, parts, work, group=None, ipart=0, npart=1):
            self.parts, self.work = parts, work
            self.w = None
            self.slot = None
            self.group, self.ipart, self.npart = group, ipart, npart

        def resident(self):
            return self.parts is not None and self.slot is not None and slot_owner.get(self.slot) is self

        def load(self):
            if self.parts is None:
                self.w = (None, None)
            elif not self.resident():
                self.w = wload(self.parts)
                self.slot = state["last_slot"]
                slot_owner[self.slot] = self

        def run(self, hc, first=True, last=True):
            if self.slot is not None:
                state["use"] += 1
                slot_use[self.slot] = state["use"]
            on(hc, lambda: self.work(self.w[0], self.w[1], first, last))()

    def natural_order(tail):
        return [(p, p.ipart == 0, p.ipart == p.npart - 1) for p in tail]

    def reuse_order(tail):
        groups = []
        for p in tail:
            if p.group not in groups:
                groups.append(p.group)
        parts = {g: [p for p in tail if p.group == g] for g in groups}
        res = {id(p) for p in tail if p.resident()}
        comp = [g for g in groups if all(id(p) in res for p in parts[g])]
        part = [g for g in groups if g not in comp and any(id(p) in res for p in parts[g])]
        rest = [g for g in groups if g not in comp and g not in part]
        order = []
        for g in comp + part + rest:
            ps = sorted(parts[g], key=lambda p: (0 if id(p) in res else 1, p.ipart))
            for i, p in enumerate(ps):
                order.append((p, i == 0, i == len(ps) - 1))
        return order

    bg = []
    clock = {"t": 0.0}

    def tick(dt):
        clock["t"] += dt
        while bg and bg[0][0] <= clock["t"]:
            bg.pop(0)[1]()

    def drain_all():
        while bg:
            bg.pop(0)[1]()

    def add_chain(items):
        t0 = clock["t"]
        for tm, th in items:
            bg.append((t0 + tm, th))

    def run_sublayer(hA, hB, head, mid, tail, make_chain, reuse=False, _u=None):
        for p in head:
            p.load()
        for p in head:
            p.run(hA)
        if head or mid:
            drain_all()
        for p in head:
            p.run(hB)
        for p in mid:
            p.load()
            p.run(hA)
            p.run(hB)
        for p, first, last in natural_order(tail):
            p.load()
            p.run(hA, first, last)
        drain_all()
        add_chain(make_chain(hA))
        for p, first, last in (reuse_order(tail) if reuse else natural_order(tail)):
            p.load()
            p.run(hB, first, last)
        drain_all()
        add_chain(make_chain(hB))

    def ffn(l, hA, hB, final_tile, next_pre_fn):
        sgate = {}

        def gu_work(pi):
            def work(wv, wkey, first=True, last=True):
                for jj in range(2):
                    j = pi * 2 + jj
                    bg_ = next_bank()
                    bu_ = next_bank()
                    groups_fm([(bg_, lambda k: k * 256 + jj * 128), (bu_, lambda k: 2048 + k * 256 + jj * 128)],
                              wv, wkey, range(KC), 0, KC - 1, lambda k: Hv(k), lambda k: (kH(k),))
                    sgv = SGv()
                    act(sgv, PSH(bg_), AF.Silu, kPSb(bg_), (kSG(),))
                    o, a = Gv(j), PSH(bu_)
                    dve(lambda e, o=o, a=a, sgv=sgv: e.tensor_tensor(out=o, in0=a, in1=sgv, op=ALU.mult),
                        kPSb(bu_) + (kSG(),), kG(j))
            return work
        pieces = [Piece([(wrows(w_gate[l], 0, 8, pi * 256, 256), 0), (wrows(w_up[l], 0, 8, pi * 256, 256), 2048)],
                        gu_work(pi)) for pi in range(NJ // 2)]
        pns = {0: PostNorm(GC_FFNPOST + l * 8), 1: PostNorm(GC_FFNPOST + l * 8)}

        dbanks = {}

        def down_work(pair, kh):
            def work(wv, wkey, first=True, last=True):
                h = cur["hc"].h
                if first:
                    dbanks[(h, pair)] = (next_bank(), next_bank())
                b0, b1 = dbanks[(h, pair)]
                ks = range(kh * 11, kh * 11 + 11)
                groups_fm([(b0, lambda kk: kk * 256), (b1, lambda kk: kk * 256 + 128)], wv, wkey,
                          ks, ks[0] if first else None, ks[-1] if last else None,
                          lambda k: Gv(k), lambda k: kG(k), kloc=lambda k: k - kh * 11)
                if last:
                    pns[h].evac(b0, pair * 2)
                    pns[h].evac(b1, pair * 2 + 1)
            return work
        tail = [Piece([(wrows(w_down[l], kh * 11, 11, pair * 256, 256), 0)], down_work(pair, kh),
                      group=pair, ipart=kh, npart=2)
                for pair in range(KC // 2) for kh in range(2)]

        def make_chain(hc):
            return on_build(hc, lambda: pns[hc.h].chain(final_tile, next_pre_fn(hc.h)))
        run_sublayer(hA, hB, pieces[:3], pieces[3:], tail, make_chain, reuse=True)

    def mixer_a(l, hA, hB, next_pre_fn):
        j = l // 2

        def v_work(vb):
            def work(wv, wkey, first=True, last=True):
                hc = cur["hc"]
                cs_all = list(range(hc.c_lo, hc.c_hi))
                for i0 in range(0, len(cs_all), 2):
                    cs = cs_all[i0:i0 + 2]
                    bks = [next_bank() for _ in cs]
                    for k in range(KC):
                        for c, b in zip(cs, bks):
                            pe_mm(ps[:, b * 512:(b + 1) * 512], H[:, k * T + c * 128: k * T + (c + 1) * 128],
                                  wv(k * 512, 512), k == 0, k == KC - 1, (kH(k),) + wkey, kPSb(b))
                        tick(len(cs) * 522 / 2400.0)
                    for c, b in zip(cs, bks):
                        act(Vv(c, vb * 512, 512), ps[:, b * 512:(b + 1) * 512], AF.Gelu, kPSb(b),
                            kV(c, vb * 512, 512) + (("ST", hc.h),), accum=ST[:, c * 4 + vb: c * 4 + vb + 1])
            return work

        def ln_square(idx):
            hc = cur["hc"]
            c = hc.c_lo + idx
            if c >= hc.c_hi:
                return
            kst = (("ST", hc.h),)
            off = (0 if hc.h == 0 else 4096) + (idx % 2) * 1024
            junk = SQ[:, off:off + 2048]
            act(junk, Vv(c), AF.Square, kV(c), blocks("SQ", off, 2048) + kst, accum=ST[:, 28 + c: 29 + c])

        def ln_stats():
            hc = cur["hc"]
            lo, hi = hc.c_lo, hc.c_hi
            kst = (("ST", hc.h),)
            junk = SQ[:, 0:2048] if hc.h == 0 else SQ[:, 4096:6144]
            kjunk = blocks("SQ", 0 if hc.h == 0 else 4096, 2048)
            dve(lambda e: e.tensor_reduce(out=ST[:, 35 + lo:35 + hi],
                                          in_=ST[:, 4 * lo:4 * hi].rearrange("p (c v) -> p c v", v=4),
                                          axis=AX.X, op=ALU.add), kst, kst)
            dve(lambda e: e.tensor_scalar(out=ST[:, 35 + lo:35 + hi], in0=ST[:, 35 + lo:35 + hi],
                                          scalar1=1.0 / 2048.0, scalar2=None, op0=ALU.mult), kst, kst)
            dve(lambda e: e.tensor_tensor(out=ST[:, 42 + lo:42 + hi], in0=ST[:, 35 + lo:35 + hi],
                                          in1=ST[:, 35 + lo:35 + hi], op=ALU.mult), kst, kst)
            dve(lambda e: e.scalar_tensor_tensor(out=ST[:, 42 + lo:42 + hi], in0=ST[:, 28 + lo:28 + hi],
                                                 scalar=1.0 / 2048.0, in1=ST[:, 42 + lo:42 + hi],
                                                 op0=ALU.mult, op1=ALU.subtract), kst, kst)
            dve(lambda e: e.tensor_scalar(out=ST[:, 42 + lo:42 + hi], in0=ST[:, 42 + lo:42 + hi], scalar1=0.0,
                                          scalar2=None, op0=ALU.max), kst, kst)
            if USE_LNEXP:
                act(ST[:, 49 + lo:49 + hi], ST[:, 42 + lo:42 + hi], AF.Ln, kst + (("EPSC",),), kst, bias=EPSC[:])
                act(ST[:, 49 + lo:49 + hi], ST[:, 49 + lo:49 + hi], AF.Exp, kst, kst, scale=-0.5)
            else:
                act(ST[:, 49 + lo:49 + hi], ST[:, 42 + lo:42 + hi], AF.Sqrt, kst + (("EPSC",),), kst, bias=EPSC[:])
                dve(lambda e: e.reciprocal(out=ST[:, 49 + lo:49 + hi], in_=ST[:, 49 + lo:49 + hi]), kst, kst)

        def ln_apply():
            hc = cur["hc"]
            kst = (("ST", hc.h),)
            for c in range(hc.c_lo, hc.c_hi):
                dve(lambda e, c=c: e.scalar_tensor_tensor(
                    out=Vv(c), in0=Vv(c), scalar=ST[:, 35 + c: 36 + c],
                    in1=GBC[:, j * 2048:(j + 1) * 2048], op0=ALU.subtract, op1=ALU.mult),
                    kV(c) + kst + (("GBC",),), kV(c))
                dve(lambda e, c=c: e.tensor_scalar(
                    out=SQ[:, c * 1024:(c + 1) * 1024], in0=WSM[:, j * 1024:(j + 1) * 1024],
                    scalar1=ST[:, 49 + c: 50 + c], scalar2=None, op0=ALU.mult),
                    (("WSM", j),) + kst, blocks("SQ", c * 1024, 1024))

        def u_work(ub):
            def work(wv, wkey, first=True, last=True):
                for qp in range(2):
                    bs_ = [next_bank(), next_bank()]
                    groups_fm([(bs_[i], lambda k, q=qp * 2 + i: k * 512 + q * 128) for i in range(2)],
                              wv, wkey, range(KC), 0, KC - 1, lambda k: Hv(k), lambda k: (kH(k),))
                    for i in range(2):
                        oc = ub * 4 + qp * 2 + i
                        act(Uv(oc), PSH(bs_[i]), AF.Gelu, kPSb(bs_[i]), kU(oc))
                    if ub < 2:
                        ln_square(ub * 2 + qp)
                if ub == 1:
                    ln_stats()
                if ub == 2:
                    ln_apply()
            return work

        def sp_work(fcs):
            def work(wv, wkey, first=True, last=True):
                hc = cur["hc"]
                lo, hi = hc.c_lo, hc.c_hi
                nl = hi - lo
                prev = None

                def gate_mul(pfc, ptmp):
                    u, m = Uv(pfc), MSv(ptmp)
                    dve(lambda e: e.tensor_tensor(out=u, in0=m, in1=u, op=ALU.mult), (kMS(ptmp),) + kU(pfc), kU(pfc))
                for fc in fcs:
                    g = fc // 2
                    b = next_bank()
                    for c in range(lo, hi):
                        pe_mm(PSH(b, (c - lo) * 128, 128), Vv(c, fc * 128, 128),
                              SQ[:, c * 1024 + g * 128: c * 1024 + (g + 1) * 128], True, True,
                              kV(c, fc * 128, 128) + blocks("SQ", c * 1024 + g * 128, 128), kPSb(b))
                    tick(nl * 0.1)
                    tmp = fc % 8
                    o = MSv(tmp).rearrange("p (c t) -> p c t", c=nl)
                    a = PSH(b).rearrange("p (c t) -> p c t", c=nl)
                    bb = BIAS[:, j * 2048 + fc * 128: j * 2048 + (fc + 1) * 128].unsqueeze(1).broadcast_to([128, nl, 128])
                    dve(lambda e, o=o, a=a, bb=bb: e.tensor_tensor(out=o, in0=a, in1=bb, op=ALU.add),
                        kPSb(b) + (("BIAS", j),), (kMS(tmp),))
                    if prev is not None:
                        gate_mul(*prev)
                    prev = (fc, tmp)
                gate_mul(*prev)
            return work

        def zero_st():
            dve(lambda e: e.memset(ST[:, 0:35], 0.0), (), (("ST", 0), ("ST", 1)))
        zero_st()
        vp = [Piece([(wrows(a_w_in[j], 0, 8, 2048 + vb * 512, 512), 0)], v_work(vb)) for vb in range(4)]
        up = [Piece([(wrows(a_w_in[j], 0, 8, ub * 512, 512), 0)], u_work(ub)) for ub in range(4)]
        spp = [Piece(None, sp_work(range(q * 4, q * 4 + 4))) for q in range(4)]
        pns = {0: PostNorm(GC_MIXPOST + l * 8), 1: PostNorm(GC_MIXPOST + l * 8)}

        def out_work(pi):
            def work(wv, wkey, first=True, last=True):
                bs_ = [next_bank(), next_bank()]
                groups_fm([(bs_[q], lambda k, q=q: k * 256 + q * 128) for q in range(2)],
                          wv, wkey, range(16), 0, 15, lambda k: Uv(k), lambda k: kU(k))
                for q in range(2):
                    pns[cur["hc"].h].evac(bs_[q], pi * 2 + q)
            return work
        tail = [Piece([(wrows(a_w_out[j], 0, 16, pi * 256, 256), 0)], out_work(pi), group=pi) for pi in range(4)]

        def make_chain(hc):
            return on_build(hc, lambda: pns[hc.h].chain(None, next_pre_fn(hc.h)))
        run_sublayer(hA, hB, vp[:3], vp[3:] + up + spp, tail, make_chain, reuse=True)

    def mixer_b(l, hA, hB, hAc, hBc, first_chunk, next_pre_fn):
        j = l // 2

        def p_work(nh):
            def work(wv, wkey, first=True, last=True):
                hc = cur["hc"]
                cs_all = list(range(hc.c_lo, hc.c_hi))
                for i0 in range(0, len(cs_all), 2):
                    cs = cs_all[i0:i0 + 2]
                    bks = [next_bank() for _ in cs]
                    for k in range(KC):
                        for c, b in zip(cs, bks):
                            pe_mm(ps[:, b * 512:(b + 1) * 512], H[:, k * T + c * 128: k * T + (c + 1) * 128],
                                  wv(k * 512, 512), k == 0, k == KC - 1, (kH(k),) + wkey, kPSb(b))
                        tick(len(cs) * 522 / 2400.0)
                    for c, b in zip(cs, bks):
                        act(PTv(c, nh * 512, 512), ps[:, b * 512:(b + 1) * 512], AF.Copy, kPSb(b),
                            kPT(c, nh * 512, 512))
            return work

        def cmap(hc):
            return hAc if hc.h == 0 else hBc

        def pool_work(wv, wkey, first=True, last=True):
            hc = cmap(cur["hc"])
            def inner():
                lo, hi = hc.c_lo, hc.c_hi
                for fc in range(KC):
                    g = fc // 2
                    b = next_bank()
                    for c in range(lo, hi):
                        pm = 2 if c == first_chunk else 0
                        if c == 0:
                            prev_ap, prev_key = PH[:, j * 1024 + fc * 128: j * 1024 + (fc + 1) * 128], (("PH", j),)
                        else:
                            prev_ap, prev_key = PTv(c - 1, fc * 128, 128), kPT(c - 1, fc * 128, 128)
                        pe_mm(PSH(b, (c - lo) * 128, 128), PTv(c, fc * 128, 128),
                              PM[:, pm * 512 + g * 128: pm * 512 + (g + 1) * 128], True, False,
                              kPT(c, fc * 128, 128) + (("PM",),), kPSb(b))
                        pe_mm(PSH(b, (c - lo) * 128, 128), prev_ap,
                              PM[:, 512 + g * 128: 512 + (g + 1) * 128], False, True,
                              prev_key + (("PM",),), kPSb(b))
                    tick((hi - lo) * 0.2)
                    act(PLv(fc), PSH(b), AF.Copy, kPSb(b), kPL(fc))
                if hc.h == 1:
                    dve(lambda e: e.tensor_copy(out=PH[:, j * 1024:(j + 1) * 1024], in_=PTv(NCH - 1)),
                        kPT(NCH - 1), (("PH", j),))
                for ec in range(KC):
                    g = ec // 2
                    b = next_bank()
                    for dc in range(2):
                        pe_mm(PSH(b), wv((g * 2 + dc) * 256 + (ec % 2) * 128, 128), PLv(2 * g + dc),
                              dc == 0, dc == 1, wkey + kPL(2 * g + dc), kPSb(b))
                    tick(2 * (rng()[1] + 10) / 2400.0)
                    act(MXv(ec), PSH(b), AF.Identity, kPSb(b) + (("GV",),), kMX(ec),
                        scale=GV[:, GC_BSCALE + j * 8 + ec: GC_BSCALE + j * 8 + ec + 1])
            on(hc, inner)()

        pp = [Piece([(wrows(b_w_in[j], 0, 8, nh * 512, 512), 0)], p_work(nh)) for nh in range(2)]
        gp = Piece([(b_w_grp[j].rearrange("g (dc p) e -> p (g dc) e", p=128), 0)], pool_work)
        pns = {0: PostNorm(GC_MIXPOST + l * 8), 1: PostNorm(GC_MIXPOST + l * 8)}

        def out_work(pi):
            def work(wv, wkey, first=True, last=True):
                hc = cmap(cur["hc"])
                def inner():
                    for qp in range(2):
                        bs_ = [next_bank(), next_bank()]
                        groups_fm([(bs_[i], lambda k, q=qp * 2 + i: k * 512 + q * 128) for i in range(2)],
                                  wv, wkey, range(KC), 0, KC - 1, lambda k: MXv(k), lambda k: kMX(k))
                        for i in range(2):
                            pns[hc.h].evac(bs_[i], pi * 4 + qp * 2 + i)
                on(hc, inner)()
            return work
        tail = [Piece([(wrows(b_w_out[j], 0, 8, pi * 512, 512), 0)], out_work(pi)) for pi in range(2)]

        def make_chain(hc):
            hcc = cmap(hc)
            return on_build(hcc, lambda: pns[hcc.h].chain(None, next_pre_fn(hcc.h)))
        run_sublayer(hA, hB, [], [], pp + [gp] + tail, make_chain, reuse=False)

    xT3 = xT.rearrange("(fc p) t -> p fc t", p=128)
    yT3 = yT.rearrange("(fc p) t -> p fc t", p=128)
    xsem2 = [[nc.alloc_semaphore("sem_x%d_%d" % (i, h)) for h in range(2)] for i in range(KC)]
    ysem2 = [[nc.alloc_semaphore("sem_y%d_%d" % (i, h)) for h in range(2)] for i in range(KC)]

    def load_x(t, fc):
        hc = cur["hc"]
        col, n = (0, 512) if hc.h == 0 else (512, 384)
        dst = X[:, fc * T + col: fc * T + col + n]
        src = xT3[:, fc, t * T + col: t * T + col + n]
        S.add("sp", lambda e: e.dma_start(out=dst, in_=src), (), (("X", fc, hc.h),), dsem=xsem2[fc][hc.h])

    def store_out(t, fc):
        hc = cur["hc"]
        col, n = rng()
        if t == 0:
            lo = max(col, HALO * 128)
            if lo >= col + n:
                return
            src = MS[:, fc * T + lo: fc * T + col + n]
            dst = yT3[:, fc, lo - HALO * 128: col + n - HALO * 128]
        else:
            o0 = (NCH - HALO) * 128 + (t - 1) * T
            src = MS[:, fc * T + col: fc * T + col + n]
            dst = yT3[:, fc, o0 + col: o0 + col + n]
        S.add("sp", lambda e: e.dma_start(out=dst, in_=src), (kMS(fc),), (), dsem=ysem2[fc][hc.h])

    full = tuple(layers) == (0, 1, 2, 3)
    live0 = {0: 1, 1: 2, 2: 2, 3: 3} if full else {l: 0 for l in layers}

    def halves_for(t, l, norm=False):
        c0 = live0[l] if t == 0 else 0
        if norm and l % 2 == 1:
            c0 = max(c0 - 1, 0)
        return Half(0, c0 * 128, 512 - c0 * 128), Half(1, 512, 384)

    seq = [(t, l, kind) for t in range(ntiles) for l in layers for kind in ("mix", "ffn")]

    def pre_gcol(l, kind):
        return (GC_MIXPRE if kind == "mix" else GC_FFNPRE) + l * 8

    for h in range(2):
        for fc in range(KC):
            on_build(Half(h, 0 if h == 0 else 512, 512 if h == 0 else 384), lambda fc=fc: load_x(0, fc))
    t0, l0, k0 = seq[0]
    for hc in halves_for(t0, l0, norm=True):
        for th in on_build(hc, lambda: prenorm_thunks(pre_gcol(l0, k0))):
            on(hc, th)()

    for i, (t, l, kind) in enumerate(seq):
        nxt = seq[i + 1] if i + 1 < len(seq) else None

        def next_pre_fn(h, nxt=nxt):
            if nxt is None:
                return None
            tn, ln, kn = nxt
            hcs = halves_for(tn, ln, norm=(kn == "mix"))
            return (hcs[h], pre_gcol(ln, kn))
        final_tile = t if (kind == "ffn" and l == layers[-1]) else None
        if kind == "mix":
            if l % 2 == 0:
                hA, hB = halves_for(t, l)
                mixer_a(l, hA, hB, next_pre_fn)
            else:
                hA, hB = halves_for(t, l, norm=True)
                hAc, hBc = halves_for(t, l)
                mixer_b(l, hA, hB, hAc, hBc, HALO if t == 0 else -1, next_pre_fn)
        else:
            hA, hB = halves_for(t, l)
            ffn(l, hA, hB, final_tile, next_pre_fn)
    drain_all()
    S.emit(nc, engsem)
    return nc


def _host_consts():
    s = np.arange(128)[:, None]
    t = np.arange(128)[None, :]
    maskT = (s <= t).astype(np.float32)
    wins = (2, 4, 8, 16)
    pm = np.zeros((3, 128, 4, 128), np.float32)
    for g, w in enumerate(wins):
        band = ((s <= t) & (s > t - w)).astype(np.float32)
        pm[0, :, g, :] = band / w - np.eye(128, dtype=np.float32)
        bandp = ((s - 128) > (t - w)).astype(np.float32)
        pm[1, :, g, :] = bandp / w
        cnt = np.minimum(t + 1, w).astype(np.float32)
        pm[2, :, g, :] = band / cnt - np.eye(128, dtype=np.float32)
    return maskT, pm.reshape(3, 128, 512)


def _vec_cols(v):
    return np.ascontiguousarray(v.reshape(-1, 128).T)


_NC_CACHE = {}


def _get_nc(layers, ntiles):
    key = (tuple(layers), ntiles)
    if key not in _NC_CACHE:
        _NC_CACHE[key] = build_nc(layers, ntiles)
    return _NC_CACHE[key]


def _make_in_maps(x, a_w_in, a_ln_g, a_ln_b, a_w_s, a_b_s, a_w_out, b_w_in, b_w_grp, b_scale, b_w_out,
                  mix_pre_g, mix_post_g, ffn_pre_g, ffn_post_g, ffn_w_gate, ffn_w_up, ffn_w_down):
    f = np.float32
    B, Sq, _ = x.shape
    maskT, pm = _host_consts()
    gv = np.zeros((128, GC_N), f)
    for l in range(4):
        gv[:, GC_MIXPRE + l * 8: GC_MIXPRE + l * 8 + 8] = _vec_cols(np.asarray(mix_pre_g[l], f))
        gv[:, GC_MIXPOST + l * 8: GC_MIXPOST + l * 8 + 8] = _vec_cols(np.asarray(mix_post_g[l], f))
        gv[:, GC_FFNPRE + l * 8: GC_FFNPRE + l * 8 + 8] = _vec_cols(np.asarray(ffn_pre_g[l], f))
        gv[:, GC_FFNPOST + l * 8: GC_FFNPOST + l * 8 + 8] = _vec_cols(np.asarray(ffn_post_g[l], f))
    for j in range(2):
        gv[:, GC_BSCALE + j * 8: GC_BSCALE + j * 8 + 8] = _vec_cols(np.asarray(b_scale[j], f))
        gv[:, GC_LNB + j * 16: GC_LNB + j * 16 + 16] = _vec_cols(np.asarray(a_ln_b[j], f))
    wsT = np.ascontiguousarray(np.transpose(np.asarray(a_w_s, f), (0, 3, 1, 2))).reshape(2, 128, 1024)
    bs = np.ascontiguousarray(np.asarray(a_b_s, f)).reshape(2, 1024)
    shared = {
        "a_w_in": np.ascontiguousarray(a_w_in, f), "a_w_out": np.ascontiguousarray(a_w_out, f),
        "b_w_in": np.ascontiguousarray(b_w_in, f), "b_w_grp": np.ascontiguousarray(b_w_grp, f),
        "b_w_out": np.ascontiguousarray(b_w_out, f), "ffn_w_gate": np.ascontiguousarray(ffn_w_gate, f),
        "ffn_w_up": np.ascontiguousarray(ffn_w_up, f), "ffn_w_down": np.ascontiguousarray(ffn_w_down, f),
        "gvec": gv, "a_ln_g": np.ascontiguousarray(a_ln_g, f), "a_w_sT": wsT, "a_b_s": bs, "maskT": maskT,
    }
    pm_mid = pm.copy()
    pm_mid[2] = pm_mid[0]
    in_maps = []
    for core in range(NCORES):
        b, half = core // 2, core % 2
        start = half * OWN - HALO * 128
        xw = np.zeros((NTOK, D), f)
        lo = max(start, 0)
        xw[lo - start:, :] = x[b, lo:start + NTOK, :]
        m = dict(shared)
        m["xT"] = np.ascontiguousarray(xw.T)
        m["poolm"] = pm if half == 0 else pm_mid
        in_maps.append(m)
    return in_maps


def kernel(**inputs):
    inputs = {k: np.asarray(v) for k, v in inputs.items()}
    x = inputs["x"].astype(np.float32, copy=False)
    B, Sq, _ = x.shape
    in_maps = _make_in_maps(**inputs)
    nc = _get_nc((0, 1, 2, 3), NTILES)
    res = run_bass_kernel_spmd(nc, in_maps, core_ids=list(range(NCORES)))
    out = np.empty((B, Sq, D), np.float32)
    for core in range(NCORES):
        b, half = core // 2, core % 2
        out[b, half * OWN:(half + 1) * OWN, :] = res.results[core]["yT"].T
    return out
```
